# Optimizing a Trainium2 kernel written in Bass

```python
import math
import jax, jax.numpy as jnp
from jax import lax
import numpy as np

D_MODEL = 1024
BATCH = 16
SEQ = 2048
DEPTH = 4
DEC_BATCH = 2
DEC_SEQ = 16384
PAST_LEN = 128

GRID_W = 64
Q_BLOCK = 128
N_MIXERS = 3
EPS = 1e-6
ROPE_THETA = 10000.0
LAYER_KINDS = tuple(i % N_MIXERS for i in range(DEPTH))
N_MLA_LAYERS = LAYER_KINDS.count(0)
N_GQA_LAYERS = LAYER_KINDS.count(1)
N_DIFF_LAYERS = LAYER_KINDS.count(2)

MLA_HEADS = 16
MLA_Q_LORA = 256
MLA_KV_LORA = 128
MLA_NOPE = 64
MLA_ROPE = 32
MLA_V = 64
MLA_WIDTH = MLA_HEADS * MLA_V
MLA_IN = MLA_Q_LORA + MLA_KV_LORA + MLA_ROPE + MLA_WIDTH

GQA_HEADS = 8
GQA_KV_HEADS = 2
GQA_HEAD_DIM = 128
GQA_REP = GQA_HEADS // GQA_KV_HEADS
GQA_WIDTH = GQA_HEADS * GQA_HEAD_DIM
GQA_KV_WIDTH = GQA_KV_HEADS * GQA_HEAD_DIM
GQA_IN = GQA_WIDTH + 2 * GQA_KV_WIDTH + GQA_WIDTH

DIFF_HEADS = 8
DIFF_HEAD_DIM = 64
DIFF_WIDTH = DIFF_HEADS * 2 * DIFF_HEAD_DIM
DIFF_IN = 4 * DIFF_WIDTH

REL_BUCKETS = 32
REL_MAX_DIST = 128

kernel_name = "hybrid_mla_gqa_diff_encoder"


def _rmsnorm(x, g):
    xf = x.astype(jnp.float32)
    y = xf * lax.rsqrt(jnp.mean(xf * xf, axis=-1, keepdims=True) + EPS)
    return (y * g.astype(jnp.float32)).astype(x.dtype)


def _rope_angles(pos, dim):
    inv = ROPE_THETA ** (-(jnp.arange(0, dim, 2, dtype=jnp.float32) / dim))
    return pos.astype(jnp.float32)[:, None] * inv[None, :]


def _rotary(x, cos, sin):
    x1, x2 = jnp.split(x, 2, axis=-1)
    c = cos.astype(x.dtype)
    s = sin.astype(x.dtype)
    return jnp.concatenate([x1 * c - x2 * s, x2 * c + x1 * s], axis=-1)


def _blocks(a):
    b, s = a.shape[:2]
    return jnp.moveaxis(a.reshape(b, s // Q_BLOCK, Q_BLOCK, *a.shape[2:]), 1, 0)


def _unblocks(o):
    o = jnp.moveaxis(o, 0, 1)
    return o.reshape(o.shape[0], -1, *o.shape[3:])


def _t5_bucket(rel):
    half = REL_BUCKETS // 2
    max_exact = half // 2
    base = (rel > 0).astype(jnp.int32) * half
    n = jnp.abs(rel)
    nf = jnp.maximum(n, 1).astype(jnp.float32)
    large = max_exact + (jnp.log(nf / max_exact) / math.log(REL_MAX_DIST / max_exact)
                         * (half - max_exact)).astype(jnp.int32)
    large = jnp.minimum(large, half - 1)
    return base + jnp.where(n < max_exact, n, large)


def _mla_mixer(h, w_in, g_q, w_uq, g_kv, w_ukv, w_o):
    b, s, _ = h.shape
    proj = h @ w_in
    q_lat, kv_lat, k_rope, gate = jnp.split(
        proj, [MLA_Q_LORA, MLA_Q_LORA + MLA_KV_LORA, MLA_Q_LORA + MLA_KV_LORA + MLA_ROPE], axis=-1)
    q = (_rmsnorm(q_lat, g_q) @ w_uq).reshape(b, s, MLA_HEADS, MLA_NOPE + MLA_ROPE)
    q_nope, q_rope = q[..., :MLA_NOPE], q[..., MLA_NOPE:]
    kv = (_rmsnorm(kv_lat, g_kv) @ w_ukv).reshape(b, s, MLA_HEADS, MLA_NOPE + MLA_V)
    k_nope, v = kv[..., :MLA_NOPE], kv[..., MLA_NOPE:]
    ang = _rope_angles(jnp.arange(s), MLA_ROPE)
    cos, sin = jnp.cos(ang), jnp.sin(ang)
    q_rope = _rotary(q_rope, cos[:, None, :], sin[:, None, :])
    k_rope = _rotary(k_rope, cos, sin)
    scale = 1.0 / math.sqrt(MLA_NOPE + MLA_ROPE)

    def blk(args):
        qn, qr = args
        sc = (jnp.einsum('bqhd,bkhd->bhqk', qn, k_nope)
              + jnp.einsum('bqhr,bkr->bhqk', qr, k_rope))
        p = jax.nn.softmax(sc.astype(jnp.float32) * scale, axis=-1).astype(v.dtype)
        return jnp.einsum('bhqk,bkhd->bqhd', p, v)

    o = _unblocks(lax.map(blk, (_blocks(q_nope), _blocks(q_rope)))).reshape(b, s, MLA_WIDTH)
    return (o * jax.nn.silu(gate)) @ w_o


def _gqa_mixer(h, w_in, g_q, g_k, w_o):
    b, s, _ = h.shape
    proj = h @ w_in
    q, k, v, gate = jnp.split(
        proj, [GQA_WIDTH, GQA_WIDTH + GQA_KV_WIDTH, GQA_WIDTH + 2 * GQA_KV_WIDTH], axis=-1)
    q = _rmsnorm(q.reshape(b, s, GQA_HEADS, GQA_HEAD_DIM), g_q)
    k = _rmsnorm(k.reshape(b, s, GQA_KV_HEADS, GQA_HEAD_DIM), g_k)
    v = v.reshape(b, s, GQA_KV_HEADS, GQA_HEAD_DIM)
    rows = s // GRID_W
    row = jnp.repeat(jnp.arange(rows), GRID_W)
    col = jnp.tile(jnp.arange(GRID_W), rows)
    ang = jnp.concatenate([_rope_angles(row, GQA_HEAD_DIM // 2),
                           _rope_angles(col, GQA_HEAD_DIM // 2)], axis=-1)
    cos, sin = jnp.cos(ang)[:, None, :], jnp.sin(ang)[:, None, :]
    q = _rotary(q, cos, sin).reshape(b, s, GQA_KV_HEADS, GQA_REP, GQA_HEAD_DIM)
    k = _rotary(k, cos, sin)
    scale = 1.0 / math.sqrt(GQA_HEAD_DIM)

    def blk(qb):
        sc = jnp.einsum('bqgrd,bkgd->bgrqk', qb, k)
        p = jax.nn.softmax(sc.astype(jnp.float32) * scale, axis=-1).astype(v.dtype)
        return jnp.einsum('bgrqk,bkgd->bqgrd', p, v)

    o = _unblocks(lax.map(blk, _blocks(q))).reshape(b, s, GQA_WIDTH)
    return (o * jax.nn.silu(gate)) @ w_o


def _diff_mixer(h, layer_idx, rel_bias, w_in, lam_q1, lam_k1, lam_q2, lam_k2, g_sub, w_o):
    b, s, _ = h.shape
    proj = h @ w_in
    q, k, v, gate = jnp.split(proj, [DIFF_WIDTH, 2 * DIFF_WIDTH, 3 * DIFF_WIDTH], axis=-1)
    q = q.reshape(b, s, DIFF_HEADS, 2, DIFF_HEAD_DIM)
    k = k.reshape(b, s, DIFF_HEADS, 2, DIFF_HEAD_DIM)
    q1, q2 = q[..., 0, :], q[..., 1, :]
    k1, k2 = k[..., 0, :], k[..., 1, :]
    v = v.reshape(b, s, DIFF_HEADS, 2 * DIFF_HEAD_DIM)
    lam_init = 0.8 - 0.6 * math.exp(-0.3 * layer_idx)
    lam = (jnp.exp(jnp.sum(lam_q1.astype(jnp.float32) * lam_k1.astype(jnp.float32)))
           - jnp.exp(jnp.sum(lam_q2.astype(jnp.float32) * lam_k2.astype(jnp.float32)))
           + lam_init)
    scale = 1.0 / math.sqrt(DIFF_HEAD_DIM)
    kpos = jnp.arange(s)
    nb = s // Q_BLOCK

    def blk(args):
        q1b, q2b, start = args
        qpos = start * Q_BLOCK + jnp.arange(Q_BLOCK)
        bias = rel_bias[_t5_bucket(kpos[None, :] - qpos[:, None])]
        bias = jnp.transpose(bias, (2, 0, 1)).astype(jnp.float32)
        s1 = jnp.einsum('bqhd,bkhd->bhqk', q1b, k1).astype(jnp.float32) * scale + bias
        s2 = jnp.einsum('bqhd,bkhd->bhqk', q2b, k2).astype(jnp.float32) * scale + bias
        a = jax.nn.softmax(s1, axis=-1) - lam * jax.nn.softmax(s2, axis=-1)
        return jnp.einsum('bhqk,bkhd->bqhd', a.astype(v.dtype), v)

    o = _unblocks(lax.map(blk, (_blocks(q1), _blocks(q2), jnp.arange(nb))))
    o = (_rmsnorm(o, g_sub) * (1.0 - lam_init)).reshape(b, s, DIFF_WIDTH)
    return (o * jax.nn.silu(gate)) @ w_o


def _trunk(x, norm_pre, norm_post, rel_bias,
           mla_w_in, mla_g_q, mla_w_uq, mla_g_kv, mla_w_ukv, mla_w_o,
           gqa_w_in, gqa_g_q, gqa_g_k, gqa_w_o,
           dif_w_in, dif_lam_q1, dif_lam_k1, dif_lam_q2, dif_lam_k2, dif_g_sub, dif_w_o):
    ia = ib = ic = 0
    for layer in range(DEPTH):
        kind = LAYER_KINDS[layer]
        h = _rmsnorm(x, norm_pre[layer])
        if kind == 0:
            m = _mla_mixer(h, mla_w_in[ia], mla_g_q[ia], mla_w_uq[ia], mla_g_kv[ia],
                           mla_w_ukv[ia], mla_w_o[ia])
            ia += 1
        elif kind == 1:
            m = _gqa_mixer(h, gqa_w_in[ib], gqa_g_q[ib], gqa_g_k[ib], gqa_w_o[ib])
            ib += 1
        else:
            m = _diff_mixer(h, layer, rel_bias, dif_w_in[ic], dif_lam_q1[ic], dif_lam_k1[ic],
                            dif_lam_q2[ic], dif_lam_k2[ic], dif_g_sub[ic], dif_w_o[ic])
            ic += 1
        x = x + _rmsnorm(m, norm_post[layer])
    return x


def setup_inputs(seed: int = 0) -> dict:
    key = jax.random.key(seed)
    ks = jax.random.split(key, 24)
    f32 = jnp.float32

    def w(k, shape, fan_in):
        return jax.random.normal(k, shape, f32) * (fan_in ** -0.5)

    def gain(k, shape):
        return 1.0 + 0.02 * jax.random.normal(k, shape, f32)

    NA, NB, NC = N_MLA_LAYERS, N_GQA_LAYERS, N_DIFF_LAYERS
    return {
        "x_prompt": jax.random.normal(ks[0], (BATCH, SEQ, D_MODEL), f32),
        "x_sample": jax.random.normal(ks[1], (DEC_BATCH, DEC_SEQ, D_MODEL), f32),
        "norm_pre": gain(ks[2], (DEPTH, D_MODEL)),
        "norm_post": gain(ks[3], (DEPTH, D_MODEL)),
        "rel_bias": 0.5 * jax.random.normal(ks[4], (REL_BUCKETS, DIFF_HEADS), f32),
        "mla_w_in": w(ks[5], (NA, D_MODEL, MLA_IN), D_MODEL),
        "mla_g_q": gain(ks[6], (NA, MLA_Q_LORA)),
        "mla_w_uq": w(ks[7], (NA, MLA_Q_LORA, MLA_HEADS * (MLA_NOPE + MLA_ROPE)), MLA_Q_LORA),
        "mla_g_kv": gain(ks[8], (NA, MLA_KV_LORA)),
        "mla_w_ukv": w(ks[9], (NA, MLA_KV_LORA, MLA_HEADS * (MLA_NOPE + MLA_V)), MLA_KV_LORA),
        "mla_w_o": w(ks[10], (NA, MLA_WIDTH, D_MODEL), MLA_WIDTH),
        "gqa_w_in": w(ks[11], (NB, D_MODEL, GQA_IN), D_MODEL),
        "gqa_g_q": gain(ks[12], (NB, GQA_HEAD_DIM)),
        "gqa_g_k": gain(ks[13], (NB, GQA_HEAD_DIM)),
        "gqa_w_o": w(ks[14], (NB, GQA_WIDTH, D_MODEL), GQA_WIDTH),
        "dif_w_in": w(ks[15], (NC, D_MODEL, DIFF_IN), D_MODEL),
        "dif_lam_q1": 0.1 * jax.random.normal(ks[16], (NC, DIFF_HEAD_DIM), f32),
        "dif_lam_k1": 0.1 * jax.random.normal(ks[17], (NC, DIFF_HEAD_DIM), f32),
        "dif_lam_q2": 0.1 * jax.random.normal(ks[18], (NC, DIFF_HEAD_DIM), f32),
        "dif_lam_k2": 0.1 * jax.random.normal(ks[19], (NC, DIFF_HEAD_DIM), f32),
        "dif_g_sub": gain(ks[20], (NC, 2 * DIFF_HEAD_DIM)),
        "dif_w_o": w(ks[21], (NC, DIFF_WIDTH, D_MODEL), DIFF_WIDTH),
    }


def reference(x_prompt, x_sample, norm_pre, norm_post, rel_bias,
              mla_w_in, mla_g_q, mla_w_uq, mla_g_kv, mla_w_ukv, mla_w_o,
              gqa_w_in, gqa_g_q, gqa_g_k, gqa_w_o,
              dif_w_in, dif_lam_q1, dif_lam_k1, dif_lam_q2, dif_lam_k2, dif_g_sub, dif_w_o):
    params = (norm_pre, norm_post, rel_bias,
              mla_w_in, mla_g_q, mla_w_uq, mla_g_kv, mla_w_ukv, mla_w_o,
              gqa_w_in, gqa_g_q, gqa_g_k, gqa_w_o,
              dif_w_in, dif_lam_q1, dif_lam_k1, dif_lam_q2, dif_lam_k2, dif_g_sub, dif_w_o)
    y_prompt = _trunk(x_prompt, *params)
    y_sample = _trunk(x_sample, *params)
    return (y_prompt, y_sample)
```

```python
import math
import contextlib
import numpy as np
import ml_dtypes
import concourse.bass as bass
import concourse.mybir as mybir
from concourse.bass_utils import run_bass_kernel_spmd

F32 = mybir.dt.float32
BF16 = mybir.dt.bfloat16
AF = mybir.ActivationFunctionType
ALU = mybir.AluOpType
AX = mybir.AxisListType
NPBF = ml_dtypes.bfloat16

DM = 1024
EPS = 1e-6
LAYER_KINDS = (0, 1, 2, 0)
NCORES = 8
STRICT = False
NR = 4
STRIP_C = 640
STRIP_W = 1408
TREV_L = 1536


class Cfg:
    def __init__(self, TS=4096, TP=2048, NPS=2):
        self.TS, self.TP, self.NPS = TS, TP, NPS
        self.LT = TS + NPS * TP
        self.SK = NR * TS


class Sem:
    def __init__(self, nc, name):
        self.h = nc.alloc_semaphore(name=name)
        self.v = 0
        self.name = name


class Buf:
    __slots__ = ("w", "r", "excl")

    def __init__(self, excl=False):
        self.w = None
        self.r = []
        self.excl = excl


class Sems:
    ENG = ("pe", "act", "dve", "pool", "sp")

    def __init__(self, nc):
        self.nc = nc
        self.selfsem = {k: Sem(nc, "self_" + k) for k in self.ENG}
        self.ring = {k: [Sem(nc, "ring_%s%d" % (k, i)) for i in range(8)] for k in ("sp", "pool", "act")}
        self.ridx = {k: 0 for k in self.ring}
        self.seen = {}
        self.named = {}

    def get(self, name):
        if name not in self.named:
            self.named[name] = Sem(self.nc, name)
        return self.named[name]


class Prog:
    ENG = Sems.ENG

    def __init__(self, nc, S):
        self.nc = nc
        self.S = S
        self.ops = {k: [] for k in self.ENG}
        self.pending = {k: [] for k in self.ENG}
        self.dma_sems = set()

    def op(self, eng, meth, *args, inc=None, amt=None, **kw):
        tok = None
        if inc is not None:
            if amt is None:
                amt = 16 if meth == "dma_start" else 1
            inc.v += amt
            tok = (inc, inc.v, eng, meth == "dma_start")
            if meth == "dma_start":
                self.dma_sems.add(inc)
        self.ops[eng].append((meth, args, kw, inc, amt))
        return tok

    def wait(self, eng, sem, val=None):
        if isinstance(sem, tuple):
            sem, val = sem[0], sem[1]
        if val is None:
            val = sem.v
        if val <= 0:
            return
        key = (eng, sem.name)
        if self.S.seen.get(key, 0) >= val:
            return
        self.S.seen[key] = val
        self.ops[eng].append(("wait_ge", (sem.h, val), {}, None, None))

    def _dep(self, eng, tok, raw):
        sem, val, src, is_dma = tok
        if src == eng and not is_dma and not raw and not STRICT:
            return
        if src == eng and eng == "pe":
            return
        self.wait(eng, sem, val)

    def t(self, eng, meth, *args, R=(), W=(), tok=True, **kw):
        xr = [b for b in R if b.excl]
        if xr:
            R = [b for b in R if not b.excl]
            W = list(W) + xr
        for b in R:
            if b.w is not None:
                self._dep(eng, b.w, True)
        for b in W:
            if b.w is not None:
                self._dep(eng, b.w, False)
            for tk in b.r:
                self._dep(eng, tk, False)
        token = None
        if meth == "dma_start":
            ring = self.S.ring[eng]
            s = ring[self.S.ridx[eng] % len(ring)]
            self.S.ridx[eng] += 1
            self.wait(eng, s, s.v)
            token = self.op(eng, meth, *args, inc=s, amt=16, **kw)
        elif tok:
            token = self.op(eng, meth, *args, inc=self.S.selfsem[eng], amt=1, **kw)
        else:
            self.op(eng, meth, *args, **kw)
        if token is None:
            self.pending[eng].append((tuple(R), tuple(W)))
            return None
        groups = self.pending[eng] + [(tuple(R), tuple(W))]
        self.pending[eng] = []
        for (rr, ww) in groups:
            for b in rr:
                b.r.append(token)
            for b in ww:
                b.w = token
                b.r = []
        return token

    def finish(self):
        for eng in ("sp", "pool", "act"):
            for s in self.S.ring[eng]:
                if s.v > 0:
                    self.wait(eng, s, s.v)
        for s in self.dma_sems:
            self.wait("sp", s, s.v)

    def emit(self):
        nc = self.nc

        def run(e, lst):
            for meth, args, kw, inc, amt in lst:
                ins = getattr(e, meth)(*args, **kw)
                if inc is not None:
                    ins.then_inc(inc.h, amt)
        with nc.Block() as block:
            @block.tensor
            def _(e):
                run(e, self.ops["pe"])

            @block.scalar
            def _(e):
                run(e, self.ops["act"])

            @block.vector
            def _(e):
                run(e, self.ops["dve"])

            @block.gpsimd
            def _(e):
                run(e, self.ops["pool"])

            @block.sync
            def _(e):
                run(e, self.ops["sp"])


class TB:
    def __init__(self, bld, name, shape, dt):
        self.t = bld.sb(name, shape, dt)
        self.b = Buf()


def kind_geom(kind):
    if kind == 0:
        return dict(NIN=1440, FQ=1536, FK=1056, WV=1024, OW=1024, dk=96, dv=64, scale=1.0 / math.sqrt(96.0))
    if kind == 1:
        return dict(NIN=2560, FQ=1024, FK=256, WV=256, OW=1024, dk=128, dv=128, scale=1.0 / math.sqrt(128.0))
    return dict(NIN=4096, FQ=1024, FK=1024, WV=1024, OW=2048, dk=64, dv=128, scale=1.0 / math.sqrt(64.0))


def kind_heads(kind):
    hs = []
    if kind == 0:
        for h in range(16):
            hs.append(dict(qrow=h * 96, kparts=[(h * 64, 64, 0), (1024, 32, 64)], vcol=h * 64, ocol=h * 64, bh=None))
    elif kind == 1:
        for h in range(8):
            g = h // 4
            hs.append(dict(qrow=h * 128, kparts=[(g * 128, 128, 0)], vcol=g * 128, ocol=h * 128, bh=None))
    else:
        for h in range(8):
            for j in range(2):
                hs.append(dict(qrow=h * 128 + j * 64, kparts=[(h * 128 + j * 64, 64, 0)], vcol=h * 128,
                               ocol=(h * 2 + j) * 128, bh=h))
    return hs


class Builder:
    def __init__(self, cfg, debug=(), stop=None):
        self.cfg = cfg
        self.stop = stop
        self.debug = set(debug)
        nc = self.nc = bass.Bass("TRN2", target_bir_lowering=False)
        self.S = Sems(nc)
        LT, TS, SK = cfg.LT, cfg.TS, cfg.SK
        din = lambda n, s, d=F32: nc.dram_tensor(n, s, d, kind="ExternalInput").ap()
        self.x_in = din("x_in", [LT, DM])
        self.y = nc.dram_tensor("y", [LT, DM], F32, kind="ExternalOutput").ap()
        self.w = {}
        for n, s in [("norm_pre", [4, DM]), ("norm_post", [4, DM]), ("rel_bias", [32, 8]),
                     ("mla_w_in", [2, DM, 1440]), ("mla_g_q", [2, 256]), ("mla_w_uq", [2, 256, 1536]),
                     ("mla_g_kv", [2, 128]), ("mla_w_ukv", [2, 128, 2048]), ("mla_w_o", [2, 1024, DM]),
                     ("gqa_w_in", [1, DM, 2560]), ("gqa_g_q", [1, 128]), ("gqa_g_k", [1, 128]),
                     ("gqa_w_o", [1, 1024, DM]),
                     ("dif_w_in", [1, DM, 4096]), ("dif_lam_q1", [1, 64]), ("dif_lam_k1", [1, 64]),
                     ("dif_lam_q2", [1, 64]), ("dif_lam_k2", [1, 64]), ("dif_g_sub", [1, 128]),
                     ("dif_w_o", [1, 1024, DM])]:
            self.w[n] = din(n, s)
        self.mla_cs = din("mla_cs", [LT, 32])
        self.gqa_cs = din("gqa_cs", [LT, 128])
        self.ident_d = din("ident", [128, 128], BF16)
        self.dif_oh = din("dif_oh", [32, TREV_L])
        self.dif_sel = din("dif_sel", [32, 10])
        self.dif_sel2 = din("dif_sel2", [32, 13])
        self.dif_m = din("dif_m", [128, 13])
        self.antiid = din("antiid", [128, 128])
        self.scr = {}
        self.ps = nc.alloc_psum_tensor("ps", [128, 8, 512], F32)
        self.ident = nc.alloc_sbuf_tensor("ident_sb", [128, 128], BF16)
        self.CB = nc.alloc_sbuf_tensor("CB", [128, 8, 23], F32)
        self.MK = nc.alloc_sbuf_tensor("MK", [128, 13], F32)
        self.eps_sb = nc.alloc_sbuf_tensor("eps_sb", [128, 1], F32)
        self.es = None
        self.qt_count = 0
        self.grp_count = 0
        self.fcount = 0

    def sb(self, name, shape, dt):
        return self.es.enter_context(self.nc.sbuf_tensor(name, shape, dt))

    def dscr(self, name, shape, dt):
        kind = "ExternalOutput" if name in self.debug else "Internal"
        t = self.nc.dram_tensor(name, shape, dt, kind=kind)
        self.scr[name] = t
        return t.ap()

    def psb(self, b):
        return self.ps[:, b, :].bitcast(BF16)

    def build(self):
        nc, cfg = self.nc, self.cfg
        P = Prog(nc, self.S)
        P.t("sp", "dma_start", out=self.ident[:, :], in_=self.ident_d[:, :])
        P.t("pool", "memset", self.eps_sb[:, :], EPS)
        P.finish()
        P.emit()
        n_mla = 0
        for L, kind in enumerate(LAYER_KINDS):
            g = kind_geom(kind)
            LT, TS, SK, TP, NPS = cfg.LT, cfg.TS, cfg.SK, cfg.TP, cfg.NPS
            D = dict(
                QT=self.dscr("QT%d" % L, [g["FQ"], LT], BF16),
                KTl=self.dscr("KTl%d" % L, [g["FK"], TS], BF16),
                KTa=[(r0, min(64, g["FK"] - r0), self.dscr("KTa%d_%d" % (L, r0), [NR * min(64, g["FK"] - r0), TS], BF16))
                     for r0 in range(0, g["FK"], 64)],
                KTp=self.dscr("KTp%d" % L, [g["FK"], NPS * TP], BF16),
                Vl=self.dscr("Vl%d" % L, [TS, g["WV"]], BF16),
                Va=[(r0, self.dscr("Va%d_%d" % (L, r0), [NR * min(TS, 262144 // g["WV"]), g["WV"]], BF16))
                    for r0 in range(0, TS, min(TS, 262144 // g["WV"]))],
                Vp=self.dscr("Vp%d" % L, [NPS * TP, g["WV"]], BF16),
                G=self.dscr("G%d" % L, [LT, DM], F32),
                O=self.dscr("O%d" % L, [LT, g["OW"]], F32),
            )
            xsrc = self.x_in if L == 0 else self.y
            idx = n_mla if kind == 0 else 0
            if kind == 0:
                n_mla += 1
            if kind == 2:
                self.diff_setup(L, D)
            self.phase_a(L, kind, idx, g, D, xsrc)
            if self.stop == (L, "A"):
                break
            self.gather(L, D)
            if self.stop == (L, "G"):
                break
            self.attention(L, kind, g, D)
            if self.stop == (L, "AT"):
                break
            self.phase_c(L, kind, idx, g, D, xsrc)
        return nc

    def load_w_bf16(self, P, w_ap, rows, cols, name, eng_cast="pool"):
        nc = self.nc
        KC = rows // 128
        wsb = self.sb(name, [128, KC, cols], BF16)
        CH = 2048
        for c in range(KC):
            for n0 in range(0, cols, CH):
                n1 = min(cols, n0 + CH)
                st = self.stg[self.stg_i % 2]
                self.stg_i += 1
                P.t("sp", "dma_start", out=st.t[:, 0:n1 - n0], in_=w_ap[c * 128:(c + 1) * 128, n0:n1], W=[st.b])
                P.t(eng_cast, "tensor_copy", out=wsb[:, c, n0:n1], in_=st.t[:, 0:n1 - n0], R=[st.b], W=[self.wbuf])
        return wsb

    def bcast_row(self, P, dst_ap, row_ap, n, buf):
        P.t("sp", "dma_start", out=dst_ap, in_=row_ap.broadcast_to([128, n]), W=[buf])

    def rstd_ops(self, P, ss, ssb, rstd, rstdb, n, width=1):
        P.t("act", "activation", out=rstd, in_=ss, func=AF.Sqrt, scale=1.0 / n, bias=self.eps_sb[:, 0:1], R=[ssb], W=[rstdb])
        P.t("dve", "reciprocal", rstd, rstd, R=[rstdb], W=[rstdb])

    def phase_a(self, L, kind, idx, g, D, xsrc):
        nc, cfg, ps = self.nc, self.cfg, self.ps
        S = self.S
        P = Prog(nc, S)
        LT, TS, TP, NPS = cfg.LT, cfg.TS, cfg.TP, cfg.NPS
        NIN = g["NIN"]
        with contextlib.ExitStack() as self.es:
            self.stg = [TB(self, "stg%d_%d" % (L, i), [128, 2048], F32) for i in range(2)]
            self.stg_i = 0
            self.wbuf = Buf()
            pre = {0: "mla", 1: "gqa", 2: "dif"}[kind]
            w_in = self.load_w_bf16(P, self.w[pre + "_w_in"][idx], DM, NIN, "w_in%d" % L)
            if kind == 0:
                w_uq = self.load_w_bf16(P, self.w["mla_w_uq"][idx], 256, 1536, "w_uq%d" % L)
                w_ukv = self.load_w_bf16(P, self.w["mla_w_ukv"][idx], 128, 2048, "w_ukv%d" % L)
            cb = Buf()
            gpre = self.sb("gpre%d" % L, [128, DM], F32)
            self.bcast_row(P, gpre[:, :], self.w["norm_pre"][L:L + 1, :], DM, cb)
            if kind == 0:
                glat = self.sb("glat%d" % L, [128, 384], F32)
                self.bcast_row(P, glat[:, 0:256], self.w["mla_g_q"][idx:idx + 1, :], 256, cb)
                self.bcast_row(P, glat[:, 256:384], self.w["mla_g_kv"][idx:idx + 1, :], 128, cb)
            if kind == 1:
                gqk = self.sb("gqk%d" % L, [128, 2, 128], F32)
                self.bcast_row(P, gqk[:, 0, :], self.w["gqa_g_q"][idx:idx + 1, :], 128, cb)
                self.bcast_row(P, gqk[:, 1, :], self.w["gqa_g_k"][idx:idx + 1, :], 128, cb)
            xts = [TB(self, "xt%d_%d" % (L, i), [128, DM], F32) for i in range(2)]
            def two(name, shape, dt):
                return [TB(self, "%s%d_%d" % (name, L, i), shape, dt) for i in range(2)]
            junk2 = two("junk", [128, 1536], BF16)
            ss2 = two("ss", [128, 16], F32)
            rstd2 = two("rstd", [128, 16], F32)
            hbf2 = two("hbf", [128, DM], BF16)
            hT2 = two("hT", [128, 8, 128], BF16)
            gsb2 = two("gsb", [128, DM], F32)
            vbf2 = two("vbf", [128, 1024], BF16)
            qkbf2 = two("qkbf", [128, 2048], BF16)
            trs = [TB(self, "trs%d_%d" % (L, i), [128, 8, 128], BF16) for i in range(2)]
            cs2 = two("cs", [128, 128], F32)
            ntmp = {0: 2, 1: 4, 2: 0}[kind]
            tmp2 = [[TB(self, "tmpa%d_%d_%d" % (L, i, j), [128, 1280 if kind == 1 else 512], F32) for i in range(ntmp)] for j in range(2)]
            latn2 = two("latn", [128, 384], BF16)
            latT2 = two("latT", [128, 3, 128], BF16)
            krot2 = two("krot", [128, 128], BF16)
            bank = [Buf(excl=True) for _ in range(8)]
            tri = [0]

            def cp(eng, out, in_, R, W):
                if eng == "act":
                    P.t("act", "activation", out=out, in_=in_, func=AF.Copy, R=R, W=W)
                else:
                    P.t(eng, "tensor_copy", out=out, in_=in_, R=R, W=W)

            def transpose_store(src_ap, srcb, ncols, dst_ap, tok, eng_cp):
                nch = (ncols + 127) // 128
                c = 0
                while c < nch:
                    k = min(8, nch - c)
                    tr = trs[tri[0] % 2]
                    bk = 7 if kind == 0 else 6 + tri[0] % 2
                    tri[0] += 1
                    for j in range(k):
                        w = min(128, ncols - (c + j) * 128)
                        P.t("pe", "transpose", out=self.psb(bk)[0:w, j * 128:(j + 1) * 128],
                            in_=src_ap[:, (c + j) * 128:(c + j) * 128 + w], identity=self.ident[:, :],
                            R=[srcb], W=[bank[bk]], tok=(j == k - 1))
                    wl = min(128, ncols - (c + k - 1) * 128)
                    if wl == 128:
                        cp(eng_cp, tr.t[:, 0:k, :], self.psb(bk)[:, 0:k * 128].rearrange("p (j t) -> p j t", j=k),
                           [bank[bk]], [tr.b])
                        P.t("sp", "dma_start",
                            out=dst_ap[c * 128:(c + k) * 128, tok:tok + 128].rearrange("(j p) t -> p j t", p=128),
                            in_=tr.t[:, 0:k, :], R=[tr.b])
                    else:
                        assert k == 1
                        cp(eng_cp, tr.t[0:wl, 0, :], self.psb(bk)[0:wl, 0:128], [bank[bk]], [tr.b])
                        P.t("sp", "dma_start", out=dst_ap[c * 128:c * 128 + wl, tok:tok + 128], in_=tr.t[0:wl, 0, :], R=[tr.b])
                    c += k

            def proj(col0, ncols, bk0, lhs, lhsb, wsb, KC):
                nb = (ncols + 511) // 512
                for b in range(nb):
                    n0 = col0 + b * 512
                    nn = min(512, col0 + ncols - n0)
                    for c in range(KC):
                        P.t("pe", "matmul", ps[:, bk0 + b, 0:nn], lhs[:, c, :], wsb[:, c, n0:n0 + nn],
                            start=(c == 0), stop=(c == KC - 1), R=[lhsb, self.wbuf], W=[bank[bk0 + b]], tok=(c == KC - 1))

            NT = LT // 128
            for i in range(NT):
                tok = i * 128
                is_s = tok < TS
                if is_s:
                    kt_dst, kt_tok = D["KTl"], tok
                    v_dst = D["Vl"][tok:tok + 128, :]
                else:
                    kt_dst, kt_tok = D["KTp"], tok - TS
                    v_dst = D["Vp"][tok - TS:tok - TS + 128, :]
                xt = xts[i % 2]
                k2 = i % 2
                junk, ss, rstd, hbf, hT, gsb, vbf, qkbf = junk2[k2], ss2[k2], rstd2[k2], hbf2[k2], hT2[k2], gsb2[k2], vbf2[k2], qkbf2[k2]
                cs, tmp, latn, latT, krot = cs2[k2], tmp2[k2], latn2[k2], latT2[k2], krot2[k2]
                P.t("sp", "dma_start", out=xt.t[:, :], in_=xsrc[tok:tok + 128, :], W=[xt.b])
                if kind == 0:
                    P.t("sp", "dma_start", out=cs.t[:, 0:32], in_=self.mla_cs[tok:tok + 128, :], W=[cs.b])
                elif kind == 1:
                    P.t("sp", "dma_start", out=cs.t[:, 0:128], in_=self.gqa_cs[tok:tok + 128, :], W=[cs.b])
                P.t("act", "activation", out=junk.t[:, 0:DM], in_=xt.t[:, :], func=AF.Square, accum_out=ss.t[:, 0:1],
                    R=[xt.b], W=[junk.b, ss.b])
                self.rstd_ops(P, ss.t[:, 0:1], ss.b, rstd.t[:, 0:1], rstd.b, DM)
                P.t("dve", "scalar_tensor_tensor", out=hbf.t[:, :], in0=xt.t[:, :], scalar=rstd.t[:, 0:1], in1=gpre[:, :],
                    op0=ALU.mult, op1=ALU.mult, R=[xt.b, rstd.b, cb], W=[hbf.b])
                for c in range(8):
                    P.t("pe", "transpose", out=self.psb(0)[:, c * 128:(c + 1) * 128], in_=hbf.t[:, c * 128:(c + 1) * 128],
                        identity=self.ident[:, :], R=[hbf.b], W=[bank[0]], tok=(c == 7))
                P.t("dve", "tensor_copy", out=hT.t[:, :, :], in_=self.psb(0)[:, :].rearrange("p (c t) -> p c t", c=8),
                    R=[bank[0]], W=[hT.b])
                if kind == 0:
                    proj(0, 416, 1, hT.t, hT.b, w_in, 8)
                    proj(416, 1024, 2, hT.t, hT.b, w_in, 8)
                    P.t("act", "activation", out=gsb.t[:, :], in_=ps[:, 2:4, :].rearrange("p b n -> p (b n)"), func=AF.Silu,
                        R=[bank[2], bank[3]], W=[gsb.b])
                    P.t("sp", "dma_start", out=D["G"][tok:tok + 128, :], in_=gsb.t[:, :], R=[gsb.b])
                    P.t("act", "activation", out=junk.t[:, 0:256], in_=ps[:, 1, 0:256], func=AF.Square, accum_out=ss.t[:, 1:2],
                        R=[bank[1]], W=[junk.b, ss.b])
                    P.t("act", "activation", out=junk.t[:, 0:128], in_=ps[:, 1, 256:384], func=AF.Square, accum_out=ss.t[:, 2:3],
                        R=[bank[1]], W=[junk.b, ss.b])
                    self.rstd_ops(P, ss.t[:, 1:2], ss.b, rstd.t[:, 1:2], rstd.b, 256)
                    self.rstd_ops(P, ss.t[:, 2:3], ss.b, rstd.t[:, 2:3], rstd.b, 128)
                    P.t("dve", "scalar_tensor_tensor", out=latn.t[:, 0:256], in0=ps[:, 1, 0:256], scalar=rstd.t[:, 1:2],
                        in1=glat[:, 0:256], op0=ALU.mult, op1=ALU.mult, R=[bank[1], rstd.b, cb], W=[latn.b])
                    P.t("dve", "scalar_tensor_tensor", out=latn.t[:, 256:384], in0=ps[:, 1, 256:384], scalar=rstd.t[:, 2:3],
                        in1=glat[:, 256:384], op0=ALU.mult, op1=ALU.mult, R=[bank[1], rstd.b, cb], W=[latn.b])
                    kr = ps[:, 1, 384:416]
                    C, Sn = cs.t[:, 0:16], cs.t[:, 16:32]
                    t0, t1 = tmp[0], tmp[1]
                    P.t("dve", "tensor_tensor", out=t0.t[:, 0:16], in0=kr[:, 0:16], in1=C, op=ALU.mult, R=[bank[1], cs.b], W=[t0.b])
                    P.t("dve", "tensor_tensor", out=t0.t[:, 16:32], in0=kr[:, 16:32], in1=Sn, op=ALU.mult, R=[bank[1], cs.b], W=[t0.b])
                    P.t("dve", "tensor_tensor", out=t1.t[:, 0:16], in0=kr[:, 16:32], in1=C, op=ALU.mult, R=[bank[1], cs.b], W=[t1.b])
                    P.t("dve", "tensor_tensor", out=t1.t[:, 16:32], in0=kr[:, 0:16], in1=Sn, op=ALU.mult, R=[bank[1], cs.b], W=[t1.b])
                    P.t("dve", "tensor_tensor", out=krot.t[:, 0:16], in0=t0.t[:, 0:16], in1=t0.t[:, 16:32], op=ALU.subtract,
                        R=[t0.b], W=[krot.b])
                    P.t("dve", "tensor_tensor", out=krot.t[:, 16:32], in0=t1.t[:, 0:16], in1=t1.t[:, 16:32], op=ALU.add,
                        R=[t1.b], W=[krot.b])
                    transpose_store(krot.t[:, 0:32], krot.b, 32, kt_dst[1024:1056, :], kt_tok, "act")
                    for c in range(3):
                        P.t("pe", "transpose", out=self.psb(0)[:, c * 128:(c + 1) * 128], in_=latn.t[:, c * 128:(c + 1) * 128],
                            identity=self.ident[:, :], R=[latn.b], W=[bank[0]], tok=(c == 2))
                    P.t("act", "activation", out=latT.t[:, :, :], in_=self.psb(0)[:, 0:384].rearrange("p (c t) -> p c t", c=3),
                        func=AF.Copy, R=[bank[0]], W=[latT.b])
                    proj(0, 1536, 4, latT.t[:, 0:2, :], latT.b, w_uq, 2)
                    qv = ps[:, 4:7, :].rearrange("p b n -> p (b n)").rearrange("p (h d) -> p h d", h=16)
                    qb3 = qkbf.t[:, 0:1536].rearrange("p (h d) -> p h d", h=16)
                    P.t("act", "activation", out=qkbf.t[:, 0:1536], in_=ps[:, 4:7, :].rearrange("p b n -> p (b n)"), func=AF.Copy,
                        R=[bank[4], bank[5], bank[6]], W=[qkbf.b])
                    Cb = cs.t[:, 0:16].unsqueeze(1).broadcast_to([128, 16, 16])
                    Sb = cs.t[:, 16:32].unsqueeze(1).broadcast_to([128, 16, 16])
                    A, B = qv[:, :, 64:80], qv[:, :, 80:96]
                    t0v = t0.t[:, 0:512].rearrange("p (h d) -> p h d", h=16)
                    t1v = t1.t[:, 0:512].rearrange("p (h d) -> p h d", h=16)
                    P.t("dve", "tensor_tensor", out=t0v[:, :, 0:16], in0=A, in1=Cb, op=ALU.mult, R=[bank[4], bank[5], bank[6], cs.b], W=[t0.b])
                    P.t("dve", "tensor_tensor", out=t0v[:, :, 16:32], in0=B, in1=Sb, op=ALU.mult, R=[bank[4], bank[5], bank[6], cs.b], W=[t0.b])
                    P.t("dve", "tensor_tensor", out=t1v[:, :, 0:16], in0=B, in1=Cb, op=ALU.mult, R=[bank[4], bank[5], bank[6], cs.b], W=[t1.b])
                    P.t("dve", "tensor_tensor", out=t1v[:, :, 16:32], in0=A, in1=Sb, op=ALU.mult, R=[bank[4], bank[5], bank[6], cs.b], W=[t1.b])
                    P.t("dve", "tensor_tensor", out=qb3[:, :, 64:80], in0=t0v[:, :, 0:16], in1=t0v[:, :, 16:32], op=ALU.subtract,
                        R=[t0.b], W=[qkbf.b])
                    P.t("dve", "tensor_tensor", out=qb3[:, :, 80:96], in0=t1v[:, :, 0:16], in1=t1v[:, :, 16:32], op=ALU.add,
                        R=[t1.b], W=[qkbf.b])
                    transpose_store(qkbf.t[:, 0:1536], qkbf.b, 1536, D["QT"], tok, "act")
                    proj(0, 2048, 0, latT.t[:, 2:3, :], latT.b, w_ukv, 1)
                    kvv = ps[:, 0:4, :].rearrange("p b n -> p (b n)").rearrange("p (h d) -> p h d", h=16)
                    P.t("act", "activation", out=vbf.t[:, :].rearrange("p (h d) -> p h d", h=16), in_=kvv[:, :, 64:128], func=AF.Copy,
                        R=[bank[0], bank[1], bank[2], bank[3]], W=[vbf.b])
                    P.t("sp", "dma_start", out=v_dst, in_=vbf.t[:, :], R=[vbf.b])
                    P.t("dve", "tensor_copy", out=qkbf.t[:, 0:1024].rearrange("p (h d) -> p h d", h=16), in_=kvv[:, :, 0:64],
                        R=[bank[0], bank[1], bank[2], bank[3]], W=[qkbf.b])
                    transpose_store(qkbf.t[:, 0:1024], qkbf.b, 1024, kt_dst, kt_tok, "act")
                elif kind == 1:
                    proj(0, 1024, 1, hT.t, hT.b, w_in, 8)
                    proj(1024, 512, 3, hT.t, hT.b, w_in, 8)
                    proj(1536, 1024, 4, hT.t, hT.b, w_in, 8)
                    P.t("act", "activation", out=gsb.t[:, :], in_=ps[:, 4:6, :].rearrange("p b n -> p (b n)"), func=AF.Silu,
                        R=[bank[4], bank[5]], W=[gsb.b])
                    P.t("sp", "dma_start", out=D["G"][tok:tok + 128, :], in_=gsb.t[:, :], R=[gsb.b])
                    P.t("act", "activation", out=vbf.t[:, 0:256], in_=ps[:, 3, 256:512], func=AF.Copy, R=[bank[3]], W=[vbf.b])
                    P.t("sp", "dma_start", out=v_dst, in_=vbf.t[:, 0:256], R=[vbf.b])
                    qk = ps[:, 1:4, :].rearrange("p b n -> p (b n)")[:, 0:1280]
                    qk3 = qk.rearrange("p (h d) -> p h d", h=10)
                    sq, nq_, t0, t1 = tmp[0], tmp[1], tmp[2], tmp[3]
                    P.t("act", "activation", out=sq.t[:, 0:1280], in_=qk, func=AF.Square, R=[bank[1], bank[2], bank[3]], W=[sq.b])
                    P.t("dve", "tensor_reduce", out=ss.t[:, 4:14], in_=sq.t[:, 0:1280].rearrange("p (h d) -> p h d", h=10),
                        axis=AX.X, op=ALU.add, R=[sq.b], W=[ss.b])
                    self.rstd_ops(P, ss.t[:, 4:14], ss.b, rstd.t[:, 4:14], rstd.b, 128)
                    n3 = nq_.t[:, 0:1280].rearrange("p (h d) -> p h d", h=10)
                    P.t("dve", "tensor_tensor", out=n3, in0=qk3, in1=rstd.t[:, 4:14].unsqueeze(2).broadcast_to([128, 10, 128]),
                        op=ALU.mult, R=[bank[1], bank[2], bank[3], rstd.b], W=[nq_.b])
                    P.t("dve", "tensor_tensor", out=n3[:, 0:8, :], in0=n3[:, 0:8, :],
                        in1=gqk[:, 0:1, :].broadcast_to([128, 8, 128]), op=ALU.mult, R=[nq_.b, cb], W=[nq_.b])
                    P.t("dve", "tensor_tensor", out=n3[:, 8:10, :], in0=n3[:, 8:10, :],
                        in1=gqk[:, 1:2, :].broadcast_to([128, 2, 128]), op=ALU.mult, R=[nq_.b, cb], W=[nq_.b])
                    Cb = cs.t[:, 0:64].unsqueeze(1).broadcast_to([128, 10, 64])
                    Sb = cs.t[:, 64:128].unsqueeze(1).broadcast_to([128, 10, 64])
                    A, B = n3[:, :, 0:64], n3[:, :, 64:128]
                    t03 = t0.t[:, 0:1280].rearrange("p (h d) -> p h d", h=10)
                    t13 = t1.t[:, 0:1280].rearrange("p (h d) -> p h d", h=10)
                    P.t("dve", "tensor_tensor", out=t03[:, :, 0:64], in0=A, in1=Cb, op=ALU.mult, R=[nq_.b, cs.b], W=[t0.b])
                    P.t("dve", "tensor_tensor", out=t03[:, :, 64:128], in0=B, in1=Sb, op=ALU.mult, R=[nq_.b, cs.b], W=[t0.b])
                    P.t("dve", "tensor_tensor", out=t13[:, :, 0:64], in0=B, in1=Cb, op=ALU.mult, R=[nq_.b, cs.b], W=[t1.b])
                    P.t("dve", "tensor_tensor", out=t13[:, :, 64:128], in0=A, in1=Sb, op=ALU.mult, R=[nq_.b, cs.b], W=[t1.b])
                    qb3 = qkbf.t[:, 0:1280].rearrange("p (h d) -> p h d", h=10)
                    P.t("dve", "tensor_tensor", out=qb3[:, :, 0:64], in0=t03[:, :, 0:64], in1=t03[:, :, 64:128], op=ALU.subtract,
                        R=[t0.b], W=[qkbf.b])
                    P.t("dve", "tensor_tensor", out=qb3[:, :, 64:128], in0=t13[:, :, 0:64], in1=t13[:, :, 64:128], op=ALU.add,
                        R=[t1.b], W=[qkbf.b])
                    transpose_store(qkbf.t[:, 0:1024], qkbf.b, 1024, D["QT"], tok, "act")
                    transpose_store(qkbf.t[:, 1024:1280], qkbf.b, 256, kt_dst, kt_tok, "act")
                else:
                    proj(0, 2048, 1, hT.t, hT.b, w_in, 8)
                    P.t("act", "activation", out=qkbf.t[:, 0:1024], in_=ps[:, 1:3, :].rearrange("p b n -> p (b n)"), func=AF.Copy,
                        R=[bank[1], bank[2]], W=[qkbf.b])
                    P.t("dve", "tensor_copy", out=qkbf.t[:, 1024:2048], in_=ps[:, 3:5, :].rearrange("p b n -> p (b n)"),
                        R=[bank[3], bank[4]], W=[qkbf.b])
                    transpose_store(qkbf.t[:, 0:1024], qkbf.b, 1024, D["QT"], tok, "act")
                    transpose_store(qkbf.t[:, 1024:2048], qkbf.b, 1024, kt_dst, kt_tok, "dve")
                    proj(2048, 2048, 1, hT.t, hT.b, w_in, 8)
                    P.t("dve", "tensor_copy", out=vbf.t[:, :], in_=ps[:, 1:3, :].rearrange("p b n -> p (b n)"),
                        R=[bank[1], bank[2]], W=[vbf.b])
                    P.t("sp", "dma_start", out=v_dst, in_=vbf.t[:, :], R=[vbf.b])
                    P.t("act", "activation", out=gsb.t[:, :], in_=ps[:, 3:5, :].rearrange("p b n -> p (b n)"), func=AF.Silu,
                        R=[bank[3], bank[4]], W=[gsb.b])
                    P.t("sp", "dma_start", out=D["G"][tok:tok + 128, :], in_=gsb.t[:, :], R=[gsb.b])
            P.finish()
            P.emit()

    def gather(self, L, D):
        nc = self.nc
        P = Prog(nc, self.S)
        cc = self.S.get("cc")
        rg = [[0, 1, 2, 3], [4, 5, 6, 7]]
        ncc = 0
        for (r0, n, dst) in D["KTa"]:
            P.op("pool", "collective_compute", "AllGather", ALU.bypass, replica_groups=rg,
                 ins=[D["KTl"][r0:r0 + n, :]], outs=[dst], inc=cc, amt=1)
        RCV = min(self.cfg.TS, 262144 // D["Vl"].shape[1])
        for (r0, dst) in D["Va"]:
            P.op("pool", "collective_compute", "AllGather", ALU.bypass, replica_groups=rg,
                 ins=[D["Vl"][r0:r0 + RCV, :]], outs=[dst], inc=cc, amt=1)
        P.wait("pool", cc)
        P.emit()

    def diff_setup(self, L, D):
        nc, ps = self.nc, self.ps
        P = Prog(nc, self.S)
        self.TT = self.dscr("TT", [8, TREV_L], F32)
        self.STRIPS = self.dscr("STRIPS", [8, 128, STRIP_W], F32)
        with contextlib.ExitStack() as self.es:
            rb = TB(self, "ds_rb", [32, 8], F32)
            oh = TB(self, "ds_oh", [32, TREV_L], F32)
            sel = TB(self, "ds_sel", [32, 23], F32)
            rs = TB(self, "ds_rs", [32, 8, 23], F32)
            ones = TB(self, "ds_ones", [32, 128], F32)
            tt = TB(self, "ds_tt", [8, TREV_L], F32)
            bank = [Buf(excl=True) for _ in range(8)]
            mkb = Buf()
            P.t("sp", "dma_start", out=rb.t[:, :], in_=self.w["rel_bias"][:, :], W=[rb.b])
            P.t("sp", "dma_start", out=oh.t[:, :], in_=self.dif_oh[:, :], W=[oh.b])
            P.t("sp", "dma_start", out=sel.t[:, 0:10], in_=self.dif_sel[:, :], W=[sel.b])
            P.t("sp", "dma_start", out=sel.t[:, 10:23], in_=self.dif_sel2[:, :], W=[sel.b])
            P.t("sp", "dma_start", out=self.MK[:, :], in_=self.dif_m[:, :], W=[mkb])
            P.t("pool", "memset", ones.t[:, :], 1.0, W=[ones.b])
            for k, n0 in enumerate(range(0, TREV_L, 512)):
                P.t("pe", "matmul", ps[0:8, k, 0:512], rb.t[:, :], oh.t[:, n0:n0 + 512], start=True, stop=True,
                    R=[rb.b, oh.b], W=[bank[k]])
                P.t("dve", "tensor_copy", out=tt.t[:, n0:n0 + 512], in_=ps[0:8, k, 0:512], R=[bank[k]], W=[tt.b])
            ttd = Buf()
            P.t("sp", "dma_start", out=self.TT[:, :], in_=tt.t[:, :], R=[tt.b], W=[ttd])
            P.t("dve", "tensor_tensor", out=rs.t[:, :, :], in0=rb.t[:, :].unsqueeze(2).broadcast_to([32, 8, 23]),
                in1=sel.t[:, :].unsqueeze(1).broadcast_to([32, 8, 23]), op=ALU.mult, R=[rb.b, sel.b], W=[rs.b])
            P.t("pe", "matmul", ps[:, 4, 0:184], ones.t[:, :], rs.t[:, :, :].rearrange("b h c -> b (h c)"), start=True, stop=True,
                R=[ones.b, rs.b], W=[bank[4]])
            cbb = Buf()
            P.t("dve", "tensor_copy", out=self.CB[:, :, :].rearrange("p h c -> p (h c)"), in_=ps[:, 4, 0:184], R=[bank[4]], W=[cbb])
            J = TB(self, "ds_J", [128, 128], F32)
            P.t("sp", "dma_start", out=J.t[:, :], in_=self.antiid[:, :], W=[J.b])
            sR = [TB(self, "ds_sR%d" % i, [128, STRIP_W], F32) for i in range(2)]
            sO = [TB(self, "ds_sO%d" % i, [128, STRIP_W], F32) for i in range(2)]
            kk = 0
            for h in range(8):
                a, o = sR[h % 2], sO[h % 2]
                src = bass.AP(tensor=self.TT.tensor, offset=h * TREV_L, ap=[[1, 128], [1, STRIP_W]])
                P.t("sp", "dma_start", out=a.t[:, :], in_=src, R=[ttd], W=[a.b])
                for n0 in range(0, STRIP_W, 512):
                    nn = min(512, STRIP_W - n0)
                    bk = 5 + kk % 3
                    kk += 1
                    P.t("pe", "matmul", ps[:, bk, 0:nn], J.t[:, :], a.t[:, n0:n0 + nn], start=True, stop=True,
                        R=[J.b, a.b], W=[bank[bk]])
                    P.t("dve", "tensor_copy", out=o.t[:, n0:n0 + nn], in_=ps[:, bk, 0:nn], R=[bank[bk]], W=[o.b])
                P.t("act", "activation", out=o.t[:, :], in_=o.t[:, :], func=AF.Exp, R=[o.b], W=[o.b])
                P.t("dve", "tensor_scalar", out=o.t[:, :], in0=o.t[:, :], scalar1=-1.0, scalar2=None, op0=ALU.add, R=[o.b], W=[o.b])
                P.t("sp", "dma_start", out=self.STRIPS[h, :, :], in_=o.t[:, :], R=[o.b])
            P.finish()
            P.emit()

    def attention(self, L, kind, g, D):
        nc, cfg, ps, S = self.nc, self.cfg, self.ps, self.S
        P = Prog(nc, S)
        LT, TS, TP, NPS, SK = cfg.LT, cfg.TS, cfg.TP, cfg.NPS, cfg.SK
        dk, dv, scale = g["dk"], g["dv"], float(g["scale"])
        FK = g["FK"]
        heads = kind_heads(kind)
        TSK = SK // 128
        s_kv = [S.get("kv0"), S.get("kv1")]
        s_S, s_P, s_PV, s_Oe = S.get("aS"), S.get("aP"), S.get("aPV"), S.get("aOe")
        s_st = [S.get("ast0"), S.get("ast1")]
        s_F, s_str, s_ms = S.get("aF"), S.get("astr"), S.get("ams")
        fix_need = {}
        nO = 1 if 4 * (dv + 1) <= 512 else 2
        with contextlib.ExitStack() as self.es:
            kt_sb = [self.sb("kt%d_%d" % (L, i), [128, SK], BF16) for i in range(2)]
            v_sb = [self.sb("v%d_%d" % (L, i), [128, TSK, dv + 1], BF16) for i in range(2)]
            q_sb = [self.sb("q%d_%d" % (L, i), [128, max(TS, TP)], BF16) for i in range(2)]
            NPB = 4
            p_sb = self.sb("p%d" % L, [128, NPB, 1024], BF16)
            on_sb = self.sb("on%d" % L, [128, 2, 4, dv], F32)
            rc_sb = self.sb("rc%d" % L, [128, 2, 4], F32)
            if kind == 2:
                strip = self.sb("strip%d" % L, [128, STRIP_W], F32)
                ftmp = self.sb("ftmp%d" % L, [128, 4, 1024], BF16)
            for i in range(2):
                P.op("pool", "memset", v_sb[i][:, :, dv:dv + 1], 1.0, inc=s_ms)
            P.wait("pe", s_ms)
            jobs = []
            for hd in heads:
                jobs.append((hd, "S", 0))
                for s in range(NPS):
                    jobs.append((hd, "P", s))

            def issue_load(ji):
                hd, typ, s = jobs[ji]
                slot = ji % 2
                sem = s_kv[slot]
                if typ == "S":
                    nk, nq, tok0 = SK, TS, 0
                else:
                    nk, nq, tok0 = TP, TP, TS + s * TP
                if ji >= 2:
                    P.wait("sp", s_PV, jobs_qt_end[ji - 2])
                P.op("sp", "dma_start", out=q_sb[slot][0:dk, 0:nq], in_=D["QT"][hd["qrow"]:hd["qrow"] + dk, tok0:tok0 + nq], inc=sem)
                for (r0, n, p0) in hd["kparts"]:
                    if typ == "S":
                        for (c0, cn, ct) in D["KTa"]:
                            lo, hi = max(r0, c0), min(r0 + n, c0 + cn)
                            if lo >= hi:
                                continue
                            src = ct.rearrange("(r d) t -> d r t", r=NR)[lo - c0:hi - c0, :, :]
                            dst = kt_sb[slot][p0 + lo - r0:p0 + hi - r0, 0:SK].rearrange("d (r t) -> d r t", r=NR)
                            P.op("sp", "dma_start", out=dst, in_=src, inc=sem)
                    else:
                        P.op("sp", "dma_start", out=kt_sb[slot][p0:p0 + n, 0:TP], in_=D["KTp"][r0:r0 + n, s * TP:(s + 1) * TP], inc=sem)
                if typ == "S":
                    RCV = min(TS, 262144 // g["WV"])
                    ii = RCV // 128
                    njj = TS // RCV
                    for jj, (c0, vt) in enumerate(D["Va"]):
                        for r in range(NR):
                            src = vt[r * RCV:(r + 1) * RCV, hd["vcol"]:hd["vcol"] + dv].rearrange("(ii p) c -> p ii c", p=128)
                            t0 = r * (TS // 128) + jj * ii
                            P.op("sp", "dma_start", out=v_sb[slot][:, t0:t0 + ii, 0:dv], in_=src, inc=sem)
                else:
                    vsrc = D["Vp"][s * TP:(s + 1) * TP, :]
                    CHT = 16
                    for t0 in range(0, nk // 128, CHT):
                        tn = min(CHT, nk // 128 - t0)
                        P.op("sp", "dma_start", out=v_sb[slot][:, t0:t0 + tn, 0:dv],
                             in_=vsrc[t0 * 128:(t0 + tn) * 128, hd["vcol"]:hd["vcol"] + dv].rearrange("(t p) c -> p t c", p=128), inc=sem)
                return sem.v

            jobs_qt_end = []
            acc = self.qt_count
            for (hd, typ, s) in jobs:
                acc += (TS if typ == "S" else TP) // 512
                jobs_qt_end.append(acc)
            m = self.qt_count
            n = self.grp_count
            kv_ready = {}
            kv_ready[0] = issue_load(0)
            cur_strip = None
            for ji, (hd, typ, s) in enumerate(jobs):
                slot = ji % 2
                if ji + 1 < len(jobs):
                    kv_ready[ji + 1] = issue_load(ji + 1)
                if typ == "S":
                    nk, nq, tok0 = SK, TS, 0
                else:
                    nk, nq, tok0 = TP, TP, TS + s * TP
                T = nk // 128
                NG = T // 2
                P.wait("pe", s_kv[slot], kv_ready[ji])
                if kind == 2 and cur_strip != hd["bh"]:
                    P.wait("sp", s_F)
                    P.op("sp", "dma_start", out=strip[:, :], in_=self.STRIPS[hd["bh"], :, :], inc=s_str)
                    P.wait("dve", s_str)
                    cur_strip = hd["bh"]
                for qi in range(nq // 512):
                    ob = m % 2
                    obank = 4 + ob * nO

                    def oacc(j):
                        if nO == 1:
                            return ps[:, obank, j * (dv + 1):(j + 1) * (dv + 1)]
                        return ps[:, obank + j // 2, (j % 2) * (dv + 1):(j % 2 + 1) * (dv + 1)]

                    def qk(gi, n):
                        sb = (n % 2) * 2
                        for i in range(2):
                            t = gi * 2 + i
                            P.op("pe", "matmul", ps[:, sb + i, :], kt_sb[slot][0:dk, t * 128:(t + 1) * 128],
                                 q_sb[slot][0:dk, qi * 512:(qi + 1) * 512], start=True, stop=True,
                                 inc=(s_S if i == 1 else None))

                    def ex(gi, n):
                        sb = (n % 2) * 2
                        src = ps[:, sb:sb + 2, :].rearrange("p b n -> p (b n)")
                        dst = p_sb[:, n % NPB, :]
                        if kind != 2:
                            P.wait("act", s_S, n + 1)
                            P.op("act", "activation", out=dst, in_=src, func=AF.Exp, scale=scale, inc=s_P)
                            return
                        t = gi * 2
                        T32 = TS // 128
                        if typ == "S":
                            rho = t // T32
                            tl = t - rho * T32
                            dcs = (-1, 0, 1)
                        else:
                            rho, tl = 4, t
                            dcs = (0,)
                        delta = tl - qi * 4
                        near = [dc for dc in dcs if -256 <= dc * TS + delta * 128 <= 512]
                        assert len(near) <= 1
                        if not near:
                            side = 0 if delta < 0 else 1
                            P.wait("act", s_S, n + 1)
                            P.op("act", "activation", out=dst, in_=src, func=AF.Exp, scale=scale,
                                 bias=self.CB[:, hd["bh"], rho * 2 + side:rho * 2 + side + 1], inc=s_P)
                        else:
                            dc = near[0]
                            w = 4 if typ != "S" else {0: rho, -1: 5 + rho, 1: 9 + rho}[dc]
                            P.wait("act", s_S, n + 1)
                            tk = P.op("act", "activation", out=dst, in_=src, func=AF.Exp, scale=scale,
                                      bias=self.CB[:, hd["bh"], 10 + w:11 + w], inc=s_P)
                            fslot = self.fcount % 4
                            self.fcount += 1
                            P.wait("dve", tk)
                            for i in range(2):
                                off = STRIP_C - (dc * TS + (delta + i) * 128)
                                P.op("dve", "scalar_tensor_tensor", out=ftmp[:, fslot, i * 512:(i + 1) * 512],
                                     in0=strip[:, off:off + 512], scalar=self.MK[:, w:w + 1], in1=dst[:, i * 512:(i + 1) * 512],
                                     op0=ALU.mult, op1=ALU.mult, inc=(s_F if i == 1 else None))
                            return (fslot, s_F.v, gi)
                        return None

                    def pvd(ent, final):
                        fslot, fval, g0 = ent
                        P.wait("pe", s_F, fval)
                        for i in range(2):
                            t = g0 * 2 + i
                            for j in range(4):
                                fin = final and i == 1 and j == 3
                                P.op("pe", "matmul", oacc(j), ftmp[:, fslot, i * 512 + j * 128:i * 512 + (j + 1) * 128],
                                     v_sb[slot][:, t, :], start=False, stop=fin, skip_group_check=True,
                                     inc=(s_PV if fin else None))

                    def pv(gi, n, first, last):
                        P.wait("pe", s_P, n + 1)
                        if first:
                            P.wait("pe", s_Oe, m - 1)
                        for i in range(2):
                            t = gi * 2 + i
                            for j in range(4):
                                st = first and i == 0 and (j == 0 or (nO == 2 and j == 2))
                                lastmm = last and i == 1 and j == 3
                                P.op("pe", "matmul", oacc(j), p_sb[:, n % NPB, i * 512 + j * 128:i * 512 + (j + 1) * 128],
                                     v_sb[slot][:, t, :], start=st, stop=(last and i == 1), skip_group_check=True,
                                     inc=(s_PV if lastmm else None))

                    qk(0, n)
                    pend = []
                    for gi in range(NG):
                        ent = ex(gi, n + gi)
                        if gi + 1 < NG:
                            qk(gi + 1, n + gi + 1)
                        lastg = gi == NG - 1
                        if ent is not None:
                            pend.append(ent)
                        flush = [e for e in pend if lastg or e[2] + 2 <= gi]
                        pend = [e for e in pend if e not in flush]
                        pv(gi, n + gi, gi == 0, lastg and not flush)
                        for k, e in enumerate(flush):
                            pvd(e, lastg and k == len(flush) - 1)
                    n += NG
                    P.wait("dve", s_PV, m + 1)
                    P.wait("dve", s_st[ob])
                    tk = None
                    for j in range(4):
                        tk = P.op("dve", "reciprocal", rc_sb[:, ob, j:j + 1], oacc(j)[:, dv:dv + 1], inc=S.selfsem["dve"])
                    P.wait("dve", tk)
                    for j in range(4):
                        P.op("dve", "tensor_scalar", out=on_sb[:, ob, j, :], in0=oacc(j)[:, 0:dv], scalar1=rc_sb[:, ob, j:j + 1],
                             scalar2=None, op0=ALU.mult, inc=(s_Oe if j == 3 else None))
                    P.wait("pool", s_Oe, m + 1)
                    r0 = tok0 + qi * 512
                    P.op("pool", "dma_start", out=D["O"][r0:r0 + 512, hd["ocol"]:hd["ocol"] + dv].rearrange("(j p) d -> p j d", p=128),
                         in_=on_sb[:, ob, :, :], inc=s_st[ob])
                    m += 1
            self.qt_count = m
            self.grp_count = n
            for sem in s_st + s_kv + [s_str]:
                P.wait("pool", sem)
            P.wait("sp", s_kv[0])
            P.wait("sp", s_kv[1])
            P.emit()

    def phase_c(self, L, kind, idx, g, D, xsrc):
        nc, cfg, ps = self.nc, self.cfg, self.ps
        P = Prog(nc, self.S)
        LT = cfg.LT
        OW = g["OW"]
        with contextlib.ExitStack() as self.es:
            self.stg = [TB(self, "cstg%d_%d" % (L, i), [128, 2048], F32) for i in range(2)]
            self.stg_i = 0
            self.wbuf = Buf()
            pre = {0: "mla", 1: "gqa", 2: "dif"}[kind]
            w_o = self.load_w_bf16(P, self.w[pre + "_w_o"][idx], 1024, DM, "w_o%d" % L)
            cb = Buf()
            gpost = self.sb("gpost%d" % L, [128, DM], F32)
            self.bcast_row(P, gpost[:, :], self.w["norm_post"][L:L + 1, :], DM, cb)
            bank = [Buf(excl=True) for _ in range(8)]
            if kind == 2:
                lam_init = 0.8 - 0.6 * math.exp(-0.3 * L)
                lv = TB(self, "lv%d" % L, [128, 4, 64], F32)
                for k, nm in enumerate(("dif_lam_q1", "dif_lam_k1", "dif_lam_q2", "dif_lam_k2")):
                    self.bcast_row(P, lv.t[:, k, :], self.w[nm][idx:idx + 1, :], 64, lv.b)
                lp = TB(self, "lp%d" % L, [128, 2, 64], F32)
                lam = TB(self, "lam%d" % L, [128, 4], F32)
                P.t("dve", "tensor_tensor", out=lp.t[:, 0, :], in0=lv.t[:, 0, :], in1=lv.t[:, 1, :], op=ALU.mult, R=[lv.b], W=[lp.b])
                P.t("dve", "tensor_tensor", out=lp.t[:, 1, :], in0=lv.t[:, 2, :], in1=lv.t[:, 3, :], op=ALU.mult, R=[lv.b], W=[lp.b])
                P.t("dve", "tensor_reduce", out=lam.t[:, 0:2], in_=lp.t[:, :, :], axis=AX.X, op=ALU.add, R=[lp.b], W=[lam.b])
                P.t("act", "activation", out=lam.t[:, 0:2], in_=lam.t[:, 0:2], func=AF.Exp, R=[lam.b], W=[lam.b])
                P.t("dve", "tensor_tensor", out=lam.t[:, 2:3], in0=lam.t[:, 1:2], in1=lam.t[:, 0:1], op=ALU.subtract, R=[lam.b], W=[lam.b])
                P.t("dve", "tensor_scalar", out=lam.t[:, 3:4], in0=lam.t[:, 2:3], scalar1=-lam_init, scalar2=None, op0=ALU.add,
                    R=[lam.b], W=[lam.b])
                gsub = self.sb("gsub%d" % L, [128, 128], F32)
                self.bcast_row(P, gsub[:, :], self.w["dif_g_sub"][idx:idx + 1, :], 128, cb)
                P.t("dve", "tensor_scalar", out=gsub[:, :], in0=gsub[:, :], scalar1=1.0 - lam_init, scalar2=None, op0=ALU.mult,
                    R=[cb], W=[cb])
            ots = [TB(self, "ot%d_%d" % (L, i), [128, OW], F32) for i in range(2)]
            gts = [TB(self, "gt%d_%d" % (L, i), [128, DM], F32) for i in range(2)]
            xts = [TB(self, "cx%d_%d" % (L, i), [128, DM], F32) for i in range(2)]
            def two(name, shape, dt):
                return [TB(self, "%s%d_%d" % (name, L, i), shape, dt) for i in range(2)]
            og2 = two("og", [128, DM], BF16)
            ogT2 = two("ogT", [128, 8, 128], BF16)
            junk2 = two("cjunk", [128, DM], BF16)
            ss2 = two("css", [128, 16], F32)
            rstd2 = two("crstd", [128, 16], F32)
            yt = [TB(self, "yt%d_%d" % (L, i), [128, DM], F32) for i in range(2)]
            od2 = two("od", [128, DM], F32) if kind == 2 else [None, None]
            sq2 = two("csq", [128, DM], F32) if kind == 2 else [None, None]
            for i in range(LT // 128):
                tok = i * 128
                ot, gt, xt, y_ = ots[i % 2], gts[i % 2], xts[i % 2], yt[i % 2]
                k2 = i % 2
                og, ogT, junk, ss, rstd, od, sq = og2[k2], ogT2[k2], junk2[k2], ss2[k2], rstd2[k2], od2[k2], sq2[k2]
                b0 = 3 * k2
                P.t("sp", "dma_start", out=ot.t[:, :], in_=D["O"][tok:tok + 128, :], W=[ot.b])
                P.t("sp", "dma_start", out=gt.t[:, :], in_=D["G"][tok:tok + 128, :], W=[gt.b])
                P.t("sp", "dma_start", out=xt.t[:, :], in_=xsrc[tok:tok + 128, :], W=[xt.b])
                if kind != 2:
                    P.t("dve", "tensor_tensor", out=og.t[:, :], in0=ot.t[:, :], in1=gt.t[:, :], op=ALU.mult, R=[ot.b, gt.b], W=[og.b])
                else:
                    o4 = ot.t[:, :].rearrange("p (h j d) -> p h j d", h=8, j=2)
                    od3 = od.t[:, :].rearrange("p (h d) -> p h d", h=8)
                    P.t("dve", "scalar_tensor_tensor", out=od3, in0=o4[:, :, 1, :], scalar=lam.t[:, 3:4], in1=o4[:, :, 0, :],
                        op0=ALU.mult, op1=ALU.add, R=[ot.b, lam.b], W=[od.b])
                    P.t("act", "activation", out=sq.t[:, :], in_=od.t[:, :], func=AF.Square, R=[od.b], W=[sq.b])
                    P.t("dve", "tensor_reduce", out=ss.t[:, 4:12], in_=sq.t[:, :].rearrange("p (h d) -> p h d", h=8), axis=AX.X,
                        op=ALU.add, R=[sq.b], W=[ss.b])
                    self.rstd_ops(P, ss.t[:, 4:12], ss.b, rstd.t[:, 4:12], rstd.b, 128)
                    P.t("dve", "tensor_tensor", out=od3, in0=od3, in1=rstd.t[:, 4:12].unsqueeze(2).broadcast_to([128, 8, 128]),
                        op=ALU.mult, R=[od.b, rstd.b], W=[od.b])
                    P.t("dve", "tensor_tensor", out=od3, in0=od3, in1=gsub[:, :].unsqueeze(1).broadcast_to([128, 8, 128]),
                        op=ALU.mult, R=[od.b, cb], W=[od.b])
                    P.t("dve", "tensor_tensor", out=og.t[:, :], in0=od.t[:, :], in1=gt.t[:, :], op=ALU.mult, R=[od.b, gt.b], W=[og.b])
                for c in range(8):
                    P.t("pe", "transpose", out=self.psb(b0)[:, c * 128:(c + 1) * 128], in_=og.t[:, c * 128:(c + 1) * 128],
                        identity=self.ident[:, :], R=[og.b], W=[bank[b0]], tok=(c == 7))
                P.t("act", "activation", out=ogT.t[:, :, :], in_=self.psb(b0)[:, :].rearrange("p (c t) -> p c t", c=8), func=AF.Copy,
                    R=[bank[b0]], W=[ogT.b])
                for b in range(2):
                    for c in range(8):
                        P.t("pe", "matmul", ps[:, b0 + 1 + b, :], ogT.t[:, c, :], w_o[:, c, b * 512:(b + 1) * 512], start=(c == 0), stop=(c == 7),
                            R=[ogT.b, self.wbuf], W=[bank[b0 + 1 + b]], tok=(c == 7))
                mv = ps[:, b0 + 1:b0 + 3, :].rearrange("p b n -> p (b n)")
                P.t("act", "activation", out=junk.t[:, :], in_=mv, func=AF.Square, accum_out=ss.t[:, 0:1],
                    R=[bank[b0 + 1], bank[b0 + 2]], W=[junk.b, ss.b])
                self.rstd_ops(P, ss.t[:, 0:1], ss.b, rstd.t[:, 0:1], rstd.b, DM)
                P.t("dve", "scalar_tensor_tensor", out=y_.t[:, :], in0=mv, scalar=rstd.t[:, 0:1], in1=gpost[:, :],
                    op0=ALU.mult, op1=ALU.mult, R=[bank[b0 + 1], bank[b0 + 2], rstd.b, cb], W=[y_.b])
                P.t("pool", "tensor_tensor", out=y_.t[:, :], in0=y_.t[:, :], in1=xt.t[:, :], op=ALU.add, R=[y_.b, xt.b], W=[y_.b])
                P.t("sp", "dma_start", out=self.y[tok:tok + 128, :], in_=y_.t[:, :], R=[y_.b])
            P.finish()
            P.emit()


def _rope_angles(pos, dim):
    inv = (np.float32(10000.0) ** (-(np.arange(0, dim, 2, dtype=np.float32) / np.float32(dim)))).astype(np.float32)
    return (pos.astype(np.float32)[:, None] * inv[None, :]).astype(np.float32)


def _t5_bucket(rel):
    half, max_exact = 16, 8
    base = (rel > 0).astype(np.int32) * half
    n = np.abs(rel)
    nf = np.maximum(n, 1).astype(np.float32)
    large = max_exact + ((np.log(nf / np.float32(max_exact)) / np.float32(math.log(128 / max_exact)))
                         * np.float32(half - max_exact)).astype(np.int32)
    large = np.minimum(large, half - 1)
    return base + np.where(n < max_exact, n, large)


def host_tables(cfg, core):
    TS, TP, NPS, LT = cfg.TS, cfg.TP, cfg.NPS, cfg.LT
    r = core % NR
    pos = np.concatenate([r * TS + np.arange(TS)] + [np.arange(TP)] * NPS).astype(np.int64)
    a = _rope_angles(pos, 32)
    mla_cs = np.concatenate([np.cos(a), np.sin(a)], axis=1).astype(np.float32)
    ang = np.concatenate([_rope_angles(pos // 64, 64), _rope_angles(pos % 64, 64)], axis=1)
    gqa_cs = np.concatenate([np.cos(ang), np.sin(ang)], axis=1).astype(np.float32)
    i = np.arange(TREV_L)
    bk = _t5_bucket(767 - i)
    real = np.zeros((32, TREV_L), np.float32)
    real[bk, i] = 1.0
    sel = np.zeros((32, 10), np.float32)
    for rho in range(5):
        if rho == 4 or rho == r:
            sel[15, rho * 2 + 0] = 1.0
            sel[31, rho * 2 + 1] = 1.0
        elif rho > r:
            sel[31, rho * 2:rho * 2 + 2] = 1.0
        else:
            sel[15, rho * 2:rho * 2 + 2] = 1.0
    sel2 = np.zeros((32, 13), np.float32)
    mk = np.zeros((128, 13), np.float32)
    mk[:, 4] = 1.0
    for rho in range(4):
        for dc, w in ((0, rho), (-1, 5 + rho), (1, 9 + rho)):
            if rho - r == dc:
                mk[:, w] = 1.0
            else:
                if rho != r:
                    pos = rho > r
                else:
                    pos = dc < 0
                sel2[31 if pos else 15, w] = 1.0
    return dict(mla_cs=mla_cs, gqa_cs=gqa_cs, dif_oh=real, dif_sel=sel, dif_sel2=sel2, dif_m=mk,
                ident=np.eye(128, dtype=np.float32).astype(NPBF),
                antiid=np.ascontiguousarray(np.eye(128, dtype=np.float32)[::-1]))


_WNAMES = ["norm_pre", "norm_post", "rel_bias", "mla_w_in", "mla_g_q", "mla_w_uq", "mla_g_kv", "mla_w_ukv", "mla_w_o",
           "gqa_w_in", "gqa_g_q", "gqa_g_k", "gqa_w_o", "dif_w_in", "dif_lam_q1", "dif_lam_k1", "dif_lam_q2",
           "dif_lam_k2", "dif_g_sub", "dif_w_o"]


def run(cfg, x_prompt, x_sample, weights, debug=(), trace=False, stop=None):
    TS, TP, NPS = cfg.TS, cfg.TP, cfg.NPS
    b = Builder(cfg, debug=debug, stop=stop)
    nc = b.build()
    in_maps = []
    for c in range(NCORES):
        sb, r = c // NR, c % NR
        xs = [x_sample[sb, r * TS:(r + 1) * TS]] + [x_prompt[c * NPS + s] for s in range(NPS)]
        m = {"x_in": np.ascontiguousarray(np.concatenate(xs, axis=0), dtype=np.float32)}
        for n in _WNAMES:
            m[n] = np.ascontiguousarray(weights[n], dtype=np.float32)
        m.update(host_tables(cfg, c))
        in_maps.append(m)
    res = run_bass_kernel_spmd(nc, in_maps, core_ids=list(range(NCORES)), trace=trace)
    return res


def kernel(**inputs):
    cfg = Cfg()
    x_prompt = np.asarray(inputs["x_prompt"], dtype=np.float32)
    x_sample = np.asarray(inputs["x_sample"], dtype=np.float32)
    weights = {n: np.asarray(inputs[n]) for n in _WNAMES}
    res = run(cfg, x_prompt, x_sample, weights)
    TS, TP, NPS = cfg.TS, cfg.TP, cfg.NPS
    y_prompt = np.empty_like(x_prompt)
    y_sample = np.empty_like(x_sample)
    for c in range(NCORES):
        y = res.results[c]["y"]
        sb, r = c // NR, c % NR
        y_sample[sb, r * TS:(r + 1) * TS] = y[0:TS]
        for s in range(NPS):
            y_prompt[c * NPS + s] = y[TS + s * TP:TS + (s + 1) * TP]
    return (y_prompt, y_sample)
```

```python
import math
import contextlib
import numpy as np
import ml_dtypes
import concourse.bass as bass
import concourse.mybir as mybir
from concourse.bass_utils import run_bass_kernel_spmd

F32 = mybir.dt.float32
BF16 = mybir.dt.bfloat16
AF = mybir.ActivationFunctionType
ALU = mybir.AluOpType
AX = mybir.AxisListType
NPBF = ml_dtypes.bfloat16

DM = 1024
EPS = 1e-6
LAYER_KINDS = (0, 1, 2, 0)
NCORES = 8
STRICT = False
NR = 4
STRIP_C = 640
STRIP_W = 1408
TREV_L = 1536


class Cfg:
    def __init__(self, TS=4096, TP=2048, NPS=2):
        self.TS, self.TP, self.NPS = TS, TP, NPS
        self.LT = TS + NPS * TP
        self.SK = NR * TS


class Sem:
    def __init__(self, nc, name):
        self.h = nc.alloc_semaphore(name=name)
        self.v = 0
        self.name = name


class Buf:
    __slots__ = ("w", "r", "excl")

    def __init__(self, excl=False):
        self.w = None
        self.r = []
        self.excl = excl


class Sems:
    ENG = ("pe", "act", "dve", "pool", "sp")

    def __init__(self, nc):
        self.nc = nc
        self.selfsem = {k: Sem(nc, "self_" + k) for k in self.ENG}
        self.ring = {k: [Sem(nc, "ring_%s%d" % (k, i)) for i in range(8)] for k in ("sp", "pool", "act")}
        self.ridx = {k: 0 for k in self.ring}
        self.seen = {}
        self.named = {}

    def get(self, name):
        if name not in self.named:
            self.named[name] = Sem(self.nc, name)
        return self.named[name]


class Prog:
    ENG = Sems.ENG

    def __init__(self, nc, S):
        self.nc = nc
        self.S = S
        self.ops = {k: [] for k in self.ENG}
        self.pending = {k: [] for k in self.ENG}
        self.dma_sems = set()

    def op(self, eng, meth, *args, inc=None, amt=None, **kw):
        tok = None
        if inc is not None:
            if amt is None:
                amt = 16 if meth == "dma_start" else 1
            inc.v += amt
            tok = (inc, inc.v, eng, meth == "dma_start")
            if meth == "dma_start":
                self.dma_sems.add(inc)
        self.ops[eng].append((meth, args, kw, inc, amt))
        return tok

    def wait(self, eng, sem, val=None):
        if isinstance(sem, tuple):
            sem, val = sem[0], sem[1]
        if val is None:
            val = sem.v
        if val <= 0:
            return
        key = (eng, sem.name)
        if self.S.seen.get(key, 0) >= val:
            return
        self.S.seen[key] = val
        self.ops[eng].append(("wait_ge", (sem.h, val), {}, None, None))

    def _dep(self, eng, tok, raw):
        sem, val, src, is_dma = tok
        if src == eng and not is_dma and not raw and not STRICT:
            return
        if src == eng and eng == "pe":
            return
        self.wait(eng, sem, val)

    def t(self, eng, meth, *args, R=(), W=(), tok=True, **kw):
        xr = [b for b in R if b.excl]
        if xr:
            R = [b for b in R if not b.excl]
            W = list(W) + xr
        for b in R:
            if b.w is not None:
                self._dep(eng, b.w, True)
        for b in W:
            if b.w is not None:
                self._dep(eng, b.w, False)
            for tk in b.r:
                self._dep(eng, tk, False)
        token = None
        if meth == "dma_start":
            ring = self.S.ring[eng]
            s = ring[self.S.ridx[eng] % len(ring)]
            self.S.ridx[eng] += 1
            self.wait(eng, s, s.v)
            token = self.op(eng, meth, *args, inc=s, amt=16, **kw)
        elif tok:
            token = self.op(eng, meth, *args, inc=self.S.selfsem[eng], amt=1, **kw)
        else:
            self.op(eng, meth, *args, **kw)
        if token is None:
            self.pending[eng].append((tuple(R), tuple(W)))
            return None
        groups = self.pending[eng] + [(tuple(R), tuple(W))]
        self.pending[eng] = []
        for (rr, ww) in groups:
            for b in rr:
                b.r.append(token)
            for b in ww:
                b.w = token
                b.r = []
        return token

    def finish(self):
        for eng in ("sp", "pool", "act"):
            for s in self.S.ring[eng]:
                if s.v > 0:
                    self.wait(eng, s, s.v)
        for s in self.dma_sems:
            self.wait("sp", s, s.v)

    def emit(self):
        nc = self.nc

        def run(e, lst):
            for meth, args, kw, inc, amt in lst:
                ins = getattr(e, meth)(*args, **kw)
                if inc is not None:
                    ins.then_inc(inc.h, amt)
        with nc.Block() as block:
            @block.tensor
            def _(e):
                run(e, self.ops["pe"])

            @block.scalar
            def _(e):
                run(e, self.ops["act"])

            @block.vector
            def _(e):
                run(e, self.ops["dve"])

            @block.gpsimd
            def _(e):
                run(e, self.ops["pool"])

            @block.sync
            def _(e):
                run(e, self.ops["sp"])


class TB:
    def __init__(self, bld, name, shape, dt):
        self.t = bld.sb(name, shape, dt)
        self.b = Buf()


def kind_geom(kind):
    if kind == 0:
        return dict(NIN=1440, FQ=1536, FK=1056, WV=1024, OW=1024, dk=96, dv=64, scale=1.0 / math.sqrt(96.0))
    if kind == 1:
        return dict(NIN=2560, FQ=1024, FK=256, WV=256, OW=1024, dk=128, dv=128, scale=1.0 / math.sqrt(128.0))
    return dict(NIN=4096, FQ=1024, FK=1024, WV=1024, OW=2048, dk=64, dv=128, scale=1.0 / math.sqrt(64.0))


def kind_heads(kind):
    hs = []
    if kind == 0:
        for h in range(16):
            hs.append(dict(qrow=h * 96, kparts=[(h * 64, 64, 0), (1024, 32, 64)], vcol=h * 64, ocol=h * 64, bh=None))
    elif kind == 1:
        for h in range(8):
            g = h // 4
            hs.append(dict(qrow=h * 128, kparts=[(g * 128, 128, 0)], vcol=g * 128, ocol=h * 128, bh=None))
    else:
        for h in range(8):
            for j in range(2):
                hs.append(dict(qrow=h * 128 + j * 64, kparts=[(h * 128 + j * 64, 64, 0)], vcol=h * 128,
                               ocol=(h * 2 + j) * 128, bh=h))
    return hs


class Builder:
    def __init__(self, cfg, debug=(), stop=None):
        self.cfg = cfg
        self.stop = stop
        self.debug = set(debug)
        nc = self.nc = bass.Bass("TRN2", target_bir_lowering=False)
        self.S = Sems(nc)
        LT, TS, SK = cfg.LT, cfg.TS, cfg.SK
        din = lambda n, s, d=F32: nc.dram_tensor(n, s, d, kind="ExternalInput").ap()
        self.x_in = din("x_in", [LT, DM])
        self.y = nc.dram_tensor("y", [LT, DM], F32, kind="ExternalOutput").ap()
        self.w = {}
        for n, s in [("norm_pre", [4, DM]), ("norm_post", [4, DM]), ("rel_bias", [32, 8]),
                     ("mla_w_in", [2, DM, 1440]), ("mla_g_q", [2, 256]), ("mla_w_uq", [2, 256, 1536]),
                     ("mla_g_kv", [2, 128]), ("mla_w_ukv", [2, 128, 2048]), ("mla_w_o", [2, 1024, DM]),
                     ("gqa_w_in", [1, DM, 2560]), ("gqa_g_q", [1, 128]), ("gqa_g_k", [1, 128]),
                     ("gqa_w_o", [1, 1024, DM]),
                     ("dif_w_in", [1, DM, 4096]), ("dif_lam_q1", [1, 64]), ("dif_lam_k1", [1, 64]),
                     ("dif_lam_q2", [1, 64]), ("dif_lam_k2", [1, 64]), ("dif_g_sub", [1, 128]),
                     ("dif_w_o", [1, 1024, DM])]:
            self.w[n] = din(n, s)
        self.mla_cs = din("mla_cs", [LT, 32])
        self.gqa_cs = din("gqa_cs", [LT, 128])
        self.ident_d = din("ident", [128, 128], BF16)
        self.dif_oh = din("dif_oh", [32, TREV_L])
        self.dif_sel = din("dif_sel", [32, 10])
        self.dif_sel2 = din("dif_sel2", [32, 13])
        self.dif_m = din("dif_m", [128, 13])
        self.antiid = din("antiid", [128, 128])
        self.scr = {}
        self.ps = nc.alloc_psum_tensor("ps", [128, 8, 512], F32)
        self.ident = nc.alloc_sbuf_tensor("ident_sb", [128, 128], BF16)
        self.CB = nc.alloc_sbuf_tensor("CB", [128, 8, 23], F32)
        self.MK = nc.alloc_sbuf_tensor("MK", [128, 13], F32)
        self.eps_sb = nc.alloc_sbuf_tensor("eps_sb", [128, 1], F32)
        self.es = None
        self.qt_count = 0
        self.grp_count = 0
        self.fcount = 0
        self.overlap_gather = True

    def sb(self, name, shape, dt):
        return self.es.enter_context(self.nc.sbuf_tensor(name, shape, dt))

    def dscr(self, name, shape, dt):
        kind = "ExternalOutput" if name in self.debug else "Internal"
        t = self.nc.dram_tensor(name, shape, dt, kind=kind)
        self.scr[name] = t
        return t.ap()

    def psb(self, b):
        return self.ps[:, b, :].bitcast(BF16)

    def build(self):
        nc, cfg = self.nc, self.cfg
        P = Prog(nc, self.S)
        P.t("sp", "dma_start", out=self.ident[:, :], in_=self.ident_d[:, :])
        P.t("pool", "memset", self.eps_sb[:, :], EPS)
        P.finish()
        P.emit()
        n_mla = 0
        for L, kind in enumerate(LAYER_KINDS):
            g = kind_geom(kind)
            LT, TS, SK, TP, NPS = cfg.LT, cfg.TS, cfg.SK, cfg.TP, cfg.NPS
            D = dict(
                QT=self.dscr("QT%d" % L, [g["FQ"], LT], BF16),
                KTl=self.dscr("KTl%d" % L, [g["FK"], TS], BF16),
                KTa=[(r0, min(64, g["FK"] - r0), self.dscr("KTa%d_%d" % (L, r0), [NR * min(64, g["FK"] - r0), TS], BF16))
                     for r0 in range(0, g["FK"], 64)],
                KTp=self.dscr("KTp%d" % L, [g["FK"], NPS * TP], BF16),
                Vl=self.dscr("Vl%d" % L, [TS, g["WV"]], BF16),
                Va=[(r0, self.dscr("Va%d_%d" % (L, r0), [NR * min(TS, 262144 // g["WV"]), g["WV"]], BF16))
                    for r0 in range(0, TS, min(TS, 262144 // g["WV"]))],
                Vp=self.dscr("Vp%d" % L, [NPS * TP, g["WV"]], BF16),
                G=self.dscr("G%d" % L, [LT, DM], F32),
                O=self.dscr("O%d" % L, [LT, g["OW"]], F32),
            )
            xsrc = self.x_in if L == 0 else self.y
            idx = n_mla if kind == 0 else 0
            if kind == 0:
                n_mla += 1
            if kind == 2:
                self.diff_setup(L, D)
            self.phase_a(L, kind, idx, g, D, xsrc)
            if self.stop == (L, "A"):
                break
            if not self.overlap_gather:
                self.gather(L, D)
            if self.stop == (L, "G"):
                break
            self.attention(L, kind, g, D)
            if self.stop == (L, "AT"):
                break
            self.phase_c(L, kind, idx, g, D, xsrc)
        return nc

    def load_w_bf16(self, P, w_ap, rows, cols, name, eng_cast="pool"):
        nc = self.nc
        KC = rows // 128
        wsb = self.sb(name, [128, KC, cols], BF16)
        CH = 2048
        for c in range(KC):
            for n0 in range(0, cols, CH):
                n1 = min(cols, n0 + CH)
                st = self.stg[self.stg_i % 2]
                self.stg_i += 1
                P.t("sp", "dma_start", out=st.t[:, 0:n1 - n0], in_=w_ap[c * 128:(c + 1) * 128, n0:n1], W=[st.b])
                P.t(eng_cast, "tensor_copy", out=wsb[:, c, n0:n1], in_=st.t[:, 0:n1 - n0], R=[st.b], W=[self.wbuf])
        return wsb

    def bcast_row(self, P, dst_ap, row_ap, n, buf):
        P.t("sp", "dma_start", out=dst_ap, in_=row_ap.broadcast_to([128, n]), W=[buf])

    def rstd_ops(self, P, ss, ssb, rstd, rstdb, n, width=1):
        P.t("act", "activation", out=rstd, in_=ss, func=AF.Sqrt, scale=1.0 / n, bias=self.eps_sb[:, 0:1], R=[ssb], W=[rstdb])
        P.t("dve", "reciprocal", rstd, rstd, R=[rstdb], W=[rstdb])

    def phase_a(self, L, kind, idx, g, D, xsrc):
        nc, cfg, ps = self.nc, self.cfg, self.ps
        S = self.S
        P = Prog(nc, S)
        LT, TS, TP, NPS = cfg.LT, cfg.TS, cfg.TP, cfg.NPS
        NIN = g["NIN"]
        with contextlib.ExitStack() as self.es:
            self.stg = [TB(self, "stg%d_%d" % (L, i), [128, 2048], F32) for i in range(2)]
            self.stg_i = 0
            self.wbuf = Buf()
            pre = {0: "mla", 1: "gqa", 2: "dif"}[kind]
            w_in = self.load_w_bf16(P, self.w[pre + "_w_in"][idx], DM, NIN, "w_in%d" % L)
            if kind == 0:
                w_uq = self.load_w_bf16(P, self.w["mla_w_uq"][idx], 256, 1536, "w_uq%d" % L)
                w_ukv = self.load_w_bf16(P, self.w["mla_w_ukv"][idx], 128, 2048, "w_ukv%d" % L)
            cb = Buf()
            gpre = self.sb("gpre%d" % L, [128, DM], F32)
            self.bcast_row(P, gpre[:, :], self.w["norm_pre"][L:L + 1, :], DM, cb)
            if kind == 0:
                glat = self.sb("glat%d" % L, [128, 384], F32)
                self.bcast_row(P, glat[:, 0:256], self.w["mla_g_q"][idx:idx + 1, :], 256, cb)
                self.bcast_row(P, glat[:, 256:384], self.w["mla_g_kv"][idx:idx + 1, :], 128, cb)
            if kind == 1:
                gqk = self.sb("gqk%d" % L, [128, 2, 128], F32)
                self.bcast_row(P, gqk[:, 0, :], self.w["gqa_g_q"][idx:idx + 1, :], 128, cb)
                self.bcast_row(P, gqk[:, 1, :], self.w["gqa_g_k"][idx:idx + 1, :], 128, cb)
            xts = [TB(self, "xt%d_%d" % (L, i), [128, DM], F32) for i in range(2)]
            def two(name, shape, dt):
                return [TB(self, "%s%d_%d" % (name, L, i), shape, dt) for i in range(2)]
            junk2 = two("junk", [128, 1536], BF16)
            ss2 = two("ss", [128, 16], F32)
            rstd2 = two("rstd", [128, 16], F32)
            hbf2 = two("hbf", [128, DM], BF16)
            hT2 = two("hT", [128, 8, 128], BF16)
            gsb2 = two("gsb", [128, DM], F32)
            vbf2 = two("vbf", [128, 1024], BF16)
            qkbf2 = two("qkbf", [128, 2048], BF16)
            trs = [TB(self, "trs%d_%d" % (L, i), [128, 8, 128], BF16) for i in range(2)]
            cs2 = two("cs", [128, 128], F32)
            ntmp = {0: 2, 1: 4, 2: 0}[kind]
            tmp2 = [[TB(self, "tmpa%d_%d_%d" % (L, i, j), [128, 1280 if kind == 1 else 512], F32) for i in range(ntmp)] for j in range(2)]
            latn2 = two("latn", [128, 384], BF16)
            latT2 = two("latT", [128, 3, 128], BF16)
            krot2 = two("krot", [128, 128], BF16)
            bank = [Buf(excl=True) for _ in range(8)]
            tri = [0]

            def cp(eng, out, in_, R, W):
                if eng == "act":
                    P.t("act", "activation", out=out, in_=in_, func=AF.Copy, R=R, W=W)
                else:
                    P.t(eng, "tensor_copy", out=out, in_=in_, R=R, W=W)

            def transpose_store(src_ap, srcb, ncols, dst_ap, tok, eng_cp):
                nch = (ncols + 127) // 128
                c = 0
                while c < nch:
                    k = min(8, nch - c)
                    tr = trs[tri[0] % 2]
                    bk = 7 if kind == 0 else 6 + tri[0] % 2
                    tri[0] += 1
                    for j in range(k):
                        w = min(128, ncols - (c + j) * 128)
                        P.t("pe", "transpose", out=self.psb(bk)[0:w, j * 128:(j + 1) * 128],
                            in_=src_ap[:, (c + j) * 128:(c + j) * 128 + w], identity=self.ident[:, :],
                            R=[srcb], W=[bank[bk]], tok=(j == k - 1))
                    wl = min(128, ncols - (c + k - 1) * 128)
                    if wl == 128:
                        cp(eng_cp, tr.t[:, 0:k, :], self.psb(bk)[:, 0:k * 128].rearrange("p (j t) -> p j t", j=k),
                           [bank[bk]], [tr.b])
                        P.t("sp", "dma_start",
                            out=dst_ap[c * 128:(c + k) * 128, tok:tok + 128].rearrange("(j p) t -> p j t", p=128),
                            in_=tr.t[:, 0:k, :], R=[tr.b])
                    else:
                        assert k == 1
                        cp(eng_cp, tr.t[0:wl, 0, :], self.psb(bk)[0:wl, 0:128], [bank[bk]], [tr.b])
                        P.t("sp", "dma_start", out=dst_ap[c * 128:c * 128 + wl, tok:tok + 128], in_=tr.t[0:wl, 0, :], R=[tr.b])
                    c += k

            def proj(col0, ncols, bk0, lhs, lhsb, wsb, KC):
                nb = (ncols + 511) // 512
                for b in range(nb):
                    n0 = col0 + b * 512
                    nn = min(512, col0 + ncols - n0)
                    for c in range(KC):
                        P.t("pe", "matmul", ps[:, bk0 + b, 0:nn], lhs[:, c, :], wsb[:, c, n0:n0 + nn],
                            start=(c == 0), stop=(c == KC - 1), R=[lhsb, self.wbuf], W=[bank[bk0 + b]], tok=(c == KC - 1))

            NT = LT // 128
            for i in range(NT):
                tok = i * 128
                if self.overlap_gather and tok == TS:
                    for rs in S.ring["sp"]:
                        P.wait("pool", rs, rs.v)
                    self.gather_ops(P, D)
                is_s = tok < TS
                if is_s:
                    kt_dst, kt_tok = D["KTl"], tok
                    v_dst = D["Vl"][tok:tok + 128, :]
                else:
                    kt_dst, kt_tok = D["KTp"], tok - TS
                    v_dst = D["Vp"][tok - TS:tok - TS + 128, :]
                xt = xts[i % 2]
                k2 = i % 2
                junk, ss, rstd, hbf, hT, gsb, vbf, qkbf = junk2[k2], ss2[k2], rstd2[k2], hbf2[k2], hT2[k2], gsb2[k2], vbf2[k2], qkbf2[k2]
                cs, tmp, latn, latT, krot = cs2[k2], tmp2[k2], latn2[k2], latT2[k2], krot2[k2]
                P.t("sp", "dma_start", out=xt.t[:, :], in_=xsrc[tok:tok + 128, :], W=[xt.b])
                if kind == 0:
                    P.t("sp", "dma_start", out=cs.t[:, 0:32], in_=self.mla_cs[tok:tok + 128, :], W=[cs.b])
                elif kind == 1:
                    P.t("sp", "dma_start", out=cs.t[:, 0:128], in_=self.gqa_cs[tok:tok + 128, :], W=[cs.b])
                P.t("act", "activation", out=junk.t[:, 0:DM], in_=xt.t[:, :], func=AF.Square, accum_out=ss.t[:, 0:1],
                    R=[xt.b], W=[junk.b, ss.b])
                self.rstd_ops(P, ss.t[:, 0:1], ss.b, rstd.t[:, 0:1], rstd.b, DM)
                P.t("dve", "scalar_tensor_tensor", out=hbf.t[:, :], in0=xt.t[:, :], scalar=rstd.t[:, 0:1], in1=gpre[:, :],
                    op0=ALU.mult, op1=ALU.mult, R=[xt.b, rstd.b, cb], W=[hbf.b])
                for c in range(8):
                    P.t("pe", "transpose", out=self.psb(0)[:, c * 128:(c + 1) * 128], in_=hbf.t[:, c * 128:(c + 1) * 128],
                        identity=self.ident[:, :], R=[hbf.b], W=[bank[0]], tok=(c == 7))
                P.t("dve", "tensor_copy", out=hT.t[:, :, :], in_=self.psb(0)[:, :].rearrange("p (c t) -> p c t", c=8),
                    R=[bank[0]], W=[hT.b])
                if kind == 0:
                    proj(0, 416, 1, hT.t, hT.b, w_in, 8)
                    proj(416, 1024, 2, hT.t, hT.b, w_in, 8)
                    P.t("act", "activation", out=gsb.t[:, :], in_=ps[:, 2:4, :].rearrange("p b n -> p (b n)"), func=AF.Silu,
                        R=[bank[2], bank[3]], W=[gsb.b])
                    P.t("sp", "dma_start", out=D["G"][tok:tok + 128, :], in_=gsb.t[:, :], R=[gsb.b])
                    P.t("act", "activation", out=junk.t[:, 0:256], in_=ps[:, 1, 0:256], func=AF.Square, accum_out=ss.t[:, 1:2],
                        R=[bank[1]], W=[junk.b, ss.b])
                    P.t("act", "activation", out=junk.t[:, 0:128], in_=ps[:, 1, 256:384], func=AF.Square, accum_out=ss.t[:, 2:3],
                        R=[bank[1]], W=[junk.b, ss.b])
                    self.rstd_ops(P, ss.t[:, 1:2], ss.b, rstd.t[:, 1:2], rstd.b, 256)
                    self.rstd_ops(P, ss.t[:, 2:3], ss.b, rstd.t[:, 2:3], rstd.b, 128)
                    P.t("dve", "scalar_tensor_tensor", out=latn.t[:, 0:256], in0=ps[:, 1, 0:256], scalar=rstd.t[:, 1:2],
                        in1=glat[:, 0:256], op0=ALU.mult, op1=ALU.mult, R=[bank[1], rstd.b, cb], W=[latn.b])
                    P.t("dve", "scalar_tensor_tensor", out=latn.t[:, 256:384], in0=ps[:, 1, 256:384], scalar=rstd.t[:, 2:3],
                        in1=glat[:, 256:384], op0=ALU.mult, op1=ALU.mult, R=[bank[1], rstd.b, cb], W=[latn.b])
                    kr = ps[:, 1, 384:416]
                    C, Sn = cs.t[:, 0:16], cs.t[:, 16:32]
                    t0, t1 = tmp[0], tmp[1]
                    P.t("dve", "tensor_tensor", out=t0.t[:, 0:16], in0=kr[:, 0:16], in1=C, op=ALU.mult, R=[bank[1], cs.b], W=[t0.b])
                    P.t("dve", "tensor_tensor", out=t0.t[:, 16:32], in0=kr[:, 16:32], in1=Sn, op=ALU.mult, R=[bank[1], cs.b], W=[t0.b])
                    P.t("dve", "tensor_tensor", out=t1.t[:, 0:16], in0=kr[:, 16:32], in1=C, op=ALU.mult, R=[bank[1], cs.b], W=[t1.b])
                    P.t("dve", "tensor_tensor", out=t1.t[:, 16:32], in0=kr[:, 0:16], in1=Sn, op=ALU.mult, R=[bank[1], cs.b], W=[t1.b])
                    P.t("dve", "tensor_tensor", out=krot.t[:, 0:16], in0=t0.t[:, 0:16], in1=t0.t[:, 16:32], op=ALU.subtract,
                        R=[t0.b], W=[krot.b])
                    P.t("dve", "tensor_tensor", out=krot.t[:, 16:32], in0=t1.t[:, 0:16], in1=t1.t[:, 16:32], op=ALU.add,
                        R=[t1.b], W=[krot.b])
                    transpose_store(krot.t[:, 0:32], krot.b, 32, kt_dst[1024:1056, :], kt_tok, "act")
                    for c in range(3):
                        P.t("pe", "transpose", out=self.psb(0)[:, c * 128:(c + 1) * 128], in_=latn.t[:, c * 128:(c + 1) * 128],
                            identity=self.ident[:, :], R=[latn.b], W=[bank[0]], tok=(c == 2))
                    P.t("act", "activation", out=latT.t[:, :, :], in_=self.psb(0)[:, 0:384].rearrange("p (c t) -> p c t", c=3),
                        func=AF.Copy, R=[bank[0]], W=[latT.b])
                    proj(0, 1536, 4, latT.t[:, 0:2, :], latT.b, w_uq, 2)
                    qv = ps[:, 4:7, :].rearrange("p b n -> p (b n)").rearrange("p (h d) -> p h d", h=16)
                    qb3 = qkbf.t[:, 0:1536].rearrange("p (h d) -> p h d", h=16)
                    P.t("act", "activation", out=qkbf.t[:, 0:1536], in_=ps[:, 4:7, :].rearrange("p b n -> p (b n)"), func=AF.Copy,
                        R=[bank[4], bank[5], bank[6]], W=[qkbf.b])
                    Cb = cs.t[:, 0:16].unsqueeze(1).broadcast_to([128, 16, 16])
                    Sb = cs.t[:, 16:32].unsqueeze(1).broadcast_to([128, 16, 16])
                    A, B = qv[:, :, 64:80], qv[:, :, 80:96]
                    t0v = t0.t[:, 0:512].rearrange("p (h d) -> p h d", h=16)
                    t1v = t1.t[:, 0:512].rearrange("p (h d) -> p h d", h=16)
                    P.t("dve", "tensor_tensor", out=t0v[:, :, 0:16], in0=A, in1=Cb, op=ALU.mult, R=[bank[4], bank[5], bank[6], cs.b], W=[t0.b])
                    P.t("dve", "tensor_tensor", out=t0v[:, :, 16:32], in0=B, in1=Sb, op=ALU.mult, R=[bank[4], bank[5], bank[6], cs.b], W=[t0.b])
                    P.t("dve", "tensor_tensor", out=t1v[:, :, 0:16], in0=B, in1=Cb, op=ALU.mult, R=[bank[4], bank[5], bank[6], cs.b], W=[t1.b])
                    P.t("dve", "tensor_tensor", out=t1v[:, :, 16:32], in0=A, in1=Sb, op=ALU.mult, R=[bank[4], bank[5], bank[6], cs.b], W=[t1.b])
                    P.t("dve", "tensor_tensor", out=qb3[:, :, 64:80], in0=t0v[:, :, 0:16], in1=t0v[:, :, 16:32], op=ALU.subtract,
                        R=[t0.b], W=[qkbf.b])
                    P.t("dve", "tensor_tensor", out=qb3[:, :, 80:96], in0=t1v[:, :, 0:16], in1=t1v[:, :, 16:32], op=ALU.add,
                        R=[t1.b], W=[qkbf.b])
                    transpose_store(qkbf.t[:, 0:1536], qkbf.b, 1536, D["QT"], tok, "act")
                    proj(0, 2048, 0, latT.t[:, 2:3, :], latT.b, w_ukv, 1)
                    kvv = ps[:, 0:4, :].rearrange("p b n -> p (b n)").rearrange("p (h d) -> p h d", h=16)
                    P.t("act", "activation", out=vbf.t[:, :].rearrange("p (h d) -> p h d", h=16), in_=kvv[:, :, 64:128], func=AF.Copy,
                        R=[bank[0], bank[1], bank[2], bank[3]], W=[vbf.b])
                    P.t("sp", "dma_start", out=v_dst, in_=vbf.t[:, :], R=[vbf.b])
                    P.t("dve", "tensor_copy", out=qkbf.t[:, 0:1024].rearrange("p (h d) -> p h d", h=16), in_=kvv[:, :, 0:64],
                        R=[bank[0], bank[1], bank[2], bank[3]], W=[qkbf.b])
                    transpose_store(qkbf.t[:, 0:1024], qkbf.b, 1024, kt_dst, kt_tok, "act")
                elif kind == 1:
                    proj(0, 1024, 1, hT.t, hT.b, w_in, 8)
                    proj(1024, 512, 3, hT.t, hT.b, w_in, 8)
                    proj(1536, 1024, 4, hT.t, hT.b, w_in, 8)
                    P.t("act", "activation", out=gsb.t[:, :], in_=ps[:, 4:6, :].rearrange("p b n -> p (b n)"), func=AF.Silu,
                        R=[bank[4], bank[5]], W=[gsb.b])
                    P.t("sp", "dma_start", out=D["G"][tok:tok + 128, :], in_=gsb.t[:, :], R=[gsb.b])
                    P.t("act", "activation", out=vbf.t[:, 0:256], in_=ps[:, 3, 256:512], func=AF.Copy, R=[bank[3]], W=[vbf.b])
                    P.t("sp", "dma_start", out=v_dst, in_=vbf.t[:, 0:256], R=[vbf.b])
                    qk = ps[:, 1:4, :].rearrange("p b n -> p (b n)")[:, 0:1280]
                    qk3 = qk.rearrange("p (h d) -> p h d", h=10)
                    sq, nq_, t0, t1 = tmp[0], tmp[1], tmp[2], tmp[3]
                    P.t("act", "activation", out=sq.t[:, 0:1280], in_=qk, func=AF.Square, R=[bank[1], bank[2], bank[3]], W=[sq.b])
                    P.t("dve", "tensor_reduce", out=ss.t[:, 4:14], in_=sq.t[:, 0:1280].rearrange("p (h d) -> p h d", h=10),
                        axis=AX.X, op=ALU.add, R=[sq.b], W=[ss.b])
                    self.rstd_ops(P, ss.t[:, 4:14], ss.b, rstd.t[:, 4:14], rstd.b, 128)
                    n3 = nq_.t[:, 0:1280].rearrange("p (h d) -> p h d", h=10)
                    P.t("dve", "tensor_tensor", out=n3, in0=qk3, in1=rstd.t[:, 4:14].unsqueeze(2).broadcast_to([128, 10, 128]),
                        op=ALU.mult, R=[bank[1], bank[2], bank[3], rstd.b], W=[nq_.b])
                    P.t("dve", "tensor_tensor", out=n3[:, 0:8, :], in0=n3[:, 0:8, :],
                        in1=gqk[:, 0:1, :].broadcast_to([128, 8, 128]), op=ALU.mult, R=[nq_.b, cb], W=[nq_.b])
                    P.t("dve", "tensor_tensor", out=n3[:, 8:10, :], in0=n3[:, 8:10, :],
                        in1=gqk[:, 1:2, :].broadcast_to([128, 2, 128]), op=ALU.mult, R=[nq_.b, cb], W=[nq_.b])
                    Cb = cs.t[:, 0:64].unsqueeze(1).broadcast_to([128, 10, 64])
                    Sb = cs.t[:, 64:128].unsqueeze(1).broadcast_to([128, 10, 64])
                    A, B = n3[:, :, 0:64], n3[:, :, 64:128]
                    t03 = t0.t[:, 0:1280].rearrange("p (h d) -> p h d", h=10)
                    t13 = t1.t[:, 0:1280].rearrange("p (h d) -> p h d", h=10)
                    P.t("dve", "tensor_tensor", out=t03[:, :, 0:64], in0=A, in1=Cb, op=ALU.mult, R=[nq_.b, cs.b], W=[t0.b])
                    P.t("dve", "tensor_tensor", out=t03[:, :, 64:128], in0=B, in1=Sb, op=ALU.mult, R=[nq_.b, cs.b], W=[t0.b])
                    P.t("dve", "tensor_tensor", out=t13[:, :, 0:64], in0=B, in1=Cb, op=ALU.mult, R=[nq_.b, cs.b], W=[t1.b])
                    P.t("dve", "tensor_tensor", out=t13[:, :, 64:128], in0=A, in1=Sb, op=ALU.mult, R=[nq_.b, cs.b], W=[t1.b])
                    qb3 = qkbf.t[:, 0:1280].rearrange("p (h d) -> p h d", h=10)
                    P.t("dve", "tensor_tensor", out=qb3[:, :, 0:64], in0=t03[:, :, 0:64], in1=t03[:, :, 64:128], op=ALU.subtract,
                        R=[t0.b], W=[qkbf.b])
                    P.t("dve", "tensor_tensor", out=qb3[:, :, 64:128], in0=t13[:, :, 0:64], in1=t13[:, :, 64:128], op=ALU.add,
                        R=[t1.b], W=[qkbf.b])
                    transpose_store(qkbf.t[:, 0:1024], qkbf.b, 1024, D["QT"], tok, "act")
                    transpose_store(qkbf.t[:, 1024:1280], qkbf.b, 256, kt_dst, kt_tok, "act")
                else:
                    proj(0, 2048, 1, hT.t, hT.b, w_in, 8)
                    P.t("act", "activation", out=qkbf.t[:, 0:1024], in_=ps[:, 1:3, :].rearrange("p b n -> p (b n)"), func=AF.Copy,
                        R=[bank[1], bank[2]], W=[qkbf.b])
                    P.t("dve", "tensor_copy", out=qkbf.t[:, 1024:2048], in_=ps[:, 3:5, :].rearrange("p b n -> p (b n)"),
                        R=[bank[3], bank[4]], W=[qkbf.b])
                    transpose_store(qkbf.t[:, 0:1024], qkbf.b, 1024, D["QT"], tok, "act")
                    transpose_store(qkbf.t[:, 1024:2048], qkbf.b, 1024, kt_dst, kt_tok, "dve")
                    proj(2048, 2048, 1, hT.t, hT.b, w_in, 8)
                    P.t("dve", "tensor_copy", out=vbf.t[:, :], in_=ps[:, 1:3, :].rearrange("p b n -> p (b n)"),
                        R=[bank[1], bank[2]], W=[vbf.b])
                    P.t("sp", "dma_start", out=v_dst, in_=vbf.t[:, :], R=[vbf.b])
                    P.t("act", "activation", out=gsb.t[:, :], in_=ps[:, 3:5, :].rearrange("p b n -> p (b n)"), func=AF.Silu,
                        R=[bank[3], bank[4]], W=[gsb.b])
                    P.t("sp", "dma_start", out=D["G"][tok:tok + 128, :], in_=gsb.t[:, :], R=[gsb.b])
            if self.overlap_gather:
                P.wait("pool", S.get("cc"))
            P.finish()
            P.emit()

    def gather_ops(self, P, D):
        cc = self.S.get("cc")
        rg = [[0, 1, 2, 3], [4, 5, 6, 7]]
        for (r0, n, dst) in D["KTa"]:
            P.op("pool", "collective_compute", "AllGather", ALU.bypass, replica_groups=rg,
                 ins=[D["KTl"][r0:r0 + n, :]], outs=[dst], inc=cc, amt=1)
        RCV = min(self.cfg.TS, 262144 // D["Vl"].shape[1])
        for (r0, dst) in D["Va"]:
            P.op("pool", "collective_compute", "AllGather", ALU.bypass, replica_groups=rg,
                 ins=[D["Vl"][r0:r0 + RCV, :]], outs=[dst], inc=cc, amt=1)

    def gather(self, L, D):
        nc = self.nc
        P = Prog(nc, self.S)
        cc = self.S.get("cc")
        rg = [[0, 1, 2, 3], [4, 5, 6, 7]]
        ncc = 0
        for (r0, n, dst) in D["KTa"]:
            P.op("pool", "collective_compute", "AllGather", ALU.bypass, replica_groups=rg,
                 ins=[D["KTl"][r0:r0 + n, :]], outs=[dst], inc=cc, amt=1)
        RCV = min(self.cfg.TS, 262144 // D["Vl"].shape[1])
        for (r0, dst) in D["Va"]:
            P.op("pool", "collective_compute", "AllGather", ALU.bypass, replica_groups=rg,
                 ins=[D["Vl"][r0:r0 + RCV, :]], outs=[dst], inc=cc, amt=1)
        P.wait("pool", cc)
        P.emit()

    def diff_setup(self, L, D):
        nc, ps = self.nc, self.ps
        P = Prog(nc, self.S)
        self.TT = self.dscr("TT", [8, TREV_L], F32)
        self.STRIPS = self.dscr("STRIPS", [8, 128, STRIP_W], F32)
        with contextlib.ExitStack() as self.es:
            rb = TB(self, "ds_rb", [32, 8], F32)
            oh = TB(self, "ds_oh", [32, TREV_L], F32)
            sel = TB(self, "ds_sel", [32, 23], F32)
            rs = TB(self, "ds_rs", [32, 8, 23], F32)
            ones = TB(self, "ds_ones", [32, 128], F32)
            tt = TB(self, "ds_tt", [8, TREV_L], F32)
            bank = [Buf(excl=True) for _ in range(8)]
            mkb = Buf()
            P.t("sp", "dma_start", out=rb.t[:, :], in_=self.w["rel_bias"][:, :], W=[rb.b])
            P.t("sp", "dma_start", out=oh.t[:, :], in_=self.dif_oh[:, :], W=[oh.b])
            P.t("sp", "dma_start", out=sel.t[:, 0:10], in_=self.dif_sel[:, :], W=[sel.b])
            P.t("sp", "dma_start", out=sel.t[:, 10:23], in_=self.dif_sel2[:, :], W=[sel.b])
            P.t("sp", "dma_start", out=self.MK[:, :], in_=self.dif_m[:, :], W=[mkb])
            P.t("pool", "memset", ones.t[:, :], 1.0, W=[ones.b])
            for k, n0 in enumerate(range(0, TREV_L, 512)):
                P.t("pe", "matmul", ps[0:8, k, 0:512], rb.t[:, :], oh.t[:, n0:n0 + 512], start=True, stop=True,
                    R=[rb.b, oh.b], W=[bank[k]])
                P.t("dve", "tensor_copy", out=tt.t[:, n0:n0 + 512], in_=ps[0:8, k, 0:512], R=[bank[k]], W=[tt.b])
            ttd = Buf()
            P.t("sp", "dma_start", out=self.TT[:, :], in_=tt.t[:, :], R=[tt.b], W=[ttd])
            P.t("dve", "tensor_tensor", out=rs.t[:, :, :], in0=rb.t[:, :].unsqueeze(2).broadcast_to([32, 8, 23]),
                in1=sel.t[:, :].unsqueeze(1).broadcast_to([32, 8, 23]), op=ALU.mult, R=[rb.b, sel.b], W=[rs.b])
            P.t("pe", "matmul", ps[:, 4, 0:184], ones.t[:, :], rs.t[:, :, :].rearrange("b h c -> b (h c)"), start=True, stop=True,
                R=[ones.b, rs.b], W=[bank[4]])
            cbb = Buf()
            P.t("dve", "tensor_copy", out=self.CB[:, :, :].rearrange("p h c -> p (h c)"), in_=ps[:, 4, 0:184], R=[bank[4]], W=[cbb])
            J = TB(self, "ds_J", [128, 128], F32)
            P.t("sp", "dma_start", out=J.t[:, :], in_=self.antiid[:, :], W=[J.b])
            sR = [TB(self, "ds_sR%d" % i, [128, STRIP_W], F32) for i in range(2)]
            sO = [TB(self, "ds_sO%d" % i, [128, STRIP_W], F32) for i in range(2)]
            kk = 0
            for h in range(8):
                a, o = sR[h % 2], sO[h % 2]
                src = bass.AP(tensor=self.TT.tensor, offset=h * TREV_L, ap=[[1, 128], [1, STRIP_W]])
                P.t("sp", "dma_start", out=a.t[:, :], in_=src, R=[ttd], W=[a.b])
                for n0 in range(0, STRIP_W, 512):
                    nn = min(512, STRIP_W - n0)
                    bk = 5 + kk % 3
                    kk += 1
                    P.t("pe", "matmul", ps[:, bk, 0:nn], J.t[:, :], a.t[:, n0:n0 + nn], start=True, stop=True,
                        R=[J.b, a.b], W=[bank[bk]])
                    P.t("dve", "tensor_copy", out=o.t[:, n0:n0 + nn], in_=ps[:, bk, 0:nn], R=[bank[bk]], W=[o.b])
                P.t("act", "activation", out=o.t[:, :], in_=o.t[:, :], func=AF.Exp, R=[o.b], W=[o.b])
                P.t("dve", "tensor_scalar", out=o.t[:, :], in0=o.t[:, :], scalar1=-1.0, scalar2=None, op0=ALU.add, R=[o.b], W=[o.b])
                P.t("sp", "dma_start", out=self.STRIPS[h, :, :], in_=o.t[:, :], R=[o.b])
            P.finish()
            P.emit()

    def attention(self, L, kind, g, D):
        nc, cfg, ps, S = self.nc, self.cfg, self.ps, self.S
        P = Prog(nc, S)
        LT, TS, TP, NPS, SK = cfg.LT, cfg.TS, cfg.TP, cfg.NPS, cfg.SK
        dk, dv, scale = g["dk"], g["dv"], float(g["scale"])
        FK = g["FK"]
        heads = kind_heads(kind)
        TSK = SK // 128
        s_kv = [S.get("kv0"), S.get("kv1")]
        s_S, s_P, s_PV, s_Oe = S.get("aS"), S.get("aP"), S.get("aPV"), S.get("aOe")
        s_st = [S.get("ast0"), S.get("ast1")]
        s_F, s_str, s_ms = S.get("aF"), S.get("astr"), S.get("ams")
        fix_need = {}
        nO = 1 if 4 * (dv + 1) <= 512 else 2
        with contextlib.ExitStack() as self.es:
            kt_sb = [self.sb("kt%d_%d" % (L, i), [128, SK], BF16) for i in range(2)]
            v_sb = [self.sb("v%d_%d" % (L, i), [128, TSK, dv + 1], BF16) for i in range(2)]
            q_sb = [self.sb("q%d_%d" % (L, i), [128, max(TS, TP)], BF16) for i in range(2)]
            NPB = 4
            p_sb = self.sb("p%d" % L, [128, NPB, 1024], BF16)
            on_sb = self.sb("on%d" % L, [128, 2, 4, dv], F32)
            rc_sb = self.sb("rc%d" % L, [128, 2, 4], F32)
            if kind == 2:
                strip = self.sb("strip%d" % L, [128, STRIP_W], F32)
                ftmp = self.sb("ftmp%d" % L, [128, 4, 1024], BF16)
            for i in range(2):
                P.op("pool", "memset", v_sb[i][:, :, dv:dv + 1], 1.0, inc=s_ms)
            P.wait("pe", s_ms)
            jobs = []
            for hd in heads:
                jobs.append((hd, "S", 0))
                for s in range(NPS):
                    jobs.append((hd, "P", s))

            def issue_load(ji):
                hd, typ, s = jobs[ji]
                slot = ji % 2
                sem = s_kv[slot]
                if typ == "S":
                    nk, nq, tok0 = SK, TS, 0
                else:
                    nk, nq, tok0 = TP, TP, TS + s * TP
                if ji >= 2:
                    P.wait("sp", s_PV, jobs_qt_end[ji - 2])
                P.op("sp", "dma_start", out=q_sb[slot][0:dk, 0:nq], in_=D["QT"][hd["qrow"]:hd["qrow"] + dk, tok0:tok0 + nq], inc=sem)
                for (r0, n, p0) in hd["kparts"]:
                    if typ == "S":
                        for (c0, cn, ct) in D["KTa"]:
                            lo, hi = max(r0, c0), min(r0 + n, c0 + cn)
                            if lo >= hi:
                                continue
                            src = ct.rearrange("(r d) t -> d r t", r=NR)[lo - c0:hi - c0, :, :]
                            dst = kt_sb[slot][p0 + lo - r0:p0 + hi - r0, 0:SK].rearrange("d (r t) -> d r t", r=NR)
                            P.op("sp", "dma_start", out=dst, in_=src, inc=sem)
                    else:
                        P.op("sp", "dma_start", out=kt_sb[slot][p0:p0 + n, 0:TP], in_=D["KTp"][r0:r0 + n, s * TP:(s + 1) * TP], inc=sem)
                if typ == "S":
                    RCV = min(TS, 262144 // g["WV"])
                    ii = RCV // 128
                    njj = TS // RCV
                    for jj, (c0, vt) in enumerate(D["Va"]):
                        for r in range(NR):
                            src = vt[r * RCV:(r + 1) * RCV, hd["vcol"]:hd["vcol"] + dv].rearrange("(ii p) c -> p ii c", p=128)
                            t0 = r * (TS // 128) + jj * ii
                            P.op("sp", "dma_start", out=v_sb[slot][:, t0:t0 + ii, 0:dv], in_=src, inc=sem)
                else:
                    vsrc = D["Vp"][s * TP:(s + 1) * TP, :]
                    CHT = 16
                    for t0 in range(0, nk // 128, CHT):
                        tn = min(CHT, nk // 128 - t0)
                        P.op("sp", "dma_start", out=v_sb[slot][:, t0:t0 + tn, 0:dv],
                             in_=vsrc[t0 * 128:(t0 + tn) * 128, hd["vcol"]:hd["vcol"] + dv].rearrange("(t p) c -> p t c", p=128), inc=sem)
                return sem.v

            jobs_qt_end = []
            acc = self.qt_count
            for (hd, typ, s) in jobs:
                acc += (TS if typ == "S" else TP) // 512
                jobs_qt_end.append(acc)
            m = self.qt_count
            n = self.grp_count
            kv_ready = {}
            kv_ready[0] = issue_load(0)
            cur_strip = None
            for ji, (hd, typ, s) in enumerate(jobs):
                slot = ji % 2
                if ji + 1 < len(jobs):
                    kv_ready[ji + 1] = issue_load(ji + 1)
                if typ == "S":
                    nk, nq, tok0 = SK, TS, 0
                else:
                    nk, nq, tok0 = TP, TP, TS + s * TP
                T = nk // 128
                NG = T // 2
                P.wait("pe", s_kv[slot], kv_ready[ji])
                if kind == 2 and cur_strip != hd["bh"]:
                    P.wait("sp", s_F)
                    P.op("sp", "dma_start", out=strip[:, :], in_=self.STRIPS[hd["bh"], :, :], inc=s_str)
                    P.wait("dve", s_str)
                    cur_strip = hd["bh"]
                for qi in range(nq // 512):
                    ob = m % 2
                    obank = 4 + ob * nO

                    def oacc(j):
                        if nO == 1:
                            return ps[:, obank, j * (dv + 1):(j + 1) * (dv + 1)]
                        return ps[:, obank + j // 2, (j % 2) * (dv + 1):(j % 2 + 1) * (dv + 1)]

                    def qk(gi, n):
                        sb = (n % 2) * 2
                        for i in range(2):
                            t = gi * 2 + i
                            P.op("pe", "matmul", ps[:, sb + i, :], kt_sb[slot][0:dk, t * 128:(t + 1) * 128],
                                 q_sb[slot][0:dk, qi * 512:(qi + 1) * 512], start=True, stop=True,
                                 inc=(s_S if i == 1 else None))

                    def ex(gi, n):
                        sb = (n % 2) * 2
                        src = ps[:, sb:sb + 2, :].rearrange("p b n -> p (b n)")
                        dst = p_sb[:, n % NPB, :]
                        if kind != 2:
                            P.wait("act", s_S, n + 1)
                            P.op("act", "activation", out=dst, in_=src, func=AF.Exp, scale=scale, inc=s_P)
                            return
                        t = gi * 2
                        T32 = TS // 128
                        if typ == "S":
                            rho = t // T32
                            tl = t - rho * T32
                            dcs = (-1, 0, 1)
                        else:
                            rho, tl = 4, t
                            dcs = (0,)
                        delta = tl - qi * 4
                        near = [dc for dc in dcs if -256 <= dc * TS + delta * 128 <= 512]
                        assert len(near) <= 1
                        if not near:
                            side = 0 if delta < 0 else 1
                            P.wait("act", s_S, n + 1)
                            P.op("act", "activation", out=dst, in_=src, func=AF.Exp, scale=scale,
                                 bias=self.CB[:, hd["bh"], rho * 2 + side:rho * 2 + side + 1], inc=s_P)
                        else:
                            dc = near[0]
                            w = 4 if typ != "S" else {0: rho, -1: 5 + rho, 1: 9 + rho}[dc]
                            P.wait("act", s_S, n + 1)
                            tk = P.op("act", "activation", out=dst, in_=src, func=AF.Exp, scale=scale,
                                      bias=self.CB[:, hd["bh"], 10 + w:11 + w], inc=s_P)
                            fslot = self.fcount % 4
                            self.fcount += 1
                            P.wait("dve", tk)
                            for i in range(2):
                                off = STRIP_C - (dc * TS + (delta + i) * 128)
                                P.op("dve", "scalar_tensor_tensor", out=ftmp[:, fslot, i * 512:(i + 1) * 512],
                                     in0=strip[:, off:off + 512], scalar=self.MK[:, w:w + 1], in1=dst[:, i * 512:(i + 1) * 512],
                                     op0=ALU.mult, op1=ALU.mult, inc=(s_F if i == 1 else None))
                            return (fslot, s_F.v, gi)
                        return None

                    def pvd(ent, final):
                        fslot, fval, g0 = ent
                        P.wait("pe", s_F, fval)
                        for i in range(2):
                            t = g0 * 2 + i
                            for j in range(4):
                                fin = final and i == 1 and j == 3
                                P.op("pe", "matmul", oacc(j), ftmp[:, fslot, i * 512 + j * 128:i * 512 + (j + 1) * 128],
                                     v_sb[slot][:, t, :], start=False, stop=fin, skip_group_check=True,
                                     inc=(s_PV if fin else None))

                    def pv(gi, n, first, last):
                        P.wait("pe", s_P, n + 1)
                        if first:
                            P.wait("pe", s_Oe, m - 1)
                        for i in range(2):
                            t = gi * 2 + i
                            for j in range(4):
                                st = first and i == 0 and (j == 0 or (nO == 2 and j == 2))
                                lastmm = last and i == 1 and j == 3
                                P.op("pe", "matmul", oacc(j), p_sb[:, n % NPB, i * 512 + j * 128:i * 512 + (j + 1) * 128],
                                     v_sb[slot][:, t, :], start=st, stop=(last and i == 1), skip_group_check=True,
                                     inc=(s_PV if lastmm else None))

                    qk(0, n)
                    pend = []
                    for gi in range(NG):
                        ent = ex(gi, n + gi)
                        if gi + 1 < NG:
                            qk(gi + 1, n + gi + 1)
                        lastg = gi == NG - 1
                        if ent is not None:
                            pend.append(ent)
                        flush = [e for e in pend if lastg or e[2] + 2 <= gi]
                        pend = [e for e in pend if e not in flush]
                        pv(gi, n + gi, gi == 0, lastg and not flush)
                        for k, e in enumerate(flush):
                            pvd(e, lastg and k == len(flush) - 1)
                    n += NG
                    P.wait("dve", s_PV, m + 1)
                    P.wait("dve", s_st[ob])
                    tk = None
                    for j in range(4):
                        tk = P.op("dve", "reciprocal", rc_sb[:, ob, j:j + 1], oacc(j)[:, dv:dv + 1], inc=S.selfsem["dve"])
                    P.wait("dve", tk)
                    for j in range(4):
                        P.op("dve", "tensor_scalar", out=on_sb[:, ob, j, :], in0=oacc(j)[:, 0:dv], scalar1=rc_sb[:, ob, j:j + 1],
                             scalar2=None, op0=ALU.mult, inc=(s_Oe if j == 3 else None))
                    P.wait("pool", s_Oe, m + 1)
                    r0 = tok0 + qi * 512
                    P.op("pool", "dma_start", out=D["O"][r0:r0 + 512, hd["ocol"]:hd["ocol"] + dv].rearrange("(j p) d -> p j d", p=128),
                         in_=on_sb[:, ob, :, :], inc=s_st[ob])
                    m += 1
            self.qt_count = m
            self.grp_count = n
            for sem in s_st + s_kv + [s_str]:
                P.wait("pool", sem)
            P.wait("sp", s_kv[0])
            P.wait("sp", s_kv[1])
            P.emit()

    def phase_c(self, L, kind, idx, g, D, xsrc):
        nc, cfg, ps = self.nc, self.cfg, self.ps
        P = Prog(nc, self.S)
        LT = cfg.LT
        OW = g["OW"]
        with contextlib.ExitStack() as self.es:
            self.stg = [TB(self, "cstg%d_%d" % (L, i), [128, 2048], F32) for i in range(2)]
            self.stg_i = 0
            self.wbuf = Buf()
            pre = {0: "mla", 1: "gqa", 2: "dif"}[kind]
            w_o = self.load_w_bf16(P, self.w[pre + "_w_o"][idx], 1024, DM, "w_o%d" % L)
            cb = Buf()
            gpost = self.sb("gpost%d" % L, [128, DM], F32)
            self.bcast_row(P, gpost[:, :], self.w["norm_post"][L:L + 1, :], DM, cb)
            bank = [Buf(excl=True) for _ in range(8)]
            if kind == 2:
                lam_init = 0.8 - 0.6 * math.exp(-0.3 * L)
                lv = TB(self, "lv%d" % L, [128, 4, 64], F32)
                for k, nm in enumerate(("dif_lam_q1", "dif_lam_k1", "dif_lam_q2", "dif_lam_k2")):
                    self.bcast_row(P, lv.t[:, k, :], self.w[nm][idx:idx + 1, :], 64, lv.b)
                lp = TB(self, "lp%d" % L, [128, 2, 64], F32)
                lam = TB(self, "lam%d" % L, [128, 4], F32)
                P.t("dve", "tensor_tensor", out=lp.t[:, 0, :], in0=lv.t[:, 0, :], in1=lv.t[:, 1, :], op=ALU.mult, R=[lv.b], W=[lp.b])
                P.t("dve", "tensor_tensor", out=lp.t[:, 1, :], in0=lv.t[:, 2, :], in1=lv.t[:, 3, :], op=ALU.mult, R=[lv.b], W=[lp.b])
                P.t("dve", "tensor_reduce", out=lam.t[:, 0:2], in_=lp.t[:, :, :], axis=AX.X, op=ALU.add, R=[lp.b], W=[lam.b])
                P.t("act", "activation", out=lam.t[:, 0:2], in_=lam.t[:, 0:2], func=AF.Exp, R=[lam.b], W=[lam.b])
                P.t("dve", "tensor_tensor", out=lam.t[:, 2:3], in0=lam.t[:, 1:2], in1=lam.t[:, 0:1], op=ALU.subtract, R=[lam.b], W=[lam.b])
                P.t("dve", "tensor_scalar", out=lam.t[:, 3:4], in0=lam.t[:, 2:3], scalar1=-lam_init, scalar2=None, op0=ALU.add,
                    R=[lam.b], W=[lam.b])
                gsub = self.sb("gsub%d" % L, [128, 128], F32)
                self.bcast_row(P, gsub[:, :], self.w["dif_g_sub"][idx:idx + 1, :], 128, cb)
                P.t("dve", "tensor_scalar", out=gsub[:, :], in0=gsub[:, :], scalar1=1.0 - lam_init, scalar2=None, op0=ALU.mult,
                    R=[cb], W=[cb])
            ots = [TB(self, "ot%d_%d" % (L, i), [128, OW], F32) for i in range(2)]
            gts = [TB(self, "gt%d_%d" % (L, i), [128, DM], F32) for i in range(2)]
            xts = [TB(self, "cx%d_%d" % (L, i), [128, DM], F32) for i in range(2)]
            def two(name, shape, dt):
                return [TB(self, "%s%d_%d" % (name, L, i), shape, dt) for i in range(2)]
            og2 = two("og", [128, DM], BF16)
            ogT2 = two("ogT", [128, 8, 128], BF16)
            junk2 = two("cjunk", [128, DM], BF16)
            ss2 = two("css", [128, 16], F32)
            rstd2 = two("crstd", [128, 16], F32)
            yt = [TB(self, "yt%d_%d" % (L, i), [128, DM], F32) for i in range(2)]
            od2 = two("od", [128, DM], F32) if kind == 2 else [None, None]
            sq2 = two("csq", [128, DM], F32) if kind == 2 else [None, None]
            for i in range(LT // 128):
                tok = i * 128
                ot, gt, xt, y_ = ots[i % 2], gts[i % 2], xts[i % 2], yt[i % 2]
                k2 = i % 2
                og, ogT, junk, ss, rstd, od, sq = og2[k2], ogT2[k2], junk2[k2], ss2[k2], rstd2[k2], od2[k2], sq2[k2]
                b0 = 3 * k2
                P.t("sp", "dma_start", out=ot.t[:, :], in_=D["O"][tok:tok + 128, :], W=[ot.b])
                P.t("sp", "dma_start", out=gt.t[:, :], in_=D["G"][tok:tok + 128, :], W=[gt.b])
                P.t("sp", "dma_start", out=xt.t[:, :], in_=xsrc[tok:tok + 128, :], W=[xt.b])
                if kind != 2:
                    P.t("dve", "tensor_tensor", out=og.t[:, :], in0=ot.t[:, :], in1=gt.t[:, :], op=ALU.mult, R=[ot.b, gt.b], W=[og.b])
                else:
                    o4 = ot.t[:, :].rearrange("p (h j d) -> p h j d", h=8, j=2)
                    od3 = od.t[:, :].rearrange("p (h d) -> p h d", h=8)
                    P.t("dve", "scalar_tensor_tensor", out=od3, in0=o4[:, :, 1, :], scalar=lam.t[:, 3:4], in1=o4[:, :, 0, :],
                        op0=ALU.mult, op1=ALU.add, R=[ot.b, lam.b], W=[od.b])
                    P.t("act", "activation", out=sq.t[:, :], in_=od.t[:, :], func=AF.Square, R=[od.b], W=[sq.b])
                    P.t("dve", "tensor_reduce", out=ss.t[:, 4:12], in_=sq.t[:, :].rearrange("p (h d) -> p h d", h=8), axis=AX.X,
                        op=ALU.add, R=[sq.b], W=[ss.b])
                    self.rstd_ops(P, ss.t[:, 4:12], ss.b, rstd.t[:, 4:12], rstd.b, 128)
                    P.t("dve", "tensor_tensor", out=od3, in0=od3, in1=rstd.t[:, 4:12].unsqueeze(2).broadcast_to([128, 8, 128]),
                        op=ALU.mult, R=[od.b, rstd.b], W=[od.b])
                    P.t("dve", "tensor_tensor", out=od3, in0=od3, in1=gsub[:, :].unsqueeze(1).broadcast_to([128, 8, 128]),
                        op=ALU.mult, R=[od.b, cb], W=[od.b])
                    P.t("dve", "tensor_tensor", out=og.t[:, :], in0=od.t[:, :], in1=gt.t[:, :], op=ALU.mult, R=[od.b, gt.b], W=[og.b])
                for c in range(8):
                    P.t("pe", "transpose", out=self.psb(b0)[:, c * 128:(c + 1) * 128], in_=og.t[:, c * 128:(c + 1) * 128],
                        identity=self.ident[:, :], R=[og.b], W=[bank[b0]], tok=(c == 7))
                P.t("act", "activation", out=ogT.t[:, :, :], in_=self.psb(b0)[:, :].rearrange("p (c t) -> p c t", c=8), func=AF.Copy,
                    R=[bank[b0]], W=[ogT.b])
                for b in range(2):
                    for c in range(8):
                        P.t("pe", "matmul", ps[:, b0 + 1 + b, :], ogT.t[:, c, :], w_o[:, c, b * 512:(b + 1) * 512], start=(c == 0), stop=(c == 7),
                            R=[ogT.b, self.wbuf], W=[bank[b0 + 1 + b]], tok=(c == 7))
                mv = ps[:, b0 + 1:b0 + 3, :].rearrange("p b n -> p (b n)")
                P.t("act", "activation", out=junk.t[:, :], in_=mv, func=AF.Square, accum_out=ss.t[:, 0:1],
                    R=[bank[b0 + 1], bank[b0 + 2]], W=[junk.b, ss.b])
                self.rstd_ops(P, ss.t[:, 0:1], ss.b, rstd.t[:, 0:1], rstd.b, DM)
                P.t("dve", "scalar_tensor_tensor", out=y_.t[:, :], in0=mv, scalar=rstd.t[:, 0:1], in1=gpost[:, :],
                    op0=ALU.mult, op1=ALU.mult, R=[bank[b0 + 1], bank[b0 + 2], rstd.b, cb], W=[y_.b])
                P.t("pool", "tensor_tensor", out=y_.t[:, :], in0=y_.t[:, :], in1=xt.t[:, :], op=ALU.add, R=[y_.b, xt.b], W=[y_.b])
                P.t("sp", "dma_start", out=self.y[tok:tok + 128, :], in_=y_.t[:, :], R=[y_.b])
            P.finish()
            P.emit()


def _rope_angles(pos, dim):
    inv = (np.float32(10000.0) ** (-(np.arange(0, dim, 2, dtype=np.float32) / np.float32(dim)))).astype(np.float32)
    return (pos.astype(np.float32)[:, None] * inv[None, :]).astype(np.float32)


def _t5_bucket(rel):
    half, max_exact = 16, 8
    base = (rel > 0).astype(np.int32) * half
    n = np.abs(rel)
    nf = np.maximum(n, 1).astype(np.float32)
    large = max_exact + ((np.log(nf / np.float32(max_exact)) / np.float32(math.log(128 / max_exact)))
                         * np.float32(half - max_exact)).astype(np.int32)
    large = np.minimum(large, half - 1)
    return base + np.where(n < max_exact, n, large)


def host_tables(cfg, core):
    TS, TP, NPS, LT = cfg.TS, cfg.TP, cfg.NPS, cfg.LT
    r = core % NR
    pos = np.concatenate([r * TS + np.arange(TS)] + [np.arange(TP)] * NPS).astype(np.int64)
    a = _rope_angles(pos, 32)
    mla_cs = np.concatenate([np.cos(a), np.sin(a)], axis=1).astype(np.float32)
    ang = np.concatenate([_rope_angles(pos // 64, 64), _rope_angles(pos % 64, 64)], axis=1)
    gqa_cs = np.concatenate([np.cos(ang), np.sin(ang)], axis=1).astype(np.float32)
    i = np.arange(TREV_L)
    bk = _t5_bucket(767 - i)
    real = np.zeros((32, TREV_L), np.float32)
    real[bk, i] = 1.0
    sel = np.zeros((32, 10), np.float32)
    for rho in range(5):
        if rho == 4 or rho == r:
            sel[15, rho * 2 + 0] = 1.0
            sel[31, rho * 2 + 1] = 1.0
        elif rho > r:
            sel[31, rho * 2:rho * 2 + 2] = 1.0
        else:
            sel[15, rho * 2:rho * 2 + 2] = 1.0
    sel2 = np.zeros((32, 13), np.float32)
    mk = np.zeros((128, 13), np.float32)
    mk[:, 4] = 1.0
    for rho in range(4):
        for dc, w in ((0, rho), (-1, 5 + rho), (1, 9 + rho)):
            if rho - r == dc:
                mk[:, w] = 1.0
            else:
                if rho != r:
                    pos = rho > r
                else:
                    pos = dc < 0
                sel2[31 if pos else 15, w] = 1.0
    return dict(mla_cs=mla_cs, gqa_cs=gqa_cs, dif_oh=real, dif_sel=sel, dif_sel2=sel2, dif_m=mk,
                ident=np.eye(128, dtype=np.float32).astype(NPBF),
                antiid=np.ascontiguousarray(np.eye(128, dtype=np.float32)[::-1]))


_WNAMES = ["norm_pre", "norm_post", "rel_bias", "mla_w_in", "mla_g_q", "mla_w_uq", "mla_g_kv", "mla_w_ukv", "mla_w_o",
           "gqa_w_in", "gqa_g_q", "gqa_g_k", "gqa_w_o", "dif_w_in", "dif_lam_q1", "dif_lam_k1", "dif_lam_q2",
           "dif_lam_k2", "dif_g_sub", "dif_w_o"]


def run(cfg, x_prompt, x_sample, weights, debug=(), trace=False, stop=None):
    TS, TP, NPS = cfg.TS, cfg.TP, cfg.NPS
    b = Builder(cfg, debug=debug, stop=stop)
    nc = b.build()
    in_maps = []
    for c in range(NCORES):
        sb, r = c // NR, c % NR
        xs = [x_sample[sb, r * TS:(r + 1) * TS]] + [x_prompt[c * NPS + s] for s in range(NPS)]
        m = {"x_in": np.ascontiguousarray(np.concatenate(xs, axis=0), dtype=np.float32)}
        for n in _WNAMES:
            m[n] = np.ascontiguousarray(weights[n], dtype=np.float32)
        m.update(host_tables(cfg, c))
        in_maps.append(m)
    res = run_bass_kernel_spmd(nc, in_maps, core_ids=list(range(NCORES)), trace=trace)
    return res


def kernel(**inputs):
    cfg = Cfg()
    x_prompt = np.asarray(inputs["x_prompt"], dtype=np.float32)
    x_sample = np.asarray(inputs["x_sample"], dtype=np.float32)
    weights = {n: np.asarray(inputs[n]) for n in _WNAMES}
    res = run(cfg, x_prompt, x_sample, weights)
    TS, TP, NPS = cfg.TS, cfg.TP, cfg.NPS
    y_prompt = np.empty_like(x_prompt)
    y_sample = np.empty_like(x_sample)
    for c in range(NCORES):
        y = res.results[c]["y"]
        sb, r = c // NR, c % NR
        y_sample[sb, r * TS:(r + 1) * TS] = y[0:TS]
        for s in range(NPS):
            y_prompt[c * NPS + s] = y[TS + s * TP:TS + (s + 1) * TP]
    return (y_prompt, y_sample)
```

```python
import math
import contextlib
import numpy as np
import ml_dtypes
import concourse.bass as bass
import concourse.mybir as mybir
from concourse.bass_utils import run_bass_kernel_spmd

F32 = mybir.dt.float32
BF16 = mybir.dt.bfloat16
AF = mybir.ActivationFunctionType
ALU = mybir.AluOpType
AX = mybir.AxisListType
NPBF = ml_dtypes.bfloat16

DM = 1024
EPS = 1e-6
LAYER_KINDS = (0, 1, 2, 0)
NCORES = 8
STRICT = False
NR = 4
STRIP_C = 640
STRIP_W = 1408
TREV_L = 1536


class Cfg:
    def __init__(self, TS=4096, TP=2048, NPS=2):
        self.TS, self.TP, self.NPS = TS, TP, NPS
        self.LT = TS + NPS * TP
        self.SK = NR * TS


class Sem:
    def __init__(self, nc, name):
        self.h = nc.alloc_semaphore(name=name)
        self.v = 0
        self.name = name


class Buf:
    __slots__ = ("w", "r", "excl")

    def __init__(self, excl=False):
        self.w = None
        self.r = []
        self.excl = excl


class Sems:
    ENG = ("pe", "act", "dve", "pool", "sp")

    def __init__(self, nc):
        self.nc = nc
        self.selfsem = {k: Sem(nc, "self_" + k) for k in self.ENG}
        self.ring = {k: [Sem(nc, "ring_%s%d" % (k, i)) for i in range(8)] for k in ("sp", "pool", "act")}
        self.ridx = {k: 0 for k in self.ring}
        self.seen = {}
        self.named = {}

    def get(self, name):
        if name not in self.named:
            self.named[name] = Sem(self.nc, name)
        return self.named[name]


class Prog:
    ENG = Sems.ENG

    def __init__(self, nc, S):
        self.nc = nc
        self.S = S
        self.ops = {k: [] for k in self.ENG}
        self.pending = {k: [] for k in self.ENG}
        self.dma_sems = set()

    def op(self, eng, meth, *args, inc=None, amt=None, **kw):
        tok = None
        if inc is not None:
            if amt is None:
                amt = 16 if meth == "dma_start" else 1
            inc.v += amt
            tok = (inc, inc.v, eng, meth == "dma_start")
            if meth == "dma_start":
                self.dma_sems.add(inc)
        self.ops[eng].append((meth, args, kw, inc, amt))
        return tok

    def wait(self, eng, sem, val=None):
        if isinstance(sem, tuple):
            sem, val = sem[0], sem[1]
        if val is None:
            val = sem.v
        if val <= 0:
            return
        key = (eng, sem.name)
        if self.S.seen.get(key, 0) >= val:
            return
        self.S.seen[key] = val
        self.ops[eng].append(("wait_ge", (sem.h, val), {}, None, None))

    def _dep(self, eng, tok, raw):
        sem, val, src, is_dma = tok
        if src == eng and not is_dma and not raw and not STRICT:
            return
        if src == eng and eng == "pe":
            return
        self.wait(eng, sem, val)

    def t(self, eng, meth, *args, R=(), W=(), tok=True, **kw):
        xr = [b for b in R if b.excl]
        if xr:
            R = [b for b in R if not b.excl]
            W = list(W) + xr
        for b in R:
            if b.w is not None:
                self._dep(eng, b.w, True)
        for b in W:
            if b.w is not None:
                self._dep(eng, b.w, False)
            for tk in b.r:
                self._dep(eng, tk, False)
        token = None
        if meth == "dma_start":
            ring = self.S.ring[eng]
            s = ring[self.S.ridx[eng] % len(ring)]
            self.S.ridx[eng] += 1
            self.wait(eng, s, s.v)
            token = self.op(eng, meth, *args, inc=s, amt=16, **kw)
        elif tok:
            token = self.op(eng, meth, *args, inc=self.S.selfsem[eng], amt=1, **kw)
        else:
            self.op(eng, meth, *args, **kw)
        if token is None:
            self.pending[eng].append((tuple(R), tuple(W)))
            return None
        groups = self.pending[eng] + [(tuple(R), tuple(W))]
        self.pending[eng] = []
        for (rr, ww) in groups:
            for b in rr:
                b.r.append(token)
            for b in ww:
                b.w = token
                b.r = []
        return token

    def finish(self):
        for eng in ("sp", "pool", "act"):
            for s in self.S.ring[eng]:
                if s.v > 0:
                    self.wait(eng, s, s.v)
        for s in self.dma_sems:
            self.wait("sp", s, s.v)

    def emit(self):
        nc = self.nc

        def run(e, lst):
            for meth, args, kw, inc, amt in lst:
                ins = getattr(e, meth)(*args, **kw)
                if inc is not None:
                    ins.then_inc(inc.h, amt)
        with nc.Block() as block:
            @block.tensor
            def _(e):
                run(e, self.ops["pe"])

            @block.scalar
            def _(e):
                run(e, self.ops["act"])

            @block.vector
            def _(e):
                run(e, self.ops["dve"])

            @block.gpsimd
            def _(e):
                run(e, self.ops["pool"])

            @block.sync
            def _(e):
                run(e, self.ops["sp"])


class TB:
    def __init__(self, bld, name, shape, dt):
        self.t = bld.sb(name, shape, dt)
        self.b = Buf()


def kind_geom(kind):
    if kind == 0:
        return dict(NIN=1440, FQ=1536, FK=1056, WV=1024, OW=1024, dk=96, dv=64, scale=1.0 / math.sqrt(96.0))
    if kind == 1:
        return dict(NIN=2560, FQ=1024, FK=256, WV=256, OW=1024, dk=128, dv=128, scale=1.0 / math.sqrt(128.0))
    return dict(NIN=4096, FQ=1024, FK=1024, WV=1024, OW=2048, dk=64, dv=128, scale=1.0 / math.sqrt(64.0))


def kind_heads(kind):
    hs = []
    if kind == 0:
        for h in range(16):
            hs.append(dict(qrow=h * 96, kparts=[(h * 64, 64, 0), (1024, 32, 64)], vcol=h * 64, ocol=h * 64, bh=None))
    elif kind == 1:
        for h in range(8):
            g = h // 4
            hs.append(dict(qrow=h * 128, kparts=[(g * 128, 128, 0)], vcol=g * 128, ocol=h * 128, bh=None))
    else:
        for h in range(8):
            for j in range(2):
                hs.append(dict(qrow=h * 128 + j * 64, kparts=[(h * 128 + j * 64, 64, 0)], vcol=h * 128,
                               ocol=(h * 2 + j) * 128, bh=h))
    return hs


class Builder:
    def __init__(self, cfg, debug=(), stop=None):
        self.cfg = cfg
        self.stop = stop
        self.debug = set(debug)
        nc = self.nc = bass.Bass("TRN2", target_bir_lowering=False)
        self.S = Sems(nc)
        LT, TS, SK = cfg.LT, cfg.TS, cfg.SK
        din = lambda n, s, d=F32: nc.dram_tensor(n, s, d, kind="ExternalInput").ap()
        self.x_in = din("x_in", [LT, DM])
        self.y = nc.dram_tensor("y", [LT, DM], F32, kind="ExternalOutput").ap()
        self.w = {}
        for n, s in [("norm_pre", [4, DM]), ("norm_post", [4, DM]), ("rel_bias", [32, 8]),
                     ("mla_w_in", [2, DM, 1440]), ("mla_g_q", [2, 256]), ("mla_w_uq", [2, 256, 1536]),
                     ("mla_g_kv", [2, 128]), ("mla_w_ukv", [2, 128, 2048]), ("mla_w_o", [2, 1024, DM]),
                     ("gqa_w_in", [1, DM, 2560]), ("gqa_g_q", [1, 128]), ("gqa_g_k", [1, 128]),
                     ("gqa_w_o", [1, 1024, DM]),
                     ("dif_w_in", [1, DM, 4096]), ("dif_lam_q1", [1, 64]), ("dif_lam_k1", [1, 64]),
                     ("dif_lam_q2", [1, 64]), ("dif_lam_k2", [1, 64]), ("dif_g_sub", [1, 128]),
                     ("dif_w_o", [1, 1024, DM])]:
            self.w[n] = din(n, s)
        self.mla_cs = din("mla_cs", [LT, 32])
        self.gqa_cs = din("gqa_cs", [LT, 128])
        self.ident_d = din("ident", [128, 128], BF16)
        self.dif_oh = din("dif_oh", [32, TREV_L])
        self.dif_sel = din("dif_sel", [32, 10])
        self.dif_sel2 = din("dif_sel2", [32, 13])
        self.dif_m = din("dif_m", [128, 13])
        self.antiid = din("antiid", [128, 128])
        self.scr = {}
        self.ps = nc.alloc_psum_tensor("ps", [128, 8, 512], F32)
        self.ident = nc.alloc_sbuf_tensor("ident_sb", [128, 128], BF16)
        self.CB = nc.alloc_sbuf_tensor("CB", [128, 8, 23], F32)
        self.MK = nc.alloc_sbuf_tensor("MK", [128, 13], F32)
        self.eps_sb = nc.alloc_sbuf_tensor("eps_sb", [128, 1], F32)
        self.es = None
        self.qt_count = 0
        self.grp_count = 0
        self.fcount = 0
        self.overlap_gather = True

    def sb(self, name, shape, dt):
        return self.es.enter_context(self.nc.sbuf_tensor(name, shape, dt))

    def dscr(self, name, shape, dt):
        kind = "ExternalOutput" if name in self.debug else "Internal"
        t = self.nc.dram_tensor(name, shape, dt, kind=kind)
        self.scr[name] = t
        return t.ap()

    def psb(self, b):
        return self.ps[:, b, :].bitcast(BF16)

    def build(self):
        nc, cfg = self.nc, self.cfg
        P = Prog(nc, self.S)
        P.t("sp", "dma_start", out=self.ident[:, :], in_=self.ident_d[:, :])
        P.t("pool", "memset", self.eps_sb[:, :], EPS)
        P.finish()
        P.emit()
        n_mla = 0
        for L, kind in enumerate(LAYER_KINDS):
            g = kind_geom(kind)
            LT, TS, SK, TP, NPS = cfg.LT, cfg.TS, cfg.SK, cfg.TP, cfg.NPS
            D = dict(
                QT=self.dscr("QT%d" % L, [g["FQ"], LT], BF16),
                KTl=self.dscr("KTl%d" % L, [g["FK"], TS], BF16),
                KTa=[(r0, min(64, g["FK"] - r0), self.dscr("KTa%d_%d" % (L, r0), [NR * min(64, g["FK"] - r0), TS], BF16))
                     for r0 in range(0, g["FK"], 64)],
                KTp=self.dscr("KTp%d" % L, [g["FK"], NPS * TP], BF16),
                Vl=self.dscr("Vl%d" % L, [TS, g["WV"]], BF16),
                Va=[(r0, self.dscr("Va%d_%d" % (L, r0), [NR * min(TS, 262144 // g["WV"]), g["WV"]], BF16))
                    for r0 in range(0, TS, min(TS, 262144 // g["WV"]))],
                Vp=self.dscr("Vp%d" % L, [NPS * TP, g["WV"]], BF16),
                G=self.dscr("G%d" % L, [LT, DM], F32),
                O=self.dscr("O%d" % L, [LT, g["OW"]], F32),
            )
            xsrc = self.x_in if L == 0 else self.y
            idx = n_mla if kind == 0 else 0
            if kind == 0:
                n_mla += 1
            if kind == 2:
                self.diff_setup(L, D)
            self.phase_a(L, kind, idx, g, D, xsrc)
            if self.stop == (L, "A"):
                break
            if not self.overlap_gather:
                self.gather(L, D)
            if self.stop == (L, "G"):
                break
            self.attention(L, kind, g, D)
            if self.stop == (L, "AT"):
                break
            self.phase_c(L, kind, idx, g, D, xsrc)
        return nc

    def load_w_bf16(self, P, w_ap, rows, cols, name, eng_cast="pool"):
        nc = self.nc
        KC = rows // 128
        wsb = self.sb(name, [128, KC, cols], BF16)
        CH = 2048
        for c in range(KC):
            for n0 in range(0, cols, CH):
                n1 = min(cols, n0 + CH)
                st = self.stg[self.stg_i % 2]
                self.stg_i += 1
                P.t("sp", "dma_start", out=st.t[:, 0:n1 - n0], in_=w_ap[c * 128:(c + 1) * 128, n0:n1], W=[st.b])
                P.t(eng_cast, "tensor_copy", out=wsb[:, c, n0:n1], in_=st.t[:, 0:n1 - n0], R=[st.b], W=[self.wbuf])
        return wsb

    def bcast_row(self, P, dst_ap, row_ap, n, buf):
        P.t("sp", "dma_start", out=dst_ap, in_=row_ap.broadcast_to([128, n]), W=[buf])

    def rstd_ops(self, P, ss, ssb, rstd, rstdb, n, width=1):
        P.t("act", "activation", out=rstd, in_=ss, func=AF.Sqrt, scale=1.0 / n, bias=self.eps_sb[:, 0:1], R=[ssb], W=[rstdb])
        P.t("dve", "reciprocal", rstd, rstd, R=[rstdb], W=[rstdb])

    def phase_a(self, L, kind, idx, g, D, xsrc):
        nc, cfg, ps = self.nc, self.cfg, self.ps
        S = self.S
        P = Prog(nc, S)
        LT, TS, TP, NPS = cfg.LT, cfg.TS, cfg.TP, cfg.NPS
        NIN = g["NIN"]
        with contextlib.ExitStack() as self.es:
            self.stg = [TB(self, "stg%d_%d" % (L, i), [128, 2048], F32) for i in range(2)]
            self.stg_i = 0
            self.wbuf = Buf()
            pre = {0: "mla", 1: "gqa", 2: "dif"}[kind]
            w_in = self.load_w_bf16(P, self.w[pre + "_w_in"][idx], DM, NIN, "w_in%d" % L)
            if kind == 0:
                w_uq = self.load_w_bf16(P, self.w["mla_w_uq"][idx], 256, 1536, "w_uq%d" % L)
                w_ukv = self.load_w_bf16(P, self.w["mla_w_ukv"][idx], 128, 2048, "w_ukv%d" % L)
            cb = Buf()
            gpre = self.sb("gpre%d" % L, [128, DM], F32)
            self.bcast_row(P, gpre[:, :], self.w["norm_pre"][L:L + 1, :], DM, cb)
            if kind == 0:
                glat = self.sb("glat%d" % L, [128, 384], F32)
                self.bcast_row(P, glat[:, 0:256], self.w["mla_g_q"][idx:idx + 1, :], 256, cb)
                self.bcast_row(P, glat[:, 256:384], self.w["mla_g_kv"][idx:idx + 1, :], 128, cb)
            if kind == 1:
                gqk = self.sb("gqk%d" % L, [128, 2, 128], F32)
                self.bcast_row(P, gqk[:, 0, :], self.w["gqa_g_q"][idx:idx + 1, :], 128, cb)
                self.bcast_row(P, gqk[:, 1, :], self.w["gqa_g_k"][idx:idx + 1, :], 128, cb)
            xts = [TB(self, "xt%d_%d" % (L, i), [128, DM], F32) for i in range(2)]
            def two(name, shape, dt):
                return [TB(self, "%s%d_%d" % (name, L, i), shape, dt) for i in range(2)]
            junk2 = two("junk", [128, 1536], BF16)
            ss2 = two("ss", [128, 16], F32)
            rstd2 = two("rstd", [128, 16], F32)
            hbf2 = two("hbf", [128, DM], BF16)
            hT2 = two("hT", [128, 8, 128], BF16)
            gsb2 = two("gsb", [128, DM], F32)
            vbf2 = two("vbf", [128, 1024], BF16)
            qkbf2 = two("qkbf", [128, 2048], BF16)
            trs = [TB(self, "trs%d_%d" % (L, i), [128, 8, 128], BF16) for i in range(2)]
            cs2 = two("cs", [128, 128], F32)
            ntmp = {0: 2, 1: 4, 2: 0}[kind]
            tmp2 = [[TB(self, "tmpa%d_%d_%d" % (L, i, j), [128, 1280 if kind == 1 else 512], F32) for i in range(ntmp)] for j in range(2)]
            latn2 = two("latn", [128, 384], BF16)
            latT2 = two("latT", [128, 3, 128], BF16)
            krot2 = two("krot", [128, 128], BF16)
            bank = [Buf(excl=True) for _ in range(8)]
            tri = [0]

            def cp(eng, out, in_, R, W):
                if eng == "act":
                    P.t("act", "activation", out=out, in_=in_, func=AF.Copy, R=R, W=W)
                else:
                    P.t(eng, "tensor_copy", out=out, in_=in_, R=R, W=W)

            def transpose_store(src_ap, srcb, ncols, dst_ap, tok, eng_cp):
                nch = (ncols + 127) // 128
                c = 0
                while c < nch:
                    k = min(8, nch - c)
                    tr = trs[tri[0] % 2]
                    bk = 7 if kind == 0 else 6 + tri[0] % 2
                    tri[0] += 1
                    for j in range(k):
                        w = min(128, ncols - (c + j) * 128)
                        P.t("pe", "transpose", out=self.psb(bk)[0:w, j * 128:(j + 1) * 128],
                            in_=src_ap[:, (c + j) * 128:(c + j) * 128 + w], identity=self.ident[:, :],
                            R=[srcb], W=[bank[bk]], tok=(j == k - 1))
                    wl = min(128, ncols - (c + k - 1) * 128)
                    if wl == 128:
                        cp(eng_cp, tr.t[:, 0:k, :], self.psb(bk)[:, 0:k * 128].rearrange("p (j t) -> p j t", j=k),
                           [bank[bk]], [tr.b])
                        P.t("sp", "dma_start",
                            out=dst_ap[c * 128:(c + k) * 128, tok:tok + 128].rearrange("(j p) t -> p j t", p=128),
                            in_=tr.t[:, 0:k, :], R=[tr.b])
                    else:
                        assert k == 1
                        cp(eng_cp, tr.t[0:wl, 0, :], self.psb(bk)[0:wl, 0:128], [bank[bk]], [tr.b])
                        P.t("sp", "dma_start", out=dst_ap[c * 128:c * 128 + wl, tok:tok + 128], in_=tr.t[0:wl, 0, :], R=[tr.b])
                    c += k

            def proj(col0, ncols, bk0, lhs, lhsb, wsb, KC):
                nb = (ncols + 511) // 512
                for b in range(nb):
                    n0 = col0 + b * 512
                    nn = min(512, col0 + ncols - n0)
                    for c in range(KC):
                        P.t("pe", "matmul", ps[:, bk0 + b, 0:nn], lhs[:, c, :], wsb[:, c, n0:n0 + nn],
                            start=(c == 0), stop=(c == KC - 1), R=[lhsb, self.wbuf], W=[bank[bk0 + b]], tok=(c == KC - 1))

            NT = LT // 128

            def tile_ops(i, part):
                tok = i * 128
                is_s = tok < TS
                if is_s:
                    kt_dst, kt_tok = D["KTl"], tok
                    v_dst = D["Vl"][tok:tok + 128, :]
                else:
                    kt_dst, kt_tok = D["KTp"], tok - TS
                    v_dst = D["Vp"][tok - TS:tok - TS + 128, :]
                xt = xts[i % 2]
                k2 = i % 2
                junk, ss, rstd, hbf, hT, gsb, vbf, qkbf = junk2[k2], ss2[k2], rstd2[k2], hbf2[k2], hT2[k2], gsb2[k2], vbf2[k2], qkbf2[k2]
                cs, tmp, latn, latT, krot = cs2[k2], tmp2[k2], latn2[k2], latT2[k2], krot2[k2]
                if part == 1:
                    P.t("sp", "dma_start", out=xt.t[:, :], in_=xsrc[tok:tok + 128, :], W=[xt.b])
                    if kind == 0:
                        P.t("sp", "dma_start", out=cs.t[:, 0:32], in_=self.mla_cs[tok:tok + 128, :], W=[cs.b])
                    elif kind == 1:
                        P.t("sp", "dma_start", out=cs.t[:, 0:128], in_=self.gqa_cs[tok:tok + 128, :], W=[cs.b])
                    P.t("act", "activation", out=junk.t[:, 0:DM], in_=xt.t[:, :], func=AF.Square, accum_out=ss.t[:, 0:1],
                        R=[xt.b], W=[junk.b, ss.b])
                    self.rstd_ops(P, ss.t[:, 0:1], ss.b, rstd.t[:, 0:1], rstd.b, DM)
                    P.t("dve", "scalar_tensor_tensor", out=hbf.t[:, :], in0=xt.t[:, :], scalar=rstd.t[:, 0:1], in1=gpre[:, :],
                        op0=ALU.mult, op1=ALU.mult, R=[xt.b, rstd.b, cb], W=[hbf.b])
                    for c in range(8):
                        P.t("pe", "transpose", out=self.psb(0)[:, c * 128:(c + 1) * 128], in_=hbf.t[:, c * 128:(c + 1) * 128],
                            identity=self.ident[:, :], R=[hbf.b], W=[bank[0]], tok=(c == 7))
                    P.t("dve", "tensor_copy", out=hT.t[:, :, :], in_=self.psb(0)[:, :].rearrange("p (c t) -> p c t", c=8),
                        R=[bank[0]], W=[hT.b])
                    return
                if kind == 0:
                    proj(0, 416, 1, hT.t, hT.b, w_in, 8)
                    proj(416, 1024, 2, hT.t, hT.b, w_in, 8)
                    P.t("act", "activation", out=gsb.t[:, :], in_=ps[:, 2:4, :].rearrange("p b n -> p (b n)"), func=AF.Silu,
                        R=[bank[2], bank[3]], W=[gsb.b])
                    P.t("sp", "dma_start", out=D["G"][tok:tok + 128, :], in_=gsb.t[:, :], R=[gsb.b])
                    P.t("act", "activation", out=junk.t[:, 0:256], in_=ps[:, 1, 0:256], func=AF.Square, accum_out=ss.t[:, 1:2],
                        R=[bank[1]], W=[junk.b, ss.b])
                    P.t("act", "activation", out=junk.t[:, 0:128], in_=ps[:, 1, 256:384], func=AF.Square, accum_out=ss.t[:, 2:3],
                        R=[bank[1]], W=[junk.b, ss.b])
                    self.rstd_ops(P, ss.t[:, 1:2], ss.b, rstd.t[:, 1:2], rstd.b, 256)
                    self.rstd_ops(P, ss.t[:, 2:3], ss.b, rstd.t[:, 2:3], rstd.b, 128)
                    P.t("dve", "scalar_tensor_tensor", out=latn.t[:, 0:256], in0=ps[:, 1, 0:256], scalar=rstd.t[:, 1:2],
                        in1=glat[:, 0:256], op0=ALU.mult, op1=ALU.mult, R=[bank[1], rstd.b, cb], W=[latn.b])
                    P.t("dve", "scalar_tensor_tensor", out=latn.t[:, 256:384], in0=ps[:, 1, 256:384], scalar=rstd.t[:, 2:3],
                        in1=glat[:, 256:384], op0=ALU.mult, op1=ALU.mult, R=[bank[1], rstd.b, cb], W=[latn.b])
                    kr = ps[:, 1, 384:416]
                    C, Sn = cs.t[:, 0:16], cs.t[:, 16:32]
                    t0, t1 = tmp[0], tmp[1]
                    P.t("dve", "tensor_tensor", out=t0.t[:, 0:16], in0=kr[:, 0:16], in1=C, op=ALU.mult, R=[bank[1], cs.b], W=[t0.b])
                    P.t("dve", "tensor_tensor", out=t0.t[:, 16:32], in0=kr[:, 16:32], in1=Sn, op=ALU.mult, R=[bank[1], cs.b], W=[t0.b])
                    P.t("dve", "tensor_tensor", out=t1.t[:, 0:16], in0=kr[:, 16:32], in1=C, op=ALU.mult, R=[bank[1], cs.b], W=[t1.b])
                    P.t("dve", "tensor_tensor", out=t1.t[:, 16:32], in0=kr[:, 0:16], in1=Sn, op=ALU.mult, R=[bank[1], cs.b], W=[t1.b])
                    P.t("dve", "tensor_tensor", out=krot.t[:, 0:16], in0=t0.t[:, 0:16], in1=t0.t[:, 16:32], op=ALU.subtract,
                        R=[t0.b], W=[krot.b])
                    P.t("dve", "tensor_tensor", out=krot.t[:, 16:32], in0=t1.t[:, 0:16], in1=t1.t[:, 16:32], op=ALU.add,
                        R=[t1.b], W=[krot.b])
                    transpose_store(krot.t[:, 0:32], krot.b, 32, kt_dst[1024:1056, :], kt_tok, "act")
                    for c in range(3):
                        P.t("pe", "transpose", out=self.psb(7)[:, c * 128:(c + 1) * 128], in_=latn.t[:, c * 128:(c + 1) * 128],
                            identity=self.ident[:, :], R=[latn.b], W=[bank[7]], tok=(c == 2))
                    P.t("act", "activation", out=latT.t[:, :, :], in_=self.psb(7)[:, 0:384].rearrange("p (c t) -> p c t", c=3),
                        func=AF.Copy, R=[bank[7]], W=[latT.b])
                    proj(0, 1536, 4, latT.t[:, 0:2, :], latT.b, w_uq, 2)
                    qv = ps[:, 4:7, :].rearrange("p b n -> p (b n)").rearrange("p (h d) -> p h d", h=16)
                    qb3 = qkbf.t[:, 0:1536].rearrange("p (h d) -> p h d", h=16)
                    P.t("act", "activation", out=qkbf.t[:, 0:1536], in_=ps[:, 4:7, :].rearrange("p b n -> p (b n)"), func=AF.Copy,
                        R=[bank[4], bank[5], bank[6]], W=[qkbf.b])
                    Cb = cs.t[:, 0:16].unsqueeze(1).broadcast_to([128, 16, 16])
                    Sb = cs.t[:, 16:32].unsqueeze(1).broadcast_to([128, 16, 16])
                    A, B = qv[:, :, 64:80], qv[:, :, 80:96]
                    t0v = t0.t[:, 0:512].rearrange("p (h d) -> p h d", h=16)
                    t1v = t1.t[:, 0:512].rearrange("p (h d) -> p h d", h=16)
                    P.t("dve", "tensor_tensor", out=t0v[:, :, 0:16], in0=A, in1=Cb, op=ALU.mult, R=[bank[4], bank[5], bank[6], cs.b], W=[t0.b])
                    P.t("dve", "tensor_tensor", out=t0v[:, :, 16:32], in0=B, in1=Sb, op=ALU.mult, R=[bank[4], bank[5], bank[6], cs.b], W=[t0.b])
                    P.t("dve", "tensor_tensor", out=t1v[:, :, 0:16], in0=B, in1=Cb, op=ALU.mult, R=[bank[4], bank[5], bank[6], cs.b], W=[t1.b])
                    P.t("dve", "tensor_tensor", out=t1v[:, :, 16:32], in0=A, in1=Sb, op=ALU.mult, R=[bank[4], bank[5], bank[6], cs.b], W=[t1.b])
                    P.t("dve", "tensor_tensor", out=qb3[:, :, 64:80], in0=t0v[:, :, 0:16], in1=t0v[:, :, 16:32], op=ALU.subtract,
                        R=[t0.b], W=[qkbf.b])
                    P.t("dve", "tensor_tensor", out=qb3[:, :, 80:96], in0=t1v[:, :, 0:16], in1=t1v[:, :, 16:32], op=ALU.add,
                        R=[t1.b], W=[qkbf.b])
                    transpose_store(qkbf.t[:, 0:1536], qkbf.b, 1536, D["QT"], tok, "act")
                    proj(0, 2048, 0, latT.t[:, 2:3, :], latT.b, w_ukv, 1)
                    kvv = ps[:, 0:4, :].rearrange("p b n -> p (b n)").rearrange("p (h d) -> p h d", h=16)
                    P.t("act", "activation", out=vbf.t[:, :].rearrange("p (h d) -> p h d", h=16), in_=kvv[:, :, 64:128], func=AF.Copy,
                        R=[bank[0], bank[1], bank[2], bank[3]], W=[vbf.b])
                    P.t("sp", "dma_start", out=v_dst, in_=vbf.t[:, :], R=[vbf.b])
                    P.t("dve", "tensor_copy", out=qkbf.t[:, 0:1024].rearrange("p (h d) -> p h d", h=16), in_=kvv[:, :, 0:64],
                        R=[bank[0], bank[1], bank[2], bank[3]], W=[qkbf.b])
                    transpose_store(qkbf.t[:, 0:1024], qkbf.b, 1024, kt_dst, kt_tok, "act")
                elif kind == 1:
                    proj(0, 1024, 1, hT.t, hT.b, w_in, 8)
                    proj(1024, 512, 3, hT.t, hT.b, w_in, 8)
                    proj(1536, 1024, 4, hT.t, hT.b, w_in, 8)
                    P.t("act", "activation", out=gsb.t[:, :], in_=ps[:, 4:6, :].rearrange("p b n -> p (b n)"), func=AF.Silu,
                        R=[bank[4], bank[5]], W=[gsb.b])
                    P.t("sp", "dma_start", out=D["G"][tok:tok + 128, :], in_=gsb.t[:, :], R=[gsb.b])
                    P.t("act", "activation", out=vbf.t[:, 0:256], in_=ps[:, 3, 256:512], func=AF.Copy, R=[bank[3]], W=[vbf.b])
                    P.t("sp", "dma_start", out=v_dst, in_=vbf.t[:, 0:256], R=[vbf.b])
                    qk = ps[:, 1:4, :].rearrange("p b n -> p (b n)")[:, 0:1280]
                    qk3 = qk.rearrange("p (h d) -> p h d", h=10)
                    sq, nq_, t0, t1 = tmp[0], tmp[1], tmp[2], tmp[3]
                    P.t("act", "activation", out=sq.t[:, 0:1280], in_=qk, func=AF.Square, R=[bank[1], bank[2], bank[3]], W=[sq.b])
                    P.t("dve", "tensor_reduce", out=ss.t[:, 4:14], in_=sq.t[:, 0:1280].rearrange("p (h d) -> p h d", h=10),
                        axis=AX.X, op=ALU.add, R=[sq.b], W=[ss.b])
                    self.rstd_ops(P, ss.t[:, 4:14], ss.b, rstd.t[:, 4:14], rstd.b, 128)
                    n3 = nq_.t[:, 0:1280].rearrange("p (h d) -> p h d", h=10)
                    P.t("dve", "tensor_tensor", out=n3, in0=qk3, in1=rstd.t[:, 4:14].unsqueeze(2).broadcast_to([128, 10, 128]),
                        op=ALU.mult, R=[bank[1], bank[2], bank[3], rstd.b], W=[nq_.b])
                    P.t("dve", "tensor_tensor", out=n3[:, 0:8, :], in0=n3[:, 0:8, :],
                        in1=gqk[:, 0:1, :].broadcast_to([128, 8, 128]), op=ALU.mult, R=[nq_.b, cb], W=[nq_.b])
                    P.t("dve", "tensor_tensor", out=n3[:, 8:10, :], in0=n3[:, 8:10, :],
                        in1=gqk[:, 1:2, :].broadcast_to([128, 2, 128]), op=ALU.mult, R=[nq_.b, cb], W=[nq_.b])
                    Cb = cs.t[:, 0:64].unsqueeze(1).broadcast_to([128, 10, 64])
                    Sb = cs.t[:, 64:128].unsqueeze(1).broadcast_to([128, 10, 64])
                    A, B = n3[:, :, 0:64], n3[:, :, 64:128]
                    t03 = t0.t[:, 0:1280].rearrange("p (h d) -> p h d", h=10)
                    t13 = t1.t[:, 0:1280].rearrange("p (h d) -> p h d", h=10)
                    P.t("dve", "tensor_tensor", out=t03[:, :, 0:64], in0=A, in1=Cb, op=ALU.mult, R=[nq_.b, cs.b], W=[t0.b])
                    P.t("dve", "tensor_tensor", out=t03[:, :, 64:128], in0=B, in1=Sb, op=ALU.mult, R=[nq_.b, cs.b], W=[t0.b])
                    P.t("dve", "tensor_tensor", out=t13[:, :, 0:64], in0=B, in1=Cb, op=ALU.mult, R=[nq_.b, cs.b], W=[t1.b])
                    P.t("dve", "tensor_tensor", out=t13[:, :, 64:128], in0=A, in1=Sb, op=ALU.mult, R=[nq_.b, cs.b], W=[t1.b])
                    qb3 = qkbf.t[:, 0:1280].rearrange("p (h d) -> p h d", h=10)
                    P.t("dve", "tensor_tensor", out=qb3[:, :, 0:64], in0=t03[:, :, 0:64], in1=t03[:, :, 64:128], op=ALU.subtract,
                        R=[t0.b], W=[qkbf.b])
                    P.t("dve", "tensor_tensor", out=qb3[:, :, 64:128], in0=t13[:, :, 0:64], in1=t13[:, :, 64:128], op=ALU.add,
                        R=[t1.b], W=[qkbf.b])
                    transpose_store(qkbf.t[:, 0:1024], qkbf.b, 1024, D["QT"], tok, "act")
                    transpose_store(qkbf.t[:, 1024:1280], qkbf.b, 256, kt_dst, kt_tok, "act")
                else:
                    proj(0, 2048, 1, hT.t, hT.b, w_in, 8)
                    P.t("act", "activation", out=qkbf.t[:, 0:1024], in_=ps[:, 1:3, :].rearrange("p b n -> p (b n)"), func=AF.Copy,
                        R=[bank[1], bank[2]], W=[qkbf.b])
                    P.t("dve", "tensor_copy", out=qkbf.t[:, 1024:2048], in_=ps[:, 3:5, :].rearrange("p b n -> p (b n)"),
                        R=[bank[3], bank[4]], W=[qkbf.b])
                    transpose_store(qkbf.t[:, 0:1024], qkbf.b, 1024, D["QT"], tok, "act")
                    transpose_store(qkbf.t[:, 1024:2048], qkbf.b, 1024, kt_dst, kt_tok, "dve")
                    proj(2048, 2048, 1, hT.t, hT.b, w_in, 8)
                    P.t("dve", "tensor_copy", out=vbf.t[:, :], in_=ps[:, 1:3, :].rearrange("p b n -> p (b n)"),
                        R=[bank[1], bank[2]], W=[vbf.b])
                    P.t("sp", "dma_start", out=v_dst, in_=vbf.t[:, :], R=[vbf.b])
                    P.t("act", "activation", out=gsb.t[:, :], in_=ps[:, 3:5, :].rearrange("p b n -> p (b n)"), func=AF.Silu,
                        R=[bank[3], bank[4]], W=[gsb.b])
                    P.t("sp", "dma_start", out=D["G"][tok:tok + 128, :], in_=gsb.t[:, :], R=[gsb.b])

            for step in range(NT + 1):
                if step < NT:
                    tile_ops(step, 1)
                if step >= 1:
                    tile_ops(step - 1, 2)
                    if self.overlap_gather and step * 128 == TS:
                        for rs in S.ring["sp"]:
                            P.wait("pool", rs, rs.v)
                        self.gather_ops(P, D)
            if self.overlap_gather:
                P.wait("pool", S.get("cc"))
            P.finish()
            P.emit()

    def gather_ops(self, P, D):
        cc = self.S.get("cc")
        rg = [[0, 1, 2, 3], [4, 5, 6, 7]]
        for (r0, n, dst) in D["KTa"]:
            P.op("pool", "collective_compute", "AllGather", ALU.bypass, replica_groups=rg,
                 ins=[D["KTl"][r0:r0 + n, :]], outs=[dst], inc=cc, amt=1)
        RCV = min(self.cfg.TS, 262144 // D["Vl"].shape[1])
        for (r0, dst) in D["Va"]:
            P.op("pool", "collective_compute", "AllGather", ALU.bypass, replica_groups=rg,
                 ins=[D["Vl"][r0:r0 + RCV, :]], outs=[dst], inc=cc, amt=1)

    def gather(self, L, D):
        nc = self.nc
        P = Prog(nc, self.S)
        cc = self.S.get("cc")
        rg = [[0, 1, 2, 3], [4, 5, 6, 7]]
        ncc = 0
        for (r0, n, dst) in D["KTa"]:
            P.op("pool", "collective_compute", "AllGather", ALU.bypass, replica_groups=rg,
                 ins=[D["KTl"][r0:r0 + n, :]], outs=[dst], inc=cc, amt=1)
        RCV = min(self.cfg.TS, 262144 // D["Vl"].shape[1])
        for (r0, dst) in D["Va"]:
            P.op("pool", "collective_compute", "AllGather", ALU.bypass, replica_groups=rg,
                 ins=[D["Vl"][r0:r0 + RCV, :]], outs=[dst], inc=cc, amt=1)
        P.wait("pool", cc)
        P.emit()

    def diff_setup(self, L, D):
        nc, ps = self.nc, self.ps
        P = Prog(nc, self.S)
        self.TT = self.dscr("TT", [8, TREV_L], F32)
        self.STRIPS = self.dscr("STRIPS", [8, 128, STRIP_W], F32)
        with contextlib.ExitStack() as self.es:
            rb = TB(self, "ds_rb", [32, 8], F32)
            oh = TB(self, "ds_oh", [32, TREV_L], F32)
            sel = TB(self, "ds_sel", [32, 23], F32)
            rs = TB(self, "ds_rs", [32, 8, 23], F32)
            ones = TB(self, "ds_ones", [32, 128], F32)
            tt = TB(self, "ds_tt", [8, TREV_L], F32)
            bank = [Buf(excl=True) for _ in range(8)]
            mkb = Buf()
            P.t("sp", "dma_start", out=rb.t[:, :], in_=self.w["rel_bias"][:, :], W=[rb.b])
            P.t("sp", "dma_start", out=oh.t[:, :], in_=self.dif_oh[:, :], W=[oh.b])
            P.t("sp", "dma_start", out=sel.t[:, 0:10], in_=self.dif_sel[:, :], W=[sel.b])
            P.t("sp", "dma_start", out=sel.t[:, 10:23], in_=self.dif_sel2[:, :], W=[sel.b])
            P.t("sp", "dma_start", out=self.MK[:, :], in_=self.dif_m[:, :], W=[mkb])
            P.t("pool", "memset", ones.t[:, :], 1.0, W=[ones.b])
            for k, n0 in enumerate(range(0, TREV_L, 512)):
                P.t("pe", "matmul", ps[0:8, k, 0:512], rb.t[:, :], oh.t[:, n0:n0 + 512], start=True, stop=True,
                    R=[rb.b, oh.b], W=[bank[k]])
                P.t("dve", "tensor_copy", out=tt.t[:, n0:n0 + 512], in_=ps[0:8, k, 0:512], R=[bank[k]], W=[tt.b])
            ttd = Buf()
            P.t("sp", "dma_start", out=self.TT[:, :], in_=tt.t[:, :], R=[tt.b], W=[ttd])
            P.t("dve", "tensor_tensor", out=rs.t[:, :, :], in0=rb.t[:, :].unsqueeze(2).broadcast_to([32, 8, 23]),
                in1=sel.t[:, :].unsqueeze(1).broadcast_to([32, 8, 23]), op=ALU.mult, R=[rb.b, sel.b], W=[rs.b])
            P.t("pe", "matmul", ps[:, 4, 0:184], ones.t[:, :], rs.t[:, :, :].rearrange("b h c -> b (h c)"), start=True, stop=True,
                R=[ones.b, rs.b], W=[bank[4]])
            cbb = Buf()
            P.t("dve", "tensor_copy", out=self.CB[:, :, :].rearrange("p h c -> p (h c)"), in_=ps[:, 4, 0:184], R=[bank[4]], W=[cbb])
            J = TB(self, "ds_J", [128, 128], F32)
            P.t("sp", "dma_start", out=J.t[:, :], in_=self.antiid[:, :], W=[J.b])
            sR = [TB(self, "ds_sR%d" % i, [128, STRIP_W], F32) for i in range(2)]
            sO = [TB(self, "ds_sO%d" % i, [128, STRIP_W], F32) for i in range(2)]
            kk = 0
            for h in range(8):
                a, o = sR[h % 2], sO[h % 2]
                src = bass.AP(tensor=self.TT.tensor, offset=h * TREV_L, ap=[[1, 128], [1, STRIP_W]])
                P.t("sp", "dma_start", out=a.t[:, :], in_=src, R=[ttd], W=[a.b])
                for n0 in range(0, STRIP_W, 512):
                    nn = min(512, STRIP_W - n0)
                    bk = 5 + kk % 3
                    kk += 1
                    P.t("pe", "matmul", ps[:, bk, 0:nn], J.t[:, :], a.t[:, n0:n0 + nn], start=True, stop=True,
                        R=[J.b, a.b], W=[bank[bk]])
                    P.t("dve", "tensor_copy", out=o.t[:, n0:n0 + nn], in_=ps[:, bk, 0:nn], R=[bank[bk]], W=[o.b])
                P.t("act", "activation", out=o.t[:, :], in_=o.t[:, :], func=AF.Exp, R=[o.b], W=[o.b])
                P.t("dve", "tensor_scalar", out=o.t[:, :], in0=o.t[:, :], scalar1=-1.0, scalar2=None, op0=ALU.add, R=[o.b], W=[o.b])
                P.t("sp", "dma_start", out=self.STRIPS[h, :, :], in_=o.t[:, :], R=[o.b])
            P.finish()
            P.emit()

    def attention(self, L, kind, g, D):
        nc, cfg, ps, S = self.nc, self.cfg, self.ps, self.S
        P = Prog(nc, S)
        LT, TS, TP, NPS, SK = cfg.LT, cfg.TS, cfg.TP, cfg.NPS, cfg.SK
        dk, dv, scale = g["dk"], g["dv"], float(g["scale"])
        FK = g["FK"]
        heads = kind_heads(kind)
        TSK = SK // 128
        s_kv = [S.get("kv0"), S.get("kv1")]
        s_S, s_P, s_PV, s_Oe = S.get("aS"), S.get("aP"), S.get("aPV"), S.get("aOe")
        s_st = [S.get("ast0"), S.get("ast1")]
        s_F, s_str, s_ms = S.get("aF"), S.get("astr"), S.get("ams")
        fix_need = {}
        nO = 1 if 4 * (dv + 1) <= 512 else 2
        with contextlib.ExitStack() as self.es:
            kt_sb = [self.sb("kt%d_%d" % (L, i), [128, SK], BF16) for i in range(2)]
            v_sb = [self.sb("v%d_%d" % (L, i), [128, TSK, dv + 1], BF16) for i in range(2)]
            q_sb = [self.sb("q%d_%d" % (L, i), [128, max(TS, TP)], BF16) for i in range(2)]
            NPB = 4
            p_sb = self.sb("p%d" % L, [128, NPB, 1024], BF16)
            on_sb = self.sb("on%d" % L, [128, 2, 4, dv], F32)
            rc_sb = self.sb("rc%d" % L, [128, 2, 4], F32)
            if kind == 2:
                strip = self.sb("strip%d" % L, [128, STRIP_W], F32)
                ftmp = self.sb("ftmp%d" % L, [128, 4, 1024], BF16)
            for i in range(2):
                P.op("pool", "memset", v_sb[i][:, :, dv:dv + 1], 1.0, inc=s_ms)
            P.wait("pe", s_ms)
            jobs = []
            for hd in heads:
                jobs.append((hd, "S", 0))
                for s in range(NPS):
                    jobs.append((hd, "P", s))

            def issue_load(ji):
                hd, typ, s = jobs[ji]
                slot = ji % 2
                sem = s_kv[slot]
                if typ == "S":
                    nk, nq, tok0 = SK, TS, 0
                else:
                    nk, nq, tok0 = TP, TP, TS + s * TP
                if ji >= 2:
                    P.wait("sp", s_PV, jobs_qt_end[ji - 2])
                P.op("sp", "dma_start", out=q_sb[slot][0:dk, 0:nq], in_=D["QT"][hd["qrow"]:hd["qrow"] + dk, tok0:tok0 + nq], inc=sem)
                for (r0, n, p0) in hd["kparts"]:
                    if typ == "S":
                        for (c0, cn, ct) in D["KTa"]:
                            lo, hi = max(r0, c0), min(r0 + n, c0 + cn)
                            if lo >= hi:
                                continue
                            src = ct.rearrange("(r d) t -> d r t", r=NR)[lo - c0:hi - c0, :, :]
                            dst = kt_sb[slot][p0 + lo - r0:p0 + hi - r0, 0:SK].rearrange("d (r t) -> d r t", r=NR)
                            P.op("sp", "dma_start", out=dst, in_=src, inc=sem)
                    else:
                        P.op("sp", "dma_start", out=kt_sb[slot][p0:p0 + n, 0:TP], in_=D["KTp"][r0:r0 + n, s * TP:(s + 1) * TP], inc=sem)
                if typ == "S":
                    RCV = min(TS, 262144 // g["WV"])
                    ii = RCV // 128
                    njj = TS // RCV
                    for jj, (c0, vt) in enumerate(D["Va"]):
                        for r in range(NR):
                            src = vt[r * RCV:(r + 1) * RCV, hd["vcol"]:hd["vcol"] + dv].rearrange("(ii p) c -> p ii c", p=128)
                            t0 = r * (TS // 128) + jj * ii
                            P.op("sp", "dma_start", out=v_sb[slot][:, t0:t0 + ii, 0:dv], in_=src, inc=sem)
                else:
                    vsrc = D["Vp"][s * TP:(s + 1) * TP, :]
                    CHT = 16
                    for t0 in range(0, nk // 128, CHT):
                        tn = min(CHT, nk // 128 - t0)
                        P.op("sp", "dma_start", out=v_sb[slot][:, t0:t0 + tn, 0:dv],
                             in_=vsrc[t0 * 128:(t0 + tn) * 128, hd["vcol"]:hd["vcol"] + dv].rearrange("(t p) c -> p t c", p=128), inc=sem)
                return sem.v

            jobs_qt_end = []
            acc = self.qt_count
            for (hd, typ, s) in jobs:
                acc += (TS if typ == "S" else TP) // 512
                jobs_qt_end.append(acc)
            m = self.qt_count
            n = self.grp_count
            kv_ready = {}
            kv_ready[0] = issue_load(0)
            cur_strip = None
            for ji, (hd, typ, s) in enumerate(jobs):
                slot = ji % 2
                if ji + 1 < len(jobs):
                    kv_ready[ji + 1] = issue_load(ji + 1)
                if typ == "S":
                    nk, nq, tok0 = SK, TS, 0
                else:
                    nk, nq, tok0 = TP, TP, TS + s * TP
                T = nk // 128
                NG = T // 2
                P.wait("pe", s_kv[slot], kv_ready[ji])
                if kind == 2 and cur_strip != hd["bh"]:
                    P.wait("sp", s_F)
                    P.op("sp", "dma_start", out=strip[:, :], in_=self.STRIPS[hd["bh"], :, :], inc=s_str)
                    P.wait("dve", s_str)
                    cur_strip = hd["bh"]
                for qi in range(nq // 512):
                    ob = m % 2
                    obank = 4 + ob * nO

                    def oacc(j):
                        if nO == 1:
                            return ps[:, obank, j * (dv + 1):(j + 1) * (dv + 1)]
                        return ps[:, obank + j // 2, (j % 2) * (dv + 1):(j % 2 + 1) * (dv + 1)]

                    def qk(gi, n):
                        sb = (n % 2) * 2
                        for i in range(2):
                            t = gi * 2 + i
                            P.op("pe", "matmul", ps[:, sb + i, :], kt_sb[slot][0:dk, t * 128:(t + 1) * 128],
                                 q_sb[slot][0:dk, qi * 512:(qi + 1) * 512], start=True, stop=True,
                                 inc=(s_S if i == 1 else None))

                    def ex(gi, n):
                        sb = (n % 2) * 2
                        src = ps[:, sb:sb + 2, :].rearrange("p b n -> p (b n)")
                        dst = p_sb[:, n % NPB, :]
                        if kind != 2:
                            P.wait("act", s_S, n + 1)
                            P.op("act", "activation", out=dst, in_=src, func=AF.Exp, scale=scale, inc=s_P)
                            return
                        t = gi * 2
                        T32 = TS // 128
                        if typ == "S":
                            rho = t // T32
                            tl = t - rho * T32
                            dcs = (-1, 0, 1)
                        else:
                            rho, tl = 4, t
                            dcs = (0,)
                        delta = tl - qi * 4
                        near = [dc for dc in dcs if -256 <= dc * TS + delta * 128 <= 512]
                        assert len(near) <= 1
                        if not near:
                            side = 0 if delta < 0 else 1
                            P.wait("act", s_S, n + 1)
                            P.op("act", "activation", out=dst, in_=src, func=AF.Exp, scale=scale,
                                 bias=self.CB[:, hd["bh"], rho * 2 + side:rho * 2 + side + 1], inc=s_P)
                        else:
                            dc = near[0]
                            w = 4 if typ != "S" else {0: rho, -1: 5 + rho, 1: 9 + rho}[dc]
                            P.wait("act", s_S, n + 1)
                            tk = P.op("act", "activation", out=dst, in_=src, func=AF.Exp, scale=scale,
                                      bias=self.CB[:, hd["bh"], 10 + w:11 + w], inc=s_P)
                            fslot = self.fcount % 4
                            self.fcount += 1
                            P.wait("dve", tk)
                            for i in range(2):
                                off = STRIP_C - (dc * TS + (delta + i) * 128)
                                P.op("dve", "scalar_tensor_tensor", out=ftmp[:, fslot, i * 512:(i + 1) * 512],
                                     in0=strip[:, off:off + 512], scalar=self.MK[:, w:w + 1], in1=dst[:, i * 512:(i + 1) * 512],
                                     op0=ALU.mult, op1=ALU.mult, inc=(s_F if i == 1 else None))
                            return (fslot, s_F.v, gi)
                        return None

                    def pvd(ent, final):
                        fslot, fval, g0 = ent
                        P.wait("pe", s_F, fval)
                        for i in range(2):
                            t = g0 * 2 + i
                            for j in range(4):
                                fin = final and i == 1 and j == 3
                                P.op("pe", "matmul", oacc(j), ftmp[:, fslot, i * 512 + j * 128:i * 512 + (j + 1) * 128],
                                     v_sb[slot][:, t, :], start=False, stop=fin, skip_group_check=True,
                                     inc=(s_PV if fin else None))

                    def pv(gi, n, first, last):
                        P.wait("pe", s_P, n + 1)
                        if first:
                            P.wait("pe", s_Oe, m - 1)
                        for i in range(2):
                            t = gi * 2 + i
                            for j in range(4):
                                st = first and i == 0 and (j == 0 or (nO == 2 and j == 2))
                                lastmm = last and i == 1 and j == 3
                                P.op("pe", "matmul", oacc(j), p_sb[:, n % NPB, i * 512 + j * 128:i * 512 + (j + 1) * 128],
                                     v_sb[slot][:, t, :], start=st, stop=(last and i == 1), skip_group_check=True,
                                     inc=(s_PV if lastmm else None))

                    qk(0, n)
                    pend = []
                    for gi in range(NG):
                        ent = ex(gi, n + gi)
                        if gi + 1 < NG:
                            qk(gi + 1, n + gi + 1)
                        lastg = gi == NG - 1
                        if ent is not None:
                            pend.append(ent)
                        flush = [e for e in pend if lastg or e[2] + 2 <= gi]
                        pend = [e for e in pend if e not in flush]
                        pv(gi, n + gi, gi == 0, lastg and not flush)
                        for k, e in enumerate(flush):
                            pvd(e, lastg and k == len(flush) - 1)
                    n += NG
                    P.wait("dve", s_PV, m + 1)
                    P.wait("dve", s_st[ob])
                    tk = None
                    for j in range(4):
                        tk = P.op("dve", "reciprocal", rc_sb[:, ob, j:j + 1], oacc(j)[:, dv:dv + 1], inc=S.selfsem["dve"])
                    P.wait("dve", tk)
                    for j in range(4):
                        P.op("dve", "tensor_scalar", out=on_sb[:, ob, j, :], in0=oacc(j)[:, 0:dv], scalar1=rc_sb[:, ob, j:j + 1],
                             scalar2=None, op0=ALU.mult, inc=(s_Oe if j == 3 else None))
                    P.wait("pool", s_Oe, m + 1)
                    r0 = tok0 + qi * 512
                    P.op("pool", "dma_start", out=D["O"][r0:r0 + 512, hd["ocol"]:hd["ocol"] + dv].rearrange("(j p) d -> p j d", p=128),
                         in_=on_sb[:, ob, :, :], inc=s_st[ob])
                    m += 1
            self.qt_count = m
            self.grp_count = n
            for sem in s_st + s_kv + [s_str]:
                P.wait("pool", sem)
            P.wait("sp", s_kv[0])
            P.wait("sp", s_kv[1])
            P.emit()

    def phase_c(self, L, kind, idx, g, D, xsrc):
        nc, cfg, ps = self.nc, self.cfg, self.ps
        P = Prog(nc, self.S)
        LT = cfg.LT
        OW = g["OW"]
        with contextlib.ExitStack() as self.es:
            self.stg = [TB(self, "cstg%d_%d" % (L, i), [128, 2048], F32) for i in range(2)]
            self.stg_i = 0
            self.wbuf = Buf()
            pre = {0: "mla", 1: "gqa", 2: "dif"}[kind]
            w_o = self.load_w_bf16(P, self.w[pre + "_w_o"][idx], 1024, DM, "w_o%d" % L)
            cb = Buf()
            gpost = self.sb("gpost%d" % L, [128, DM], F32)
            self.bcast_row(P, gpost[:, :], self.w["norm_post"][L:L + 1, :], DM, cb)
            bank = [Buf(excl=True) for _ in range(8)]
            if kind == 2:
                lam_init = 0.8 - 0.6 * math.exp(-0.3 * L)
                lv = TB(self, "lv%d" % L, [128, 4, 64], F32)
                for k, nm in enumerate(("dif_lam_q1", "dif_lam_k1", "dif_lam_q2", "dif_lam_k2")):
                    self.bcast_row(P, lv.t[:, k, :], self.w[nm][idx:idx + 1, :], 64, lv.b)
                lp = TB(self, "lp%d" % L, [128, 2, 64], F32)
                lam = TB(self, "lam%d" % L, [128, 4], F32)
                P.t("dve", "tensor_tensor", out=lp.t[:, 0, :], in0=lv.t[:, 0, :], in1=lv.t[:, 1, :], op=ALU.mult, R=[lv.b], W=[lp.b])
                P.t("dve", "tensor_tensor", out=lp.t[:, 1, :], in0=lv.t[:, 2, :], in1=lv.t[:, 3, :], op=ALU.mult, R=[lv.b], W=[lp.b])
                P.t("dve", "tensor_reduce", out=lam.t[:, 0:2], in_=lp.t[:, :, :], axis=AX.X, op=ALU.add, R=[lp.b], W=[lam.b])
                P.t("act", "activation", out=lam.t[:, 0:2], in_=lam.t[:, 0:2], func=AF.Exp, R=[lam.b], W=[lam.b])
                P.t("dve", "tensor_tensor", out=lam.t[:, 2:3], in0=lam.t[:, 1:2], in1=lam.t[:, 0:1], op=ALU.subtract, R=[lam.b], W=[lam.b])
                P.t("dve", "tensor_scalar", out=lam.t[:, 3:4], in0=lam.t[:, 2:3], scalar1=-lam_init, scalar2=None, op0=ALU.add,
                    R=[lam.b], W=[lam.b])
                gsub = self.sb("gsub%d" % L, [128, 128], F32)
                self.bcast_row(P, gsub[:, :], self.w["dif_g_sub"][idx:idx + 1, :], 128, cb)
                P.t("dve", "tensor_scalar", out=gsub[:, :], in0=gsub[:, :], scalar1=1.0 - lam_init, scalar2=None, op0=ALU.mult,
                    R=[cb], W=[cb])
            ots = [TB(self, "ot%d_%d" % (L, i), [128, OW], F32) for i in range(2)]
            gts = [TB(self, "gt%d_%d" % (L, i), [128, DM], F32) for i in range(2)]
            xts = [TB(self, "cx%d_%d" % (L, i), [128, DM], F32) for i in range(2)]
            def two(name, shape, dt):
                return [TB(self, "%s%d_%d" % (name, L, i), shape, dt) for i in range(2)]
            og2 = two("og", [128, DM], BF16)
            ogT2 = two("ogT", [128, 8, 128], BF16)
            junk2 = two("cjunk", [128, DM], BF16)
            ss2 = two("css", [128, 16], F32)
            rstd2 = two("crstd", [128, 16], F32)
            yt = [TB(self, "yt%d_%d" % (L, i), [128, DM], F32) for i in range(2)]
            od2 = two("od", [128, DM], F32) if kind == 2 else [None, None]
            sq2 = two("csq", [128, DM], F32) if kind == 2 else [None, None]
            for i in range(LT // 128):
                tok = i * 128
                ot, gt, xt, y_ = ots[i % 2], gts[i % 2], xts[i % 2], yt[i % 2]
                k2 = i % 2
                og, ogT, junk, ss, rstd, od, sq = og2[k2], ogT2[k2], junk2[k2], ss2[k2], rstd2[k2], od2[k2], sq2[k2]
                b0 = 3 * k2
                P.t("sp", "dma_start", out=ot.t[:, :], in_=D["O"][tok:tok + 128, :], W=[ot.b])
                P.t("sp", "dma_start", out=gt.t[:, :], in_=D["G"][tok:tok + 128, :], W=[gt.b])
                P.t("sp", "dma_start", out=xt.t[:, :], in_=xsrc[tok:tok + 128, :], W=[xt.b])
                if kind != 2:
                    P.t("dve", "tensor_tensor", out=og.t[:, :], in0=ot.t[:, :], in1=gt.t[:, :], op=ALU.mult, R=[ot.b, gt.b], W=[og.b])
                else:
                    o4 = ot.t[:, :].rearrange("p (h j d) -> p h j d", h=8, j=2)
                    od3 = od.t[:, :].rearrange("p (h d) -> p h d", h=8)
                    P.t("dve", "scalar_tensor_tensor", out=od3, in0=o4[:, :, 1, :], scalar=lam.t[:, 3:4], in1=o4[:, :, 0, :],
                        op0=ALU.mult, op1=ALU.add, R=[ot.b, lam.b], W=[od.b])
                    P.t("act", "activation", out=sq.t[:, :], in_=od.t[:, :], func=AF.Square, R=[od.b], W=[sq.b])
                    P.t("dve", "tensor_reduce", out=ss.t[:, 4:12], in_=sq.t[:, :].rearrange("p (h d) -> p h d", h=8), axis=AX.X,
                        op=ALU.add, R=[sq.b], W=[ss.b])
                    self.rstd_ops(P, ss.t[:, 4:12], ss.b, rstd.t[:, 4:12], rstd.b, 128)
                    P.t("dve", "tensor_tensor", out=od3, in0=od3, in1=rstd.t[:, 4:12].unsqueeze(2).broadcast_to([128, 8, 128]),
                        op=ALU.mult, R=[od.b, rstd.b], W=[od.b])
                    P.t("dve", "tensor_tensor", out=od3, in0=od3, in1=gsub[:, :].unsqueeze(1).broadcast_to([128, 8, 128]),
                        op=ALU.mult, R=[od.b, cb], W=[od.b])
                    P.t("dve", "tensor_tensor", out=og.t[:, :], in0=od.t[:, :], in1=gt.t[:, :], op=ALU.mult, R=[od.b, gt.b], W=[og.b])
                for c in range(8):
                    P.t("pe", "transpose", out=self.psb(b0)[:, c * 128:(c + 1) * 128], in_=og.t[:, c * 128:(c + 1) * 128],
                        identity=self.ident[:, :], R=[og.b], W=[bank[b0]], tok=(c == 7))
                P.t("act", "activation", out=ogT.t[:, :, :], in_=self.psb(b0)[:, :].rearrange("p (c t) -> p c t", c=8), func=AF.Copy,
                    R=[bank[b0]], W=[ogT.b])
                for b in range(2):
                    for c in range(8):
                        P.t("pe", "matmul", ps[:, b0 + 1 + b, :], ogT.t[:, c, :], w_o[:, c, b * 512:(b + 1) * 512], start=(c == 0), stop=(c == 7),
                            R=[ogT.b, self.wbuf], W=[bank[b0 + 1 + b]], tok=(c == 7))
                mv = ps[:, b0 + 1:b0 + 3, :].rearrange("p b n -> p (b n)")
                P.t("act", "activation", out=junk.t[:, :], in_=mv, func=AF.Square, accum_out=ss.t[:, 0:1],
                    R=[bank[b0 + 1], bank[b0 + 2]], W=[junk.b, ss.b])
                self.rstd_ops(P, ss.t[:, 0:1], ss.b, rstd.t[:, 0:1], rstd.b, DM)
                P.t("dve", "scalar_tensor_tensor", out=y_.t[:, :], in0=mv, scalar=rstd.t[:, 0:1], in1=gpost[:, :],
                    op0=ALU.mult, op1=ALU.mult, R=[bank[b0 + 1], bank[b0 + 2], rstd.b, cb], W=[y_.b])
                P.t("pool", "tensor_tensor", out=y_.t[:, :], in0=y_.t[:, :], in1=xt.t[:, :], op=ALU.add, R=[y_.b, xt.b], W=[y_.b])
                P.t("sp", "dma_start", out=self.y[tok:tok + 128, :], in_=y_.t[:, :], R=[y_.b])
            P.finish()
            P.emit()


def _rope_angles(pos, dim):
    inv = (np.float32(10000.0) ** (-(np.arange(0, dim, 2, dtype=np.float32) / np.float32(dim)))).astype(np.float32)
    return (pos.astype(np.float32)[:, None] * inv[None, :]).astype(np.float32)


def _t5_bucket(rel):
    half, max_exact = 16, 8
    base = (rel > 0).astype(np.int32) * half
    n = np.abs(rel)
    nf = np.maximum(n, 1).astype(np.float32)
    large = max_exact + ((np.log(nf / np.float32(max_exact)) / np.float32(math.log(128 / max_exact)))
                         * np.float32(half - max_exact)).astype(np.int32)
    large = np.minimum(large, half - 1)
    return base + np.where(n < max_exact, n, large)


def host_tables(cfg, core):
    TS, TP, NPS, LT = cfg.TS, cfg.TP, cfg.NPS, cfg.LT
    r = core % NR
    pos = np.concatenate([r * TS + np.arange(TS)] + [np.arange(TP)] * NPS).astype(np.int64)
    a = _rope_angles(pos, 32)
    mla_cs = np.concatenate([np.cos(a), np.sin(a)], axis=1).astype(np.float32)
    ang = np.concatenate([_rope_angles(pos // 64, 64), _rope_angles(pos % 64, 64)], axis=1)
    gqa_cs = np.concatenate([np.cos(ang), np.sin(ang)], axis=1).astype(np.float32)
    i = np.arange(TREV_L)
    bk = _t5_bucket(767 - i)
    real = np.zeros((32, TREV_L), np.float32)
    real[bk, i] = 1.0
    sel = np.zeros((32, 10), np.float32)
    for rho in range(5):
        if rho == 4 or rho == r:
            sel[15, rho * 2 + 0] = 1.0
            sel[31, rho * 2 + 1] = 1.0
        elif rho > r:
            sel[31, rho * 2:rho * 2 + 2] = 1.0
        else:
            sel[15, rho * 2:rho * 2 + 2] = 1.0
    sel2 = np.zeros((32, 13), np.float32)
    mk = np.zeros((128, 13), np.float32)
    mk[:, 4] = 1.0
    for rho in range(4):
        for dc, w in ((0, rho), (-1, 5 + rho), (1, 9 + rho)):
            if rho - r == dc:
                mk[:, w] = 1.0
            else:
                if rho != r:
                    pos = rho > r
                else:
                    pos = dc < 0
                sel2[31 if pos else 15, w] = 1.0
    return dict(mla_cs=mla_cs, gqa_cs=gqa_cs, dif_oh=real, dif_sel=sel, dif_sel2=sel2, dif_m=mk,
                ident=np.eye(128, dtype=np.float32).astype(NPBF),
                antiid=np.ascontiguousarray(np.eye(128, dtype=np.float32)[::-1]))


_WNAMES = ["norm_pre", "norm_post", "rel_bias", "mla_w_in", "mla_g_q", "mla_w_uq", "mla_g_kv", "mla_w_ukv", "mla_w_o",
           "gqa_w_in", "gqa_g_q", "gqa_g_k", "gqa_w_o", "dif_w_in", "dif_lam_q1", "dif_lam_k1", "dif_lam_q2",
           "dif_lam_k2", "dif_g_sub", "dif_w_o"]


def run(cfg, x_prompt, x_sample, weights, debug=(), trace=False, stop=None):
    TS, TP, NPS = cfg.TS, cfg.TP, cfg.NPS
    b = Builder(cfg, debug=debug, stop=stop)
    nc = b.build()
    in_maps = []
    for c in range(NCORES):
        sb, r = c // NR, c % NR
        xs = [x_sample[sb, r * TS:(r + 1) * TS]] + [x_prompt[c * NPS + s] for s in range(NPS)]
        m = {"x_in": np.ascontiguousarray(np.concatenate(xs, axis=0), dtype=np.float32)}
        for n in _WNAMES:
            m[n] = np.ascontiguousarray(weights[n], dtype=np.float32)
        m.update(host_tables(cfg, c))
        in_maps.append(m)
    res = run_bass_kernel_spmd(nc, in_maps, core_ids=list(range(NCORES)), trace=trace)
    return res


def kernel(**inputs):
    cfg = Cfg()
    x_prompt = np.asarray(inputs["x_prompt"], dtype=np.float32)
    x_sample = np.asarray(inputs["x_sample"], dtype=np.float32)
    weights = {n: np.asarray(inputs[n]) for n in _WNAMES}
    res = run(cfg, x_prompt, x_sample, weights)
    TS, TP, NPS = cfg.TS, cfg.TP, cfg.NPS
    y_prompt = np.empty_like(x_prompt)
    y_sample = np.empty_like(x_sample)
    for c in range(NCORES):
        y = res.results[c]["y"]
        sb, r = c // NR, c % NR
        y_sample[sb, r * TS:(r + 1) * TS] = y[0:TS]
        for s in range(NPS):
            y_prompt[c * NPS + s] = y[TS + s * TP:TS + (s + 1) * TP]
    return (y_prompt, y_sample)
```
